# Optimizing a Trainium2 kernel written in Bass

```python
import math
import jax, jax.numpy as jnp
from jax import lax
import numpy as np

D_MODEL = 1024
BATCH = 2
SEQ = 8192
DEPTH = 1

D_MIX = D_MODEL
SSM_WIDTH = D_MIX // 2
SSM_GROUP = 16
SSM_GROUPS = SSM_WIDTH // SSM_GROUP
SSM_STATE = 64
DT_MIN = 1e-3
DT_MAX = 1e-1

MLA_HEADS = 8
QK_NOPE = 64
QK_ROPE = 32
QK_HEAD = QK_NOPE + QK_ROPE
V_HEAD = 64
MLA_WIDTH = MLA_HEADS * V_HEAD
Q_LORA = 384
KV_LORA = 256
ROPE_THETA = 10000.0
Q_BLOCK = 128
EPS = 1e-6
NEG_INF = -1e30

D_IN = SSM_WIDTH + SSM_WIDTH + Q_LORA + KV_LORA + QK_ROPE + MLA_WIDTH

kernel_name = "hymba_s5_mla_hybrid_block"


def rms_norm(x, g):
    xf = x.astype(jnp.float32)
    y = xf * lax.rsqrt(jnp.mean(xf * xf, axis=-1, keepdims=True) + EPS)
    return (y * g.astype(jnp.float32)).astype(x.dtype)


def rope_tables(positions):
    half = QK_ROPE // 2
    inv_freq = ROPE_THETA ** (-jnp.arange(half, dtype=jnp.float32) * 2.0 / QK_ROPE)
    ang = positions.astype(jnp.float32)[..., None] * inv_freq
    return jnp.cos(ang), jnp.sin(ang)


def apply_rope(x, cos, sin):
    half = x.shape[-1] // 2
    xf = x.astype(jnp.float32)
    x1, x2 = xf[..., :half], xf[..., half:]
    out = jnp.concatenate([x1 * cos - x2 * sin, x2 * cos + x1 * sin], axis=-1)
    return out.astype(x.dtype)


def s5_mixer(u, log_dt, lam_re, lam_im, b_re, b_im, c_re, c_im, d_skip, w_glu, b_glu):
    bsz, seq, _ = u.shape
    uf = u.astype(jnp.float32).reshape(bsz, seq, SSM_GROUPS, SSM_GROUP)
    lam = lax.complex(lam_re.astype(jnp.float32), lam_im.astype(jnp.float32))
    dt = jnp.exp(log_dt.astype(jnp.float32))[:, None]
    lam_bar = jnp.exp(lam * dt)
    b_c = lax.complex(b_re.astype(jnp.float32), b_im.astype(jnp.float32))
    b_bar = ((lam_bar - 1.0) / lam)[..., None] * b_c
    c_c = lax.complex(c_re.astype(jnp.float32), c_im.astype(jnp.float32))
    bu = jnp.einsum('blgh,gph->blgp', uf.astype(jnp.complex64), b_bar)
    a = jnp.broadcast_to(lam_bar, bu.shape)

    def combine(left, right):
        a_l, b_l = left
        a_r, b_r = right
        return a_r * a_l, a_r * b_l + b_r

    _, states = lax.associative_scan(combine, (a, bu), axis=1)
    y = jnp.real(jnp.einsum('blgp,ghp->blgh', states, c_c)) + d_skip.astype(jnp.float32) * uf
    y = jax.nn.gelu(y.reshape(bsz, seq, SSM_WIDTH))
    y = y * jax.nn.sigmoid(y @ w_glu.astype(jnp.float32) + b_glu.astype(jnp.float32))
    return y.astype(u.dtype)


def mla_mixer(c_q, c_kv, k_r, cos, sin, q_a_g, w_q_b, kv_a_g, w_kv_b, q_norm_g, k_norm_g):
    bsz, seq, _ = c_q.shape
    q = (rms_norm(c_q, q_a_g) @ w_q_b).reshape(bsz, seq, MLA_HEADS, QK_HEAD)
    kv = (rms_norm(c_kv, kv_a_g) @ w_kv_b).reshape(bsz, seq, MLA_HEADS, QK_NOPE + V_HEAD)
    k_nope, v = kv[..., :QK_NOPE], kv[..., QK_NOPE:]
    k_rope = jnp.broadcast_to(k_r[:, :, None, :], (bsz, seq, MLA_HEADS, QK_ROPE))
    k = jnp.concatenate([k_nope, k_rope.astype(k_nope.dtype)], axis=-1)
    q = rms_norm(q, q_norm_g)
    k = rms_norm(k, k_norm_g)
    cos_h, sin_h = cos[:, :, None, :], sin[:, :, None, :]
    q = jnp.concatenate([q[..., :QK_NOPE], apply_rope(q[..., QK_NOPE:], cos_h, sin_h)], axis=-1)
    k = jnp.concatenate([k[..., :QK_NOPE], apply_rope(k[..., QK_NOPE:], cos_h, sin_h)], axis=-1)

    qh = q.transpose(0, 2, 1, 3)
    kh = k.transpose(0, 2, 1, 3)
    vh = v.transpose(0, 2, 1, 3)
    n_blocks = seq // Q_BLOCK
    qb = qh.reshape(bsz, MLA_HEADS, n_blocks, Q_BLOCK, QK_HEAD).transpose(2, 0, 1, 3, 4)
    scale = 1.0 / math.sqrt(QK_HEAD)
    k_idx = jnp.arange(seq)

    def one_block(args):
        qi, bi = args
        s = jnp.einsum('bhqd,bhkd->bhqk', qi, kh).astype(jnp.float32) * scale
        q_idx = bi * Q_BLOCK + jnp.arange(Q_BLOCK)
        mask = k_idx[None, :] <= q_idx[:, None]
        p = jax.nn.softmax(jnp.where(mask, s, NEG_INF), axis=-1)
        return jnp.einsum('bhqk,bhkd->bhqd', p.astype(vh.dtype), vh)

    ob = lax.map(one_block, (qb, jnp.arange(n_blocks)))
    return ob.transpose(1, 0, 3, 2, 4).reshape(bsz, seq, MLA_WIDTH)


def setup_inputs(seed: int = 0) -> dict:
    key = jax.random.key(seed)
    ks = jax.random.split(key, 32)
    f32 = jnp.float32

    def nrm(k, shape, std):
        return jax.random.normal(k, shape, f32) * std

    x = nrm(ks[0], (BATCH, SEQ, D_MODEL), 1.0)
    c = nrm(ks[1], (BATCH, D_MODEL), 1.0)
    offsets = jax.random.randint(ks[2], (BATCH, 1), 0, 1024, dtype=jnp.int32)
    positions = offsets + jnp.arange(SEQ, dtype=jnp.int32)[None, :]

    w_ada = nrm(ks[3], (DEPTH, D_MODEL, 3 * D_MODEL), 0.5 * D_MODEL ** -0.5)
    b_ada = nrm(ks[4], (DEPTH, 3 * D_MODEL), 0.02)
    norm_g = 1.0 + nrm(ks[5], (DEPTH, D_MODEL), 0.02)
    w_in = nrm(ks[6], (DEPTH, D_MODEL, D_IN), D_MODEL ** -0.5)

    log_dt = jax.random.uniform(ks[7], (DEPTH, SSM_GROUPS), f32, math.log(DT_MIN), math.log(DT_MAX))
    n_idx = jnp.arange(SSM_STATE, dtype=f32)
    lam_re = -0.5 + nrm(ks[8], (DEPTH, SSM_GROUPS, SSM_STATE), 0.01)
    lam_im = math.pi * n_idx + nrm(ks[9], (DEPTH, SSM_GROUPS, SSM_STATE), 0.01)
    b_std = (2.0 * SSM_GROUP) ** -0.5
    c_std = (2.0 * SSM_STATE) ** -0.5
    b_re = nrm(ks[10], (DEPTH, SSM_GROUPS, SSM_STATE, SSM_GROUP), b_std)
    b_im = nrm(ks[11], (DEPTH, SSM_GROUPS, SSM_STATE, SSM_GROUP), b_std)
    c_re = nrm(ks[12], (DEPTH, SSM_GROUPS, SSM_GROUP, SSM_STATE), c_std)
    c_im = nrm(ks[13], (DEPTH, SSM_GROUPS, SSM_GROUP, SSM_STATE), c_std)
    d_skip = nrm(ks[14], (DEPTH, SSM_GROUPS, SSM_GROUP), 1.0)
    w_glu = nrm(ks[15], (DEPTH, SSM_WIDTH, SSM_WIDTH), SSM_WIDTH ** -0.5)
    b_glu = nrm(ks[16], (DEPTH, SSM_WIDTH), 0.02)

    q_a_g = 1.0 + nrm(ks[17], (DEPTH, Q_LORA), 0.02)
    w_q_b = nrm(ks[18], (DEPTH, Q_LORA, MLA_HEADS * QK_HEAD), Q_LORA ** -0.5)
    kv_a_g = 1.0 + nrm(ks[19], (DEPTH, KV_LORA), 0.02)
    w_kv_b = nrm(ks[20], (DEPTH, KV_LORA, MLA_HEADS * (QK_NOPE + V_HEAD)), KV_LORA ** -0.5)
    q_norm_g = 1.0 + nrm(ks[21], (DEPTH, QK_HEAD), 0.02)
    k_norm_g = 1.0 + nrm(ks[22], (DEPTH, QK_HEAD), 0.02)

    w_out = nrm(ks[23], (DEPTH, D_MIX, D_MODEL), D_MIX ** -0.5)

    return {"x": x, "c": c, "positions": positions,
            "w_ada": w_ada, "b_ada": b_ada, "norm_g": norm_g, "w_in": w_in,
            "log_dt": log_dt, "lam_re": lam_re, "lam_im": lam_im,
            "b_re": b_re, "b_im": b_im, "c_re": c_re, "c_im": c_im,
            "d_skip": d_skip, "w_glu": w_glu, "b_glu": b_glu,
            "q_a_g": q_a_g, "w_q_b": w_q_b, "kv_a_g": kv_a_g, "w_kv_b": w_kv_b,
            "q_norm_g": q_norm_g, "k_norm_g": k_norm_g, "w_out": w_out}


def reference(x, c, positions, w_ada, b_ada, norm_g, w_in,
              log_dt, lam_re, lam_im, b_re, b_im, c_re, c_im, d_skip, w_glu, b_glu,
              q_a_g, w_q_b, kv_a_g, w_kv_b, q_norm_g, k_norm_g, w_out):
    cos, sin = rope_tables(positions)
    c_act = jax.nn.silu(c)
    o1 = SSM_WIDTH
    o2 = o1 + SSM_WIDTH
    o3 = o2 + Q_LORA
    o4 = o3 + KV_LORA
    o5 = o4 + QK_ROPE
    for l in range(DEPTH):
        mod = c_act @ w_ada[l] + b_ada[l]
        shift, scale, gate = jnp.split(mod, 3, axis=-1)
        h = rms_norm(x, norm_g[l]) * (1.0 + scale[:, None, :]) + shift[:, None, :]
        proj = h @ w_in[l]
        u_ssm = proj[..., :o1]
        z_ssm = proj[..., o1:o2]
        c_q = proj[..., o2:o3]
        c_kv = proj[..., o3:o4]
        k_r = proj[..., o4:o5]
        z_mla = proj[..., o5:]
        y_ssm = s5_mixer(u_ssm, log_dt[l], lam_re[l], lam_im[l], b_re[l], b_im[l],
                         c_re[l], c_im[l], d_skip[l], w_glu[l], b_glu[l]) * jax.nn.silu(z_ssm)
        y_mla = mla_mixer(c_q, c_kv, k_r, cos, sin, q_a_g[l], w_q_b[l], kv_a_g[l], w_kv_b[l],
                          q_norm_g[l], k_norm_g[l]) * jax.nn.silu(z_mla)
        y = jnp.concatenate([y_ssm, y_mla], axis=-1) @ w_out[l]
        x = x + gate[:, None, :] * y
    return x
```

```python
import math
from contextlib import ExitStack
import numpy as np
import concourse.bass as bass
import concourse.mybir as mybir
from concourse.bass_utils import run_bass_kernel_spmd

F32 = mybir.dt.float32
BF16 = mybir.dt.bfloat16
I32 = mybir.dt.int32
ALU = mybir.AluOpType
AF = mybir.ActivationFunctionType
AX = mybir.AxisListType

D = 1024
L = 8192
NT = 64
NO = 16
EPS = 1e-6
TWO_PI = 2.0 * math.pi
TWO_PI_HI = float(np.float32(TWO_PI))
TWO_PI_LO = TWO_PI - TWO_PI_HI
IFS = -math.log(10000.0) / 16.0
IFS_HI = float(np.float32(IFS))
IFS_LO = IFS - IFS_HI
DEBUG = False


class Tok:
    __slots__ = ("w", "r", "wdma")

    def __init__(self):
        self.w = []
        self.r = {}
        self.wdma = False


class Eng:
    def __init__(self, name, nsem=0):
        self.name = name
        self.key = (name, "c")
        self.ops = []
        self.known = {}
        self.count = 0
        self.nsem = nsem
        self.dnext = 0
        self.duses = [0] * nsem


class Prog:
    def __init__(self):
        self.pe = Eng("pe")
        self.act = Eng("act", nsem=8)
        self.dve = Eng("dve")
        self.pool = Eng("pool", nsem=14)
        self.sp = Eng("sp", nsem=24)
        self.engs = [self.pe, self.act, self.dve, self.pool, self.sp]

    def _deps(self, E, R, W, is_dma=False):
        deps = []
        for t in R:
            deps.extend(t.w)
        for t in W:
            if not (is_dma and t.wdma and not t.r):
                deps.extend(t.w)
            deps.extend(t.r.items())
        return deps

    def _waits(self, E, deps):
        waits = {}
        for key, val in deps:
            if key == E.key and E.name == "pe":
                continue
            if E.known.get(key, 0) >= val:
                continue
            if waits.get(key, 0) < val:
                waits[key] = val
        for k, v in waits.items():
            E.known[k] = v
        return list(waits.items())

    def _mark(self, ev, R, W, is_dma=False):
        for t in R:
            if t.r.get(ev[0], 0) < ev[1]:
                t.r[ev[0]] = ev[1]
        for t in W:
            if is_dma and t.wdma and not t.r:
                t.w = t.w + [ev]
            else:
                t.w = [ev]
            t.wdma = is_dma
            t.r = {}

    def emit(self, E, fn, R=(), W=(), inc=True):
        waits = self._waits(E, self._deps(E, R, W))
        if inc:
            E.count += 1
            ev = (E.key, E.count)
            E.ops.append((waits, fn, (E.key, 1)))
        else:
            ev = (E.key, E.count + 1)
            E.ops.append((waits, fn, None))
        self._mark(ev, R, W)

    def dma(self, Q, out, in_, R=(), W=()):
        i = Q.dnext % Q.nsem
        Q.dnext += 1
        key = (Q.name, "d", i)
        n = Q.duses[i]
        deps = self._deps(Q, R, W, is_dma=True)
        if n > 0:
            deps.append((key, 16 * n))
        Q.duses[i] = n + 1
        waits = self._waits(Q, deps)
        ev = (key, 16 * (n + 1))
        Q.ops.append((waits, (lambda e, o=out, s=in_: e.dma_start(out=o, in_=s)), (key, 16)))
        self._mark(ev, R, W, is_dma=True)

    def barrier(self):
        evs = []
        for E in self.engs:
            if E.count:
                evs.append((E.key, E.count))
            for i in range(E.nsem):
                if E.duses[i]:
                    evs.append(((E.name, "d", i), 16 * E.duses[i]))
        for E in self.engs:
            w = self._waits(E, [e for e in evs if not (e[0] == E.key)])
            if w:
                E.ops.append((w, None, None))

    def barrier_on(self, E):
        evs = []
        for X in self.engs:
            if X.count and X is not E:
                evs.append((X.key, X.count))
            for i in range(X.nsem):
                if X.duses[i]:
                    evs.append(((X.name, "d", i), 16 * X.duses[i]))
        w = self._waits(E, evs)
        if w:
            E.ops.append((w, None, None))

    def all_keys(self):
        ks = []
        for E in self.engs:
            ks.append(E.key)
            for i in range(E.nsem):
                ks.append((E.name, "d", i))
        return ks


def build():
    nc = bass.Bass("TRN2", target_bir_lowering=False)
    P = Prog()
    PE, ACT, DVE, POOL, SP = P.pe, P.act, P.dve, P.pool, P.sp

    def din(name, shape, dt=F32):
        return nc.dram_tensor(name, list(shape), dt, kind="ExternalInput").ap()

    xb = din("xb", [L, D])
    xo = din("xo", [NO * 128, D])
    posb = din("posb", [128, NT], I32)
    poso = din("poso", [128, NO], I32)
    cT = din("cT", [128, 8])
    w_ada = din("w_ada", [D, 3 * D])
    b_adaT = din("b_adaT", [128, 24])
    b_gate = din("b_gate", [1, D])
    norm_gT = din("norm_gT", [128, 8])
    w_in = din("w_in", [D, 2208])
    w_q_b = din("w_q_b", [384, 768])
    q_a_gT = din("q_a_gT", [128, 3])
    w_kv_b = din("w_kv_b", [256, 1024])
    kv_a_gT = din("kv_a_gT", [128, 2])
    qg_rep = din("qg_rep", [128, 96])
    kg_rep = din("kg_rep", [128, 96])
    w_glu = din("w_glu", [512, 512])
    b_gluT = din("b_gluT", [128, 4])
    w_out = din("w_out", [D, D])
    ldt_rep = din("ldt_rep", [128, 32])
    lamP = din("lamP", [128, 2, 32])
    bP = din("bP", [128, 2, 32, 16])
    cP = din("cP", [128, 2, 32, 16])
    lamS = din("lamS", [128, 2, 32, 64])
    bS = din("bS", [128, 2, 32, 64])
    dS = din("dS", [128, 32])
    mdiag = din("mdiag", [128, 4, 128])
    sel4 = din("sel4", [128, 4])
    y_out = nc.dram_tensor("y", [NO * 128, D], F32, kind="ExternalOutput").ap()
    kT_s = nc.dram_tensor("kT_s", [8, 96, L], BF16).ap()
    v_s = nc.dram_tensor("v_s", [L, 512], BF16).ap()
    uT_s = nc.dram_tensor("uT_s", [512, 8, 1024], BF16).ap()
    uTo_s = nc.dram_tensor("uTo_s", [512, 8, 256], BF16).ap()
    ys_s = nc.dram_tensor("ys_s", [32, 8, 16, 256], BF16).ap()
    w1_s = nc.dram_tensor("w1_s", [128, 32, 128], BF16).ap()
    ymat_s = nc.dram_tensor("ymat_s", [128, 32, 128], BF16).ap()
    x_s = nc.dram_tensor("x_s", [128, 32, 128], F32).ap()
    yk_s = nc.dram_tensor("yk_s", [128, 32, 128], F32).ap()
    a_s = nc.dram_tensor("a_s", [128, 32, 128], F32).ap()
    rv_s = nc.dram_tensor("rv_s", [128, 2, 32, 15], F32).ap()

    es = ExitStack()
    ARENA = 105184
    IAR = 520
    arena = es.enter_context(nc.sbuf_tensor("arena", [128, ARENA], BF16))
    iarena = es.enter_context(nc.sbuf_tensor("iarena", [128, IAR], I32))
    psum = es.enter_context(nc.psum_tensor("psum", [128, 4096], F32))
    sems = {}
    for k in P.all_keys():
        sems[k] = es.enter_context(nc.semaphore("s_" + "_".join(str(x) for x in k)))

    class Alloc:
        def __init__(self):
            self.off = 0
            self.hole = None

        def __call__(self, shape, dt=BF16):
            n = 1
            for s in shape[1:]:
                n *= s
            nb = n * (4 if dt in (F32, I32) else 2)
            nb = (nb + 63) // 64 * 64
            o = self.off
            if self.hole is not None and o < self.hole[1] and o + nb // 2 > self.hole[0]:
                o = self.hole[1]
            self.off = o + nb // 2
            assert self.off <= ARENA, ("arena overflow", self.off)
            v = arena[0:shape[0], o:o + nb // 2]
            if dt != BF16:
                v = v.bitcast(dt)
            v = v[:, 0:n]
            if len(shape) == 3:
                v = v.rearrange("p (a b) -> p a b", a=shape[1])
            elif len(shape) == 4:
                v = v.rearrange("p (a b c) -> p a b c", a=shape[1], b=shape[2])
            return v

    A = Alloc()
    ioff = [0]

    def AI(shape):
        n = 1
        for d_ in shape[1:]:
            n *= d_
        o = ioff[0]
        ioff[0] += n
        assert ioff[0] <= IAR, ioff[0]
        v = iarena[0:shape[0], o:o + n]
        if len(shape) == 3:
            v = v.rearrange("p (a b) -> p a b", a=shape[1])
        return v

    def KB(k):
        return int(k * 512)

    def PS(bank, n=512, dt=F32, parts=128, nb=1):
        v = psum[0:parts, bank * 512:(bank + nb) * 512]
        if dt == BF16:
            v = v.bitcast(BF16)
        return v[:, 0:n]

    pst = [Tok() for _ in range(8)]

    def MM(out, lhsT, rhs, start, stop, R=(), W=(), inc=True):
        P.emit(PE, lambda e: e.matmul(out, lhsT=lhsT, rhs=rhs, start=start, stop=stop), R, W, inc=inc)

    def TRN(out, in_, ident_ap, R=(), W=(), inc=True):
        P.emit(PE, lambda e: e.transpose(out=out, in_=in_, identity=ident_ap), R, W, inc=inc)

    def ACTV(out, in_, func, R=(), W=(), bias=None, scale=None, accum=None):
        kw = {}
        if bias is not None:
            kw["bias"] = bias
        if scale is not None:
            kw["scale"] = scale
        if accum is not None:
            kw["accum_out"] = accum
        P.emit(ACT, lambda e: e.activation(out=out, in_=in_, func=func, **kw), R, W)

    def TS(E, out, in0, s1, s2, op0, op1=None, R=(), W=()):
        if op1 is None:
            P.emit(E, lambda e: e.tensor_scalar(out=out, in0=in0, scalar1=s1, scalar2=None, op0=op0), R, W)
        else:
            P.emit(E, lambda e: e.tensor_scalar(out=out, in0=in0, scalar1=s1, scalar2=s2, op0=op0, op1=op1), R, W)

    def TT(E, out, in0, in1, op, R=(), W=()):
        P.emit(E, lambda e: e.tensor_tensor(out=out, in0=in0, in1=in1, op=op), R, W)

    def STT(E, out, in0, scalar, in1, op0, op1, R=(), W=()):
        P.emit(E, lambda e: e.scalar_tensor_tensor(out=out, in0=in0, scalar=scalar, in1=in1, op0=op0, op1=op1), R, W)

    def CP(E, out, in_, R=(), W=()):
        if E is ACT:
            P.emit(E, lambda e: e.activation(out=out, in_=in_, func=AF.Copy), R, W)
        else:
            P.emit(E, lambda e: e.tensor_copy(out=out, in_=in_), R, W)

    def RED(E, out, in_, op, R=(), W=()):
        P.emit(E, lambda e: e.tensor_reduce(out=out, in_=in_, axis=AX.X, op=op), R, W)

    def MEMSET(E, ap, val, W=()):
        P.emit(E, lambda e: e.memset(ap, val), (), W)

    def bc(ap, shape):
        return ap.to_broadcast(list(shape))

    tk = Tok
    ident = A([128, 128]); t_ident = Tok()
    identf = A([128, 128], F32)
    ones_bf = A([1, 512]); ones_f = A([1, 128], F32); t_ones = Tok()
    mhalf = A([128, 8], F32)
    c_act = A([128, 8], F32); t_cact = Tok()
    modT = A([128, 24], F32); t_mod = Tok()
    gs = A([128, 8], F32)
    gate_row = A([1, D], F32); t_grow = Tok()
    biasrow = A([1, 2208]); t_brow = Tok()
    biasT = A([128, 12], F32)
    small = A([128, 64], F32); t_small = Tok()
    qg = A([128, 96], F32); kg = A([128, 96], F32); t_g = Tok()
    nbias = A([128, 1], F32); t_nb = Tok()
    invf = A([128, 16], F32)
    md = A([128, 4, 128]); t_md = Tok()
    assert A.off <= KB(14), A.off
    A.off = KB(14)
    QT = A([128, 8, NO * 128]); t_qt = Tok()
    zm = A([128, 4, 8, 256]); zs = A([128, 4, 8, 256]); t_z = Tok()
    w_in_b = A([128, 8, 2208]); t_win = Tok()
    w_qb_b = A([128, 3, 768]); w_kvb_b = A([128, 2, 1024]); t_wsm = Tok()
    cosb = A([128, NT, 16], F32); sinb = A([128, NT, 16], F32); t_ropeb = Tok()
    coso = A([128, NO, 16], F32); sino = A([128, NO, 16], F32); t_ropeo = Tok()
    assert A.off <= KB(131), A.off
    A.off = KB(78)
    ymix = A([128, 8, NO * 128]); t_ymix = [Tok() for _ in range(8)]
    W1 = A([128, 32, 128]); Ymat = A([128, 32, 128]); K0 = A([128, 32, 128]); t_ssmw = Tok()
    t_rv = Tok()
    w_glu_b = A([128, 4, 512])
    assert A.off <= KB(141), A.off
    base_off = KB(131)
    A.off = base_off

    sb_adaT = small[:, 0:24]; s_normg = small[:, 24:32]; s_qag = small[:, 32:35]; s_kvag = small[:, 35:37]
    s_bglu = small[:, 37:41]; s_sel = small[:, 41:45]

    P.dma(SP, small[:, 0:24], b_adaT, W=[t_small])
    P.dma(SP, small[:, 24:32], norm_gT, W=[t_small])
    P.dma(SP, small[:, 32:35], q_a_gT, W=[t_small])
    P.dma(SP, small[:, 35:37], kv_a_gT, W=[t_small])
    P.dma(SP, small[:, 37:41], b_gluT, W=[t_small])
    P.dma(SP, small[:, 41:45], sel4, W=[t_small])
    P.dma(SP, qg, qg_rep, W=[t_g])
    P.dma(SP, kg, kg_rep, W=[t_g])
    P.dma(SP, c_act, cT, W=[t_cact])
    P.dma(SP, gate_row, b_gate, W=[t_grow])
    MEMSET(POOL, ident, 1.0, W=[t_ident])
    P.emit(POOL, lambda e: e.affine_select(out=ident, in_=ident, pattern=[[1, 128]], compare_op=ALU.is_equal,
                                           fill=0.0, base=0, channel_multiplier=-1), [t_ident], [t_ident])
    MEMSET(POOL, identf, 1.0, W=[t_ident])
    P.emit(POOL, lambda e: e.affine_select(out=identf, in_=identf, pattern=[[1, 128]], compare_op=ALU.is_equal,
                                           fill=0.0, base=0, channel_multiplier=-1), [t_ident], [t_ident])
    MEMSET(POOL, ones_bf, 1.0, W=[t_ones])
    MEMSET(POOL, ones_f, 1.0, W=[t_ones])
    MEMSET(POOL, mhalf, -0.5, W=[t_ones])

    def rsqrt_mean(out, ssq, n, width, R, W):
        TS(POOL, out, ssq, 1.0 / n, EPS, ALU.mult, ALU.add, R=R, W=W)
        TT(POOL, out, out, mhalf[:, 0:width], ALU.pow, R=list(W) + [t_ones], W=W)

    def sincos(ang, sin_out, cos_out, tmp, tmpi, toks, E=None):
        E = DVE if E is None else E

        def STT(E_, out, in0, scalar, in1, op0, op1, R=(), W=()):
            if E_ is DVE:
                P.emit(E_, lambda e: e.scalar_tensor_tensor(out=out, in0=in0, scalar=scalar, in1=in1, op0=op0, op1=op1), R, W)
            else:
                TS(E_, in0, in0, scalar, None, op0, R=R, W=W)
                TT(E_, out, in0, in1, op1, R=R, W=W)

        def reduce_into(dst, src, shift):
            TS(E, tmp, src, shift, 1.0 / TWO_PI, ALU.add, ALU.mult, R=toks, W=toks)
            CP(E, tmpi, tmp, R=toks, W=toks)
            CP(E, tmp, tmpi, R=toks, W=toks)
            STT(E, dst, tmp, -TWO_PI_HI, src, ALU.mult, ALU.add, R=toks, W=toks)
            STT(E, dst, tmp, -TWO_PI_LO, dst, ALU.mult, ALU.add, R=toks, W=toks)
            if shift != 0.0:
                TS(E, dst, dst, shift, None, ALU.add, R=toks, W=toks)
            TS(E, tmp, dst, math.pi, None, ALU.is_gt, R=toks, W=toks)
            STT(E, dst, tmp, -TWO_PI, dst, ALU.mult, ALU.add, R=toks, W=toks)
            TS(E, tmp, dst, -1.0, None, ALU.mult, R=toks, W=toks)
            TS(E, tmp, tmp, math.pi, None, ALU.is_gt, R=toks, W=toks)
            STT(E, dst, tmp, TWO_PI, dst, ALU.mult, ALU.add, R=toks, W=toks)
            TS(E, dst, dst, math.pi, -math.pi, ALU.min, ALU.max, R=toks, W=toks)
        reduce_into(sin_out, ang, 0.0)
        TS(E, cos_out, sin_out, math.pi / 2.0, None, ALU.add, R=toks, W=toks)
        TS(E, tmp, cos_out, math.pi, None, ALU.is_gt, R=toks, W=toks)
        STT(E, cos_out, tmp, -TWO_PI, cos_out, ALU.mult, ALU.add, R=toks, W=toks)
        TS(E, cos_out, cos_out, math.pi, -math.pi, ALU.min, ALU.max, R=toks, W=toks)
        ACTV(cos_out, cos_out, AF.Sin, R=toks, W=toks)
        ACTV(sin_out, sin_out, AF.Sin, R=toks, W=toks)

    mark0 = KB(14)
    A.off = mark0
    ACTV(c_act, c_act, AF.Silu, R=[t_cact], W=[t_cact])
    wst = [A([128, 8, 512], F32), A([128, 8, 512], F32)]
    t_wst = [Tok(), Tok()]
    ps_mod = PS(0, 24)
    ps_grow = [PS(1, 512, parts=1), PS(2, 512, parts=1)]
    ada_n = [0]

    def ada_chunk(ch):
        sl = ada_n[0] % 2
        ada_n[0] += 1
        P.dma(SP, wst[sl], w_ada[:, ch * 512:(ch + 1) * 512].rearrange("(k p) n -> p k n", p=128), W=[t_wst[sl]])
        for c4 in range(4 if ch < 4 else 0):
            cc = ch * 4 + c4
            for kc in range(8):
                MM(ps_mod[:, cc:cc + 1], wst[sl][:, kc, c4 * 128:(c4 + 1) * 128], c_act[:, kc:kc + 1],
                   kc == 0, kc == 7, R=[t_wst[sl], t_cact], W=[pst[0]])
        if ch >= 4:
            for kc in range(8):
                MM(ps_grow[ch - 4], c_act[:, kc:kc + 1], wst[sl][:, kc, :], kc == 0, kc == 7,
                   R=[t_wst[sl], t_cact], W=[pst[1 + ch - 4]])

    ada_chunk(2)
    ada_chunk(3)
    t_gs = Tok()
    TT(DVE, modT[:, 8:16], ps_mod[:, 8:16], sb_adaT[:, 8:16], ALU.add, R=[pst[0], t_small], W=[t_gs])
    STT(DVE, gs, modT[:, 8:16], 1.0, s_normg, ALU.add, ALU.mult, R=[t_gs, t_small], W=[t_gs])
    ada_chunk(0)
    ada_chunk(1)
    TT(DVE, modT[:, 0:8], ps_mod[:, 0:8], sb_adaT[:, 0:8], ALU.add, R=[pst[0], t_small], W=[t_mod])
    wst2 = [A([128, 2208], F32), A([128, 2208], F32)]
    t_wst2 = [Tok(), Tok()]
    ps_brow = [PS(1 + i, 512, parts=1) for i in range(5)]
    for kc in range(8):
        sl = kc % 2
        P.dma(POOL, wst2[sl], w_in[kc * 128:(kc + 1) * 128, :], W=[t_wst2[sl]])
        if kc % 2 == 0:
            TS(DVE, w_in_b[:, kc, :], wst2[sl], gs[:, kc:kc + 1], None, ALU.mult, R=[t_wst2[sl], t_gs], W=[t_win])
        else:
            ACTV(w_in_b[:, kc, :], wst2[sl], AF.Copy, R=[t_wst2[sl], t_gs], W=[t_win], scale=gs[:, kc:kc + 1])
        for i in range(5):
            n0 = i * 512
            n1 = min(2208, n0 + 512)
            MM(ps_brow[i][:, 0:n1 - n0], modT[:, kc:kc + 1], wst2[sl][:, n0:n1], kc == 0, kc == 7,
               R=[t_wst2[sl], t_mod], W=[pst[1 + i]])
    for i in range(5):
        n0 = i * 512
        n1 = min(2208, n0 + 512)
        CP(DVE, biasrow[:, n0:n1], ps_brow[i][:, 0:n1 - n0], R=[pst[1 + i]], W=[t_brow])
    ada_chunk(4)
    ada_chunk(5)
    for hf in range(2):
        TT(DVE, gate_row[:, hf * 512:(hf + 1) * 512], ps_grow[hf], gate_row[:, hf * 512:(hf + 1) * 512], ALU.add,
           R=[pst[1 + hf], t_grow], W=[t_grow])
    FM_COLS = [0, 128, 256, 384, 512, 640, 768, 896, 1696, 1824, 1952, 2080]
    ps_bt = PS(6, 16)
    for j_, c0_ in enumerate(FM_COLS):
        MM(ps_bt[:, j_:j_ + 1], biasrow[0:1, c0_:c0_ + 128], ones_bf[0:1, 0:1], True, True, R=[t_brow, t_ones], W=[pst[6]])
    CP(DVE, biasT, ps_bt[:, 0:12], R=[pst[6]], W=[t_brow])
    for kc in range(3):
        sl = kc % 2
        P.dma(SP, wst2[sl][:, 0:768], w_q_b[kc * 128:(kc + 1) * 128, :], W=[t_wst2[sl]])
        ACTV(w_qb_b[:, kc, :], wst2[sl][:, 0:768], AF.Copy, R=[t_wst2[sl], t_small], W=[t_wsm], scale=s_qag[:, kc:kc + 1])
    for kc in range(2):
        sl = (kc + 1) % 2
        P.dma(SP, wst2[sl][:, 0:1024], w_kv_b[kc * 128:(kc + 1) * 128, :], W=[t_wst2[sl]])
        TS(DVE, w_kvb_b[:, kc, :], wst2[sl][:, 0:1024], s_kvag[:, kc:kc + 1], None, ALU.mult, R=[t_wst2[sl], t_small], W=[t_wsm])
    P.dma(SP, wst2[0][:, 0:512].rearrange("p (a b) -> p a b", a=4), mdiag, W=[t_wst2[0]])
    CP(DVE, md, wst2[0][:, 0:512].rearrange("p (a b) -> p a b", a=4), R=[t_wst2[0]], W=[t_md])
    tq = A([128, 96], F32); tmx = A([128, 2], F32); t_tmp0 = Tok()
    TS(DVE, tq, qg, -1.0, None, ALU.mult, R=[t_g], W=[t_tmp0])
    TT(DVE, tq, tq, qg, ALU.max, R=[t_g, t_tmp0], W=[t_tmp0])
    RED(DVE, tmx[:, 0:1], tq, ALU.max, R=[t_tmp0], W=[t_tmp0])
    TS(DVE, tq, kg, -1.0, None, ALU.mult, R=[t_g, t_tmp0], W=[t_tmp0])
    TT(DVE, tq, tq, kg, ALU.max, R=[t_g, t_tmp0], W=[t_tmp0])
    RED(DVE, tmx[:, 1:2], tq, ALU.max, R=[t_tmp0], W=[t_tmp0])
    TT(DVE, nbias, tmx[:, 0:1], tmx[:, 1:2], ALU.mult, R=[t_tmp0], W=[t_nb])
    TS(DVE, nbias, nbias, -math.sqrt(96.0), None, ALU.mult, R=[t_nb], W=[t_nb])
    t_rp = Tok()
    ii = AI([128, 16])
    P.emit(POOL, lambda e: e.iota(ii, pattern=[[1, 16]], base=0, channel_multiplier=0), (), [t_rp])
    CP(DVE, invf, ii, R=[t_rp], W=[t_rp])
    invc = A([128, 16], F32)
    TS(DVE, invc, invf, IFS_LO, 1.0, ALU.mult, ALU.add, R=[t_rp], W=[t_rp])
    ACTV(invf, invf, AF.Exp, R=[t_rp], W=[t_rp], scale=IFS_HI)
    TT(DVE, invf, invf, invc, ALU.mult, R=[t_rp], W=[t_rp])
    pbi = AI([128, NT]); pbf = A([128, NT], F32); poi = AI([128, NO]); pof = A([128, NO], F32)
    P.dma(SP, pbi, posb, W=[t_rp])
    P.dma(SP, poi, poso, W=[t_rp])
    CP(DVE, pbf, pbi, R=[t_rp], W=[t_rp])
    CP(DVE, pof, poi, R=[t_rp], W=[t_rp])
    angb = A([128, NO, 16], F32); tmpb = A([128, NO, 16], F32); tmpbi = AI([128, NO, 16])
    rope_ops = []
    _re, _rd = P.emit, P.dma
    for ch_ in range(NT // NO):
        if ch_ == 1:
            P.emit = lambda E, fn, R=(), W=(): rope_ops.append((E, fn, tuple(R), tuple(W)))
        sl_ = slice(ch_ * NO, (ch_ + 1) * NO)
        TT(DVE, angb, bc(pbf[:, sl_].rearrange("p (t o) -> p t o", o=1), [128, NO, 16]),
           bc(invf.rearrange("p (o i) -> p o i", o=1), [128, NO, 16]), ALU.mult, R=[t_rp], W=[t_rp])
        sincos(angb, sinb[:, sl_, :], cosb[:, sl_, :], tmpb, tmpbi, [t_rp, t_ropeb])
    TT(DVE, angb, bc(pof.rearrange("p (t o) -> p t o", o=1), [128, NO, 16]),
       bc(invf.rearrange("p (o i) -> p o i", o=1), [128, NO, 16]), ALU.mult, R=[t_rp], W=[t_rp])
    sincos(angb, sino, coso, tmpb, tmpbi, [t_rp, t_ropeo])
    P.emit, P.dma = _re, _rd
    rope_pos = [0]

    def rope_pump(n):
        for _ in range(n):
            if rope_pos[0] >= len(rope_ops):
                return
            E_, fn_, R_, W_ = rope_ops[rope_pos[0]]
            rope_pos[0] += 1
            P.emit(E_, fn_, R_, W_)

    assert A.off <= KB(78), A.off
    A.off = base_off
    own_sync = [False]

    def own_region_sync():
        if not own_sync[0]:
            own_sync[0] = True
            P.barrier_on(ACT)
            P.barrier_on(DVE)
    mark1 = A.off
    NX = 3
    xt = [A([128, D], F32) for _ in range(NX)]; t_xt = [Tok() for _ in range(NX)]
    xs = [A([128, D]), A([128, D])]; t_xs = [Tok(), Tok()]
    junk = A([128, 384]); t_junk = Tok()
    NSTAT = 5
    st8 = A([128, NSTAT, 32], F32); t_st = [Tok() for _ in range(NSTAT)]
    hT = [A([128, 8, 512]), A([128, 8, 512])]; t_hT = [Tok(), Tok()]
    uTp = A([128, 4, 8, 256]); t_uTp = Tok()
    kTq = [A([128, 8, 256]), A([128, 8, 256])]; t_kTq = [Tok(), Tok()]
    ckvn = [A([128, 384]), A([128, 384])]; t_ckv = [Tok(), Tok()]
    ckvnT = [A([128, 3, 128]), A([128, 3, 128])]; t_ckvT = [Tok(), Tok()]
    sqt = [A([128, 8, 96], F32), A([128, 8, 96], F32)]; t_sqt = [Tok(), Tok()]
    Kt = [A([128, 8, 96]), A([128, 8, 96])]; t_Kt = [Tok(), Tok()]
    krg = [A([128, 32], F32) for _ in range(3)]; krot = [A([128, 32], F32) for _ in range(3)]
    kr4 = A([128, 4, 16], F32); t_kr = [Tok() for _ in range(3)]; t_kr4 = Tok()
    qr4 = A([128, 4, 8, 16], F32)
    vt = [A([128, 512]), A([128, 512])]; t_vt = [Tok(), Tok()]
    kg64 = bc(kg[:, 0:64].rearrange("p (o d) -> p o d", o=1), [128, 8, 64])
    qg96 = bc(qg.rearrange("p (o d) -> p o d", o=1), [128, 8, 96])

    def rope32(dst_re, dst_im, x1, x2, cos_t, sin_t, tmp4, toks):
        TT(DVE, tmp4[0], x1, cos_t, ALU.mult, R=toks, W=toks)
        TT(DVE, tmp4[1], x2, sin_t, ALU.mult, R=toks, W=toks)
        TT(DVE, tmp4[2], x2, cos_t, ALU.mult, R=toks, W=toks)
        TT(DVE, tmp4[3], x1, sin_t, ALU.mult, R=toks, W=toks)
        TT(DVE, dst_re, tmp4[0], tmp4[1], ALU.subtract, R=toks, W=toks)
        TT(DVE, dst_im, tmp4[2], tmp4[3], ALU.add, R=toks, W=toks)

    def stA(t, src_rows):
        sl = t % NX
        s2 = t % 2
        ss = t % NSTAT
        P.dma(SP, xt[sl], src_rows, W=[t_xt[sl]])
        ACTV(xs[s2], xt[sl], AF.Square, R=[t_xt[sl]], W=[t_xs[s2], t_st[ss]], accum=st8[:, ss, 0:1])
        rsqrt_mean(st8[:, ss, 1:2], st8[:, ss, 0:1], float(D), 1, R=[t_st[ss]], W=[t_st[ss]])
        TS(DVE, xs[s2], xt[sl], st8[:, ss, 1:2], None, ALU.mult, R=[t_xt[sl], t_st[ss]], W=[t_xs[s2]])

    def stB(t, gbase=0):
        s2 = t % 2
        hs = ((t + gbase) // 4) % 2
        tq = t % 4
        pT = PS(0, 1024, BF16).rearrange("p (k t) -> p k t", k=8)
        for kc in range(8):
            TRN(pT[:, kc, :], xs[s2][:, kc * 128:(kc + 1) * 128], ident, R=[t_xs[s2], t_ident], W=[pst[0]], inc=(kc == 7))
        CP(ACT, hT[hs][:, :, tq * 128:(tq + 1) * 128], pT, R=[pst[0]], W=[t_hT[hs]])

    def proj_fm(ps_ap, pstok, col0, hslot, n=512):
        for kc in range(8):
            MM(ps_ap, w_in_b[:, kc, col0:col0 + 128], hT[hslot][:, kc, 0:n], kc == 0, kc == 7,
               R=[t_win, t_hT[hslot]], W=[pstok], inc=(kc == 7))

    def proj_tm(ps_ap, pstok, col0, ncol, hslot, tq):
        for kc in range(8):
            MM(ps_ap, hT[hslot][:, kc, tq * 128:(tq + 1) * 128], w_in_b[:, kc, col0:col0 + ncol], kc == 0, False,
               R=[t_win, t_hT[hslot]], W=[pstok], inc=False)
        MM(ps_ap, ones_bf[0:1, 0:128], biasrow[0:1, col0:col0 + ncol], False, True, R=[t_brow, t_ones], W=[pstok])

    def run_pipeline(ntiles, stages, after_step):
        nst = len(stages)
        for step in range(ntiles + nst - 1):
            if 0 <= step < ntiles:
                stages[0](step)
            for si in range(nst - 1, 0, -1):
                t = step - si
                if 0 <= t < ntiles:
                    stages[si](t)
            after_step(step)

    def p1A(t):
        stA(t, xb[t * 128:(t + 1) * 128, :])

    def p1B(t):
        stB(t)

    def p1C(t):
        hs = (t // 4) % 2
        tq = t % 4
        s2 = t % 2
        s3 = t % 3
        ss = t % NSTAT
        pkv = PS(3, 288)
        proj_tm(pkv, pst[3], 1408, 288, hs, tq)
        ACTV(junk[:, 0:256], pkv[:, 0:256], AF.Square, R=[pst[3]], W=[t_st[ss]], accum=st8[:, ss, 2:3])
        rsqrt_mean(st8[:, ss, 3:4], st8[:, ss, 2:3], 256.0, 1, R=[t_st[ss]], W=[t_st[ss]])
        TS(DVE, ckvn[s2][:, 0:256], pkv[:, 0:256], st8[:, ss, 3:4], None, ALU.mult, R=[pst[3], t_st[ss]], W=[t_ckv[s2]])
        ACTV(junk[:, 256:288], pkv[:, 256:288], AF.Square, R=[pst[3]], W=[t_st[ss]], accum=st8[:, ss, 4:5])
        TT(DVE, krg[s3], pkv[:, 256:288], kg[:, 64:96], ALU.mult, R=[pst[3], t_g], W=[t_kr[s3]])
        rope32(krot[s3][:, 0:16], krot[s3][:, 16:32], krg[s3][:, 0:16], krg[s3][:, 16:32], cosb[:, t, :], sinb[:, t, :],
               [kr4[:, i, :] for i in range(4)], [t_kr[s3], t_ropeb, t_kr4])

    def p1D(t):
        s2 = t % 2
        pT2 = PS(4, 256, BF16).rearrange("p (k t) -> p k t", k=2)
        for k2 in range(2):
            TRN(pT2[:, k2, :], ckvn[s2][:, k2 * 128:(k2 + 1) * 128], ident, R=[t_ckv[s2], t_ident], W=[pst[4]], inc=(k2 == 1))
        CP(ACT, ckvnT[s2][:, 0:2, :], pT2, R=[pst[4]], W=[t_ckvT[s2]])

    def p1E(t):
        s2 = t % 2
        s3 = t % 3
        ss = t % NSTAT
        pkvf = PS(5, 1024, nb=2)
        for hf in range(2):
            for k2 in range(2):
                MM(pkvf[:, hf * 512:(hf + 1) * 512], ckvnT[s2][:, k2, :], w_kvb_b[:, k2, hf * 512:(hf + 1) * 512],
                   k2 == 0, k2 == 1, R=[t_ckvT[s2], t_wsm], W=[pst[5], pst[6]], inc=(hf == 1 and k2 == 1))
        kv3 = pkvf.rearrange("p (h d) -> p h d", h=8)
        ACTV(sqt[s2][:, :, 0:64], kv3[:, :, 0:64], AF.Square, R=[pst[5], pst[6]], W=[t_sqt[s2]])
        ssh = st8[:, ss, 8:16]
        RED(DVE, ssh, sqt[s2][:, :, 0:64], ALU.add, R=[t_sqt[s2]], W=[t_st[ss]])
        TS(POOL, ssh, ssh, st8[:, ss, 4:5], None, ALU.add, R=[t_st[ss]], W=[t_st[ss]])
        rk = st8[:, ss, 16:24]
        rsqrt_mean(rk, ssh, 96.0, 8, R=[t_st[ss]], W=[t_st[ss]])
        rk64 = bc(rk.rearrange("p (h o) -> p h o", o=1), [128, 8, 64])
        TT(DVE, sqt[s2][:, :, 0:64], kv3[:, :, 0:64], rk64, ALU.mult, R=[pst[5], pst[6], t_st[ss]], W=[t_sqt[s2]])
        TT(DVE, Kt[s2][:, :, 0:64], sqt[s2][:, :, 0:64], kg64, ALU.mult, R=[t_sqt[s2], t_g], W=[t_Kt[s2]])
        TT(DVE, Kt[s2][:, :, 64:96], bc(krot[s3].rearrange("p (o d) -> p o d", o=1), [128, 8, 32]),
           bc(rk.rearrange("p (h o) -> p h o", o=1), [128, 8, 32]), ALU.mult, R=[t_kr[s3], t_st[ss]], W=[t_Kt[s2]])
        CP(ACT, vt[s2].rearrange("p (h d) -> p h d", h=8), kv3[:, :, 64:128], R=[pst[5], pst[6]], W=[t_vt[s2]])
        P.dma(ACT, v_s[t * 128:(t + 1) * 128, :], vt[s2], R=[t_vt[s2]])

    def p1F(t):
        s2 = t % 2
        ks = (t // 2) % 2
        pKT = PS(7, 1024, BF16, parts=96).rearrange("p (h t) -> p h t", h=8)
        for h in range(8):
            TRN(pKT[:, h, :], Kt[s2][:, h, :], ident, R=[t_Kt[s2], t_ident], W=[pst[7]], inc=(h == 7))
        CP(ACT, kTq[ks][0:96, :, (t % 2) * 128:(t % 2 + 1) * 128], pKT, R=[pst[7]], W=[t_kTq[ks]])
        if t % 2 == 1:
            P.dma(ACT, kT_s.rearrange("h d t -> d h t")[:, :, (t - 1) * 128:(t + 1) * 128], kTq[ks][0:96], R=[t_kTq[ks]])

    def p1after(step):
        u_ = step - 4
        if u_ < 0:
            return
        qd = u_ // 4
        cc = u_ % 4
        if qd >= NT // 4:
            return
        hs = qd % 2
        b_ = 1 + cc % 2
        pu = PS(b_)
        proj_fm(pu, pst[b_], cc * 128, hs)
        TS(DVE, uTp[:, cc, :, (qd % 4) * 64:(qd % 4) * 64 + 64], pu.rearrange("p (c s) -> p s c", s=8),
           biasT[:, cc:cc + 1], None, ALU.add, R=[pst[b_], t_brow], W=[t_uTp])
        if qd % 4 == 3 and cc == 3:
            sq_ = qd // 4
            for c2 in range(4):
                P.dma(POOL, uT_s[c2 * 128:(c2 + 1) * 128, :, sq_ * 256:(sq_ + 1) * 256], uTp[:, c2], R=[t_uTp])

    wg_ops = []
    real_emit, real_dma = P.emit, P.dma

    def rec_emit(E, fn, R=(), W=()):
        wg_ops.append(("e", E, fn, tuple(R), tuple(W)))

    def rec_dma(Q, out, in_, R=(), W=()):
        wg_ops.append(("d", Q, out, in_, tuple(R), tuple(W)))

    saved_off = A.off
    A.off = KB(78)
    A.hole = (KB(94), KB(110))
    P.emit, P.dma = rec_emit, rec_dma
    t_sg = Tok()
    lp = A([128, 2, 32], F32); dtp = A([128, 32], F32)
    P.dma(SP, lp, lamP, W=[t_sg])
    P.dma(SP, dtp, ldt_rep, W=[t_sg])
    ACTV(dtp, dtp, AF.Exp, R=[t_sg], W=[t_sg])
    mag = A([128, 32], F32); ang = A([128, 32], F32); sn = A([128, 32], F32); cs = A([128, 32], F32)
    tmpa = A([128, 32], F32); tmpai = AI([128, 32])
    TT(DVE, mag, lp[:, 0, :], dtp, ALU.mult, R=[t_sg], W=[t_sg])
    ACTV(mag, mag, AF.Exp, R=[t_sg], W=[t_sg])
    TT(DVE, ang, lp[:, 1, :], dtp, ALU.mult, R=[t_sg], W=[t_sg])
    sincos(ang, sn, cs, tmpa, tmpai, [t_sg])
    PW = A([128, 2, 32, 13], F32)
    RVt = A([128, 2, 32, 15], F32)
    pr_ = sn; pi_ = cs
    TT(DVE, PW[:, 0, :, 0], mag, cs, ALU.mult, R=[t_sg], W=[t_sg])
    TT(DVE, PW[:, 1, :, 0], mag, sn, ALU.mult, R=[t_sg], W=[t_sg])
    for k in range(12):
        TT(DVE, tmpa, PW[:, 0, :, k], PW[:, 0, :, k], ALU.mult, R=[t_sg], W=[t_sg])
        TT(DVE, ang, PW[:, 1, :, k], PW[:, 1, :, k], ALU.mult, R=[t_sg], W=[t_sg])
        TT(DVE, PW[:, 0, :, k + 1], tmpa, ang, ALU.subtract, R=[t_sg], W=[t_sg])
        TT(DVE, tmpa, PW[:, 0, :, k], PW[:, 1, :, k], ALU.mult, R=[t_sg], W=[t_sg])
        TS(DVE, PW[:, 1, :, k + 1], tmpa, 2.0, None, ALU.mult, R=[t_sg], W=[t_sg])
    RV_SPEC = [(3, None), (4, None), (3, 4), (5, None), (6, None), (5, 6),
               (7, None), (8, None), (7, 8), (9, None), (10, None), (9, 10), (11, None), (12, None), (11, 12)]
    for k, (ka, kb_) in enumerate(RV_SPEC):
        if kb_ is None:
            sre, sim = PW[:, 0, :, ka], PW[:, 1, :, ka]
        else:
            TT(DVE, tmpa, PW[:, 0, :, ka], PW[:, 0, :, kb_], ALU.mult, R=[t_sg], W=[t_sg])
            TT(DVE, ang, PW[:, 1, :, ka], PW[:, 1, :, kb_], ALU.mult, R=[t_sg], W=[t_sg])
            TT(DVE, pr_, tmpa, ang, ALU.subtract, R=[t_sg], W=[t_sg])
            TT(DVE, tmpa, PW[:, 0, :, ka], PW[:, 1, :, kb_], ALU.mult, R=[t_sg], W=[t_sg])
            TT(DVE, ang, PW[:, 1, :, ka], PW[:, 0, :, kb_], ALU.mult, R=[t_sg], W=[t_sg])
            TT(DVE, pi_, tmpa, ang, ALU.add, R=[t_sg], W=[t_sg])
            sre, sim = pr_, pi_
        CP(DVE, RVt[0:64, 0, :, k], sre[0:64], R=[t_sg], W=[t_sg])
        TS(DVE, RVt[64:128, 0, :, k], sim[64:128], -1.0, None, ALU.mult, R=[t_sg], W=[t_sg])
        CP(DVE, RVt[0:64, 1, :, k], sim[0:64], R=[t_sg], W=[t_sg])
        CP(DVE, RVt[64:128, 1, :, k], sre[64:128], R=[t_sg], W=[t_sg])
    P.dma(POOL, rv_s, RVt, R=[t_sg])
    PT = A([128, 2, 32, 9], F32)
    MEMSET(DVE, PT[:, 0, :, 0], 1.0, W=[t_sg])
    MEMSET(DVE, PT[:, 1, :, 0], 0.0, W=[t_sg])

    def cmul(E, ore, oim, are, aim, bre, bim, t1, t2, toks):
        TT(E, t1, are, bre, ALU.mult, R=toks, W=toks)
        TT(E, t2, aim, bim, ALU.mult, R=toks, W=toks)
        TT(E, ore, t1, t2, ALU.subtract, R=toks, W=toks)
        TT(E, t1, are, bim, ALU.mult, R=toks, W=toks)
        TT(E, t2, aim, bre, ALU.mult, R=toks, W=toks)
        TT(E, oim, t1, t2, ALU.add, R=toks, W=toks)

    for t in range(8):
        cmul(DVE, PT[:, 0, :, t + 1], PT[:, 1, :, t + 1], PT[:, 0, :, t], PT[:, 1, :, t], PW[:, 0, :, 0], PW[:, 1, :, 0],
             tmpa, ang, [t_sg])
    PI_ = A([128, 2, 32, 8], F32)
    xa_raw = A([128, 512], F32); xb_raw = A([128, 512], F32)
    n2 = xa_raw[:, 0:256].rearrange("p (g s) -> p g s", g=32); n2b = xb_raw[:, 0:256].rearrange("p (g s) -> p g s", g=32)
    TT(DVE, n2, PT[:, 0, :, 0:8], PT[:, 0, :, 0:8], ALU.mult, R=[t_sg], W=[t_sg])
    TT(DVE, n2b, PT[:, 1, :, 0:8], PT[:, 1, :, 0:8], ALU.mult, R=[t_sg], W=[t_sg])
    TT(DVE, n2, n2, n2b, ALU.add, R=[t_sg], W=[t_sg])
    P.emit(DVE, lambda e: e.reciprocal(out=n2, in_=n2), [t_sg], [t_sg])
    TT(DVE, PI_[:, 0, :, :], PT[:, 0, :, 0:8], n2, ALU.mult, R=[t_sg], W=[t_sg])
    TT(DVE, n2b, PT[:, 1, :, 0:8], n2, ALU.mult, R=[t_sg], W=[t_sg])
    TS(DVE, PI_[:, 1, :, :], n2b, -1.0, None, ALU.mult, R=[t_sg], W=[t_sg])
    PTr = A([128, 2, 32, 8], F32)
    for s_ in range(8):
        CP(DVE, PTr[:, :, :, s_], PT[:, :, :, 7 - s_], R=[t_sg], W=[t_sg])
    cf = A([128, 2, 32], F32); l2 = A([128, 32], F32); l2b = A([128, 32], F32)
    TT(DVE, l2, lp[:, 0, :], lp[:, 0, :], ALU.mult, R=[t_sg], W=[t_sg])
    TT(DVE, l2b, lp[:, 1, :], lp[:, 1, :], ALU.mult, R=[t_sg], W=[t_sg])
    TT(DVE, l2, l2, l2b, ALU.add, R=[t_sg], W=[t_sg])
    P.emit(DVE, lambda e: e.reciprocal(out=l2, in_=l2), [t_sg], [t_sg])
    lb1 = A([128, 32], F32)
    TS(DVE, lb1, PW[:, 0, :, 0], -1.0, None, ALU.add, R=[t_sg], W=[t_sg])
    TT(DVE, tmpa, lb1, lp[:, 0, :], ALU.mult, R=[t_sg], W=[t_sg])
    TT(DVE, ang, PW[:, 1, :, 0], lp[:, 1, :], ALU.mult, R=[t_sg], W=[t_sg])
    TT(DVE, tmpa, tmpa, ang, ALU.add, R=[t_sg], W=[t_sg])
    TT(DVE, cf[:, 0, :], tmpa, l2, ALU.mult, R=[t_sg], W=[t_sg])
    TT(DVE, tmpa, PW[:, 1, :, 0], lp[:, 0, :], ALU.mult, R=[t_sg], W=[t_sg])
    TT(DVE, ang, lb1, lp[:, 1, :], ALU.mult, R=[t_sg], W=[t_sg])
    TT(DVE, tmpa, tmpa, ang, ALU.subtract, R=[t_sg], W=[t_sg])
    TT(DVE, cf[:, 1, :], tmpa, l2, ALU.mult, R=[t_sg], W=[t_sg])
    GB = 4
    bp2 = [A([128, 2, GB, 16], F32), A([128, 2, GB, 16], F32)]; BB = A([128, 2, GB, 16], F32)
    cp2 = [A([128, 2, GB, 16], F32), A([128, 2, GB, 16], F32)]
    t16a = A([128, GB, 16], F32); t16b = A([128, GB, 16], F32)
    X = A([128, GB, 8, 16], F32)
    xa = xa_raw.rearrange("p (g s h) -> p g s h", g=GB, s=8); xb_ = xb_raw.rearrange("p (g s h) -> p g s h", g=GB, s=8)
    YY = A([128, GB, 9, 16], F32); A7 = A([128, GB, 8, 16], F32)
    xa9 = A([128, GB, 9, 16], F32); xb9 = A([128, GB, 9, 16], F32)
    YmS = A([128, GB, 128])

    t_bpin = [Tok(), Tok()]

    def b4(ap3, axis):
        shp = [ap3.shape[0], GB, 8, 16]
        if axis == 3:
            return bc(ap3.rearrange("p g (s o) -> p g s o", o=1), shp)
        return bc(ap3.rearrange("p g (o h) -> p g o h", o=1), shp)

    def cstack(dst, pw_re, pw_im, c_re_, c_im_, neg_im, xtok=None, n=8):
        RR = [t_sg] + ([xtok] if xtok is not None else [])
        ta, tb = (xa, xb_) if n == 8 else (xa9, xb9)

        def bb(ap3, axis, lo, hi):
            shp = [hi - lo, GB, n, 16]
            if axis == 3:
                return bc(ap3[lo:hi].rearrange("p g (s o) -> p g s o", o=1), shp)
            return bc(ap3[lo:hi].rearrange("p g (o h) -> p g o h", o=1), shp)
        TT(DVE, ta[0:64], bb(pw_re, 3, 0, 64), bb(c_re_, 2, 0, 64), ALU.mult, R=RR, W=[t_sg])
        TT(DVE, tb[0:64], bb(pw_im, 3, 0, 64), bb(c_im_, 2, 0, 64), ALU.mult, R=RR, W=[t_sg])
        TT(DVE, dst[0:64], ta[0:64], tb[0:64], ALU.subtract, R=RR, W=[t_sg])
        TT(DVE, ta[64:128], bb(pw_re, 3, 64, 128), bb(c_im_, 2, 64, 128), ALU.mult, R=RR, W=[t_sg])
        TT(DVE, tb[64:128], bb(pw_im, 3, 64, 128), bb(c_re_, 2, 64, 128), ALU.mult, R=RR, W=[t_sg])
        TT(DVE, dst[64:128], ta[64:128], tb[64:128], ALU.add, R=RR, W=[t_sg])
        if neg_im:
            TS(DVE, dst[64:128], dst[64:128], -1.0, None, ALU.mult, R=RR, W=[t_sg])

    for gb in range(32 // GB):
        g0 = gb * GB
        bp = bp2[gb % 2]
        cp_ = cp2[gb % 2]
        P.dma(POOL, bp, bP[:, :, g0:g0 + GB, :], W=[t_bpin[gb % 2]])
        P.dma(POOL, cp_, cP[:, :, g0:g0 + GB, :], W=[t_bpin[gb % 2]])
        cfre = bc(cf[:, 0, g0:g0 + GB].rearrange("p (g o) -> p g o", o=1), [128, GB, 16])
        cfim = bc(cf[:, 1, g0:g0 + GB].rearrange("p (g o) -> p g o", o=1), [128, GB, 16])
        cmul(DVE, BB[:, 0], BB[:, 1], cfre, cfim, bp[:, 0], bp[:, 1], t16a, t16b, [t_sg, t_bpin[gb % 2]])
        cstack(X, PI_[:, 0, g0:g0 + GB, :], PI_[:, 1, g0:g0 + GB, :], BB[:, 0], BB[:, 1], False)
        cstack(A7, PTr[:, 0, g0:g0 + GB, :], PTr[:, 1, g0:g0 + GB, :], BB[:, 0], BB[:, 1], False)
        cstack(YY, PT[:, 0, g0:g0 + GB, :], PT[:, 1, g0:g0 + GB, :], cp_[:, 0], cp_[:, 1], True, t_bpin[gb % 2], n=9)
        CP(DVE, YmS, YY[:, :, 1:9, :].rearrange("p g t h -> p g (t h)"), R=[t_sg], W=[t_sg])
        P.dma(POOL, ymat_s[:, g0:g0 + GB, :], YmS, R=[t_sg])
        P.dma(POOL, x_s[:, g0:g0 + GB, :], X.rearrange("p g s h -> p g (s h)"), R=[t_sg])
        P.dma(POOL, a_s[:, g0:g0 + GB, :], A7.rearrange("p g s h -> p g (s h)"), R=[t_sg])
        P.dma(POOL, yk_s[:, g0:g0 + GB, :], YY[:, :, 0:8, :].rearrange("p g t h -> p g (t h)"), R=[t_sg])
    ops_a = wg_ops
    ops_b = []
    P.emit, P.dma = real_emit, real_dma
    assert A.off <= KB(131), A.off
    A.hole = None
    A.off = saved_off
    merged = []
    ia = ib = 0
    while ia < len(ops_a) or ib < len(ops_b):
        if ia < len(ops_a):
            merged.append(ops_a[ia]); ia += 1
        if ib < len(ops_b):
            merged.append(ops_b[ib]); ib += 1
    wg_pos = [0]
    wg_per_step = (len(merged) + 39) // 40

    def wg_pump(n):
        for _ in range(n):
            if wg_pos[0] >= len(merged):
                return
            op = merged[wg_pos[0]]
            wg_pos[0] += 1
            if op[0] == "e":
                P.emit(op[1], op[2], op[3], op[4])
            else:
                P.dma(op[1], op[2], op[3], op[4], op[5])


    Qt = Kt
    t_Qt = t_Kt
    uTo = uTp
    xo_t = xo.rearrange("(t p) d -> t p d", p=128)

    def p2A(t):
        stA(t, xo_t[t - NT])

    def p2B(t):
        stB(t)

    def p2C(t):
        hs = (t // 4) % 2
        tq = t % 4
        s2 = t % 2
        ss = t % NSTAT
        pq = PS(3, 384)
        proj_tm(pq, pst[3], 1024, 384, hs, tq)
        ACTV(junk[:, 0:384], pq, AF.Square, R=[pst[3]], W=[t_st[ss]], accum=st8[:, ss, 2:3])
        rsqrt_mean(st8[:, ss, 3:4], st8[:, ss, 2:3], 384.0, 1, R=[t_st[ss]], W=[t_st[ss]])
        TS(DVE, ckvn[s2], pq, st8[:, ss, 3:4], None, ALU.mult, R=[pst[3], t_st[ss]], W=[t_ckv[s2]])

    def p2D(t):
        s2 = t % 2
        pT3 = PS(4, 384, BF16).rearrange("p (k t) -> p k t", k=3)
        for k3 in range(3):
            TRN(pT3[:, k3, :], ckvn[s2][:, k3 * 128:(k3 + 1) * 128], ident, R=[t_ckv[s2], t_ident], W=[pst[4]], inc=(k3 == 2))
        CP(ACT, ckvnT[s2], pT3, R=[pst[4]], W=[t_ckvT[s2]])

    def p2E(t):
        s2 = t % 2
        ss = t % NSTAT
        pqf = PS(5, 768, nb=2)
        for (n0, n1) in ((0, 512), (512, 768)):
            for k3 in range(3):
                MM(pqf[:, n0:n1], ckvnT[s2][:, k3, :], w_qb_b[:, k3, n0:n1], k3 == 0, k3 == 2,
                   R=[t_ckvT[s2], t_wsm], W=[pst[5], pst[6]], inc=(n0 == 512 and k3 == 2))
        q3 = pqf.rearrange("p (h d) -> p h d", h=8)
        ACTV(sqt[s2], q3, AF.Square, R=[pst[5], pst[6]], W=[t_sqt[s2]])
        ssh = st8[:, ss, 8:16]
        RED(DVE, ssh, sqt[s2], ALU.add, R=[t_sqt[s2]], W=[t_st[ss]])
        rk = st8[:, ss, 16:24]
        rsqrt_mean(rk, ssh, 96.0, 8, R=[t_st[ss]], W=[t_st[ss]])
        TT(DVE, sqt[s2], q3, bc(rk.rearrange("p (h o) -> p h o", o=1), [128, 8, 96]), ALU.mult,
           R=[pst[5], pst[6], t_st[ss]], W=[t_sqt[s2]])
        TT(DVE, sqt[s2], sqt[s2], qg96, ALU.mult, R=[t_sqt[s2], t_g], W=[t_sqt[s2]])
        CP(DVE, Qt[s2][:, :, 0:64], sqt[s2][:, :, 0:64], R=[t_sqt[s2]], W=[t_Qt[s2]])
        cos8 = bc(coso[:, t - NT, :].rearrange("p (o d) -> p o d", o=1), [128, 8, 16])
        sin8 = bc(sino[:, t - NT, :].rearrange("p (o d) -> p o d", o=1), [128, 8, 16])
        rope32(Qt[s2][:, :, 64:80], Qt[s2][:, :, 80:96], sqt[s2][:, :, 64:80], sqt[s2][:, :, 80:96], cos8, sin8,
               [qr4[:, i] for i in range(4)], [t_sqt[s2], t_Qt[s2], t_ropeo, t_kr4])

    def p2F(t):
        own_region_sync()
        s2 = t % 2
        pQT = PS(7, 1024, BF16, parts=96).rearrange("p (h t) -> p h t", h=8)
        for h in range(8):
            TRN(pQT[:, h, :], Qt[s2][:, h, :], ident, R=[t_Qt[s2], t_ident], W=[pst[7]], inc=(h == 7))
        CP(ACT, QT[0:96, :, (t - NT) * 128:(t - NT + 1) * 128], pQT, R=[pst[7], t_sg], W=[t_qt])

    def p2after(step):
        u_ = step - 4
        if u_ < 0:
            return
        qg = u_ // 4
        part = u_ % 4
        if qg < NT // 4 or qg >= (NT + NO) // 4:
            return
        own_region_sync()
        qd = qg - NT // 4
        hs = qg % 2
        for cc in range(3 * part, 3 * part + 3):
            b_ = 1 + cc % 2
            pu = PS(b_)
            col0 = [0, 128, 256, 384, 512, 640, 768, 896, 1696, 1824, 1952, 2080][cc]
            proj_fm(pu, pst[b_], col0, hs)
            src = pu.rearrange("p (c s) -> p s c", s=8)
            if cc < 4:
                TS(DVE, uTo[:, cc, :, qd * 64:(qd + 1) * 64], src, biasT[:, cc:cc + 1], None, ALU.add, R=[pst[b_], t_brow], W=[t_uTp])
            elif cc < 8:
                ACTV(zs[:, cc - 4, :, qd * 64:(qd + 1) * 64], src, AF.Silu, R=[pst[b_], t_sg, t_brow], W=[t_z], bias=biasT[:, cc:cc + 1])
            else:
                ACTV(zm[:, cc - 8, :, qd * 64:(qd + 1) * 64], src, AF.Silu, R=[pst[b_], t_sg, t_brow], W=[t_z], bias=biasT[:, cc:cc + 1])

    def mk(f1, f2):
        return lambda t: f1(t) if t < NT else f2(t)

    def both_after(step):
        rope_pump(24)
        p1after(step)
        p2after(step)

    run_pipeline(NT + NO, [mk(p1A, p2A), mk(p1B, p2B), mk(p1C, p2C), mk(p1D, p2D), mk(p1E, p2E), mk(p1F, p2F)], both_after)
    for cc in range(4):
        P.dma(POOL, uTo_s[cc * 128:(cc + 1) * 128, :, :], uTo[:, cc], R=[t_uTp])

    P.barrier()
    A.off = mark1
    kTh = [A([128, L]), A([128, L])]; t_kTh = [Tok(), Tok()]
    Va = [A([128, NT, 128]), A([128, NT, 128])]; t_Va = [Tok(), Tok()]
    PT3 = [A([128, 2, 512]) for _ in range(3)]; t_PT = [Tok() for _ in range(3)]
    rden = A([128, 512], F32); t_rd = Tok()
    scale = 1.0 / math.sqrt(96.0)
    MEMSET(POOL, Va[0][:, :, 64:128], 1.0, W=[t_Va[0]])
    MEMSET(POOL, Va[1][:, :, 0:64], 1.0, W=[t_Va[1]])
    ymv = ymix.rearrange("p a (s c) -> p a s c", s=8)

    def load_head(h):
        hsl = h % 2
        P.dma(SP, kTh[hsl][0:96, :], kT_s[h], W=[t_kTh[hsl]])
        voff = 0 if hsl == 0 else 64
        for part in range(4):
            P.dma(SP, Va[hsl][:, part * 16:(part + 1) * 16, voff:voff + 64],
                  v_s[part * 2048:(part + 1) * 2048, h * 64:(h + 1) * 64].rearrange("(b p) d -> p b d", p=128),
                  W=[t_Va[hsl]])

    items = []
    for h in range(8):
        for gq in range(4):
            nkb = 4 * (4 * gq + 3) + 4
            for kp in range(nkb // 2):
                items.append((h, gq, kp, nkb))

    def geom(it):
        h, gq, kp, nkb = it
        m0 = gq * 4
        kb0 = 2 * kp
        mk = kb0 // 4
        first = max(mk, m0) - m0
        return h, gq, m0, kb0, mk, first * 128, nkb

    def emit_qk(idx):
        h, gq, m0, kb0, mk, c0, nkb = geom(items[idx])
        hsl = h % 2
        sb_ = idx % 3
        b0 = sb_ * 2
        for u in range(2):
            MM(PS(b0 + u)[:, c0:512], kTh[hsl][0:96, (kb0 + u) * 128:(kb0 + u + 1) * 128],
               QT[0:96, h, m0 * 128 + c0:(m0 + 4) * 128], True, True, R=[t_kTh[hsl], t_qt], W=[pst[b0 + u]], inc=(u == 1))
        S2 = psum[:, b0 * 512:(b0 + 2) * 512].rearrange("p (u n) -> p u n", u=2)
        ACTV(PT3[sb_][:, :, c0:512], S2[:, :, c0:512], AF.Exp, R=[pst[b0], pst[b0 + 1], t_nb], W=[t_PT[sb_]],
             bias=nbias[:, 0:1], scale=scale)
        if mk >= m0:
            for u in range(2):
                TT(DVE, PT3[sb_][:, u, c0:c0 + 128], PT3[sb_][:, u, c0:c0 + 128], md[:, (kb0 + u) % 4, :], ALU.mult,
                   R=[t_PT[sb_], t_md], W=[t_PT[sb_]])

    def emit_pv(idx):
        h, gq, m0, kb0, mk, c0, nkb = geom(items[idx])
        hsl = h % 2
        sb_ = idx % 3
        ob = 6 + (h * 4 + gq) % 2
        po = PS(ob)
        if kb0 == 0 and gq == 0 and h + 1 < 8:
            load_head(h + 1)
        for u in range(2):
            kb = kb0 + u
            MM(po[:, c0:512], Va[hsl][:, kb, :], PT3[sb_][:, u, c0:512], kb == 0, kb == nkb - 1,
               R=[t_Va[hsl], t_PT[sb_]], W=[pst[ob]], inc=(u == 1))
        if kb0 + 2 == nkb:
            hp = h // 2
            if hsl == 0:
                nr, dr = slice(0, 64), slice(64, 128)
            else:
                nr, dr = slice(64, 128), slice(0, 64)
            P.emit(DVE, lambda e, o=rden[dr, :], i=po[dr, :]: e.reciprocal(out=o, in_=i), [pst[ob]], [t_rd])
            TT(DVE, rden[nr, :], po[nr, :], rden[dr, :], ALU.mult, R=[pst[ob], t_rd], W=[t_rd])
            dst = ymv[nr, 4 + hp, :, m0 * 16:m0 * 16 + 64]
            TT(DVE, dst, rden[nr, :].rearrange("p (c s) -> p s c", s=8), zm[nr, hp, :, m0 * 16:m0 * 16 + 64], ALU.mult,
               R=[t_rd, t_z], W=[t_ymix[4 + hp]])

    load_head(0)
    LOOK = 2
    for idx in range(min(LOOK, len(items))):
        emit_qk(idx)
    for idx in range(len(items)):
        if idx + LOOK < len(items):
            emit_qk(idx + LOOK)
        emit_pv(idx)
        tgt = (len(merged) * (idx + 1)) // max(1, len(items) - 8)
        if tgt > wg_pos[0]:
            wg_pump(tgt - wg_pos[0])
    wg_pump(len(merged))

    P.barrier()
    A.off = KB(141)
    RV = A([128, 2, 32, 15], F32)
    mark3 = A.off
    t_w1 = Tok()
    t_sg2 = Tok()
    P.dma(SP, Ymat, ymat_s, W=[t_ssmw])
    t_sg = Tok()
    tri = A([128, 128], F32)
    rci = AI([128, 1]); rcf = A([128, 1], F32); colf = A([128, 128], F32); coli = AI([128, 128])
    P.emit(POOL, lambda e: e.iota(rci, pattern=[[0, 1]], base=0, channel_multiplier=1), (), [t_sg])
    P.emit(DVE, lambda e: e.tensor_scalar(out=rci, in0=rci, scalar1=4, scalar2=4, op0=ALU.arith_shift_right,
                                          op1=ALU.logical_shift_left), [t_sg], [t_sg])
    CP(DVE, rcf, rci, R=[t_sg], W=[t_sg])
    P.emit(POOL, lambda e: e.iota(coli, pattern=[[1, 128]], base=0, channel_multiplier=0), (), [t_sg])
    CP(DVE, colf, coli, R=[t_sg], W=[t_sg])
    TS(DVE, tri, colf, rcf[:, 0:1], None, ALU.subtract, R=[t_sg], W=[t_sg])
    TS(DVE, tri, tri, -0.5, None, ALU.is_gt, R=[t_sg], W=[t_sg])
    dsd = A([128, 32], F32)
    P.dma(SP, dsd, dS, W=[t_sg])
    Xf = [A([128, 8, 128], F32), A([128, 8, 128], F32)]; Ykf = [A([128, 8, 128], F32), A([128, 8, 128], F32)]
    t_xf = [Tok(), Tok()]
    Xb3 = [A([128, 128], F32), A([128, 128], F32)]; t_xb = [Tok(), Tok()]
    t_k0 = [Tok() for _ in range(32)]; t_w1g = [Tok() for _ in range(32)]
    Af = [A([128, 8, 128], F32), A([128, 8, 128], F32)]
    for ob in range(4):
        sl = ob % 2
        P.dma(SP, Xf[sl], x_s[:, ob * 8:(ob + 1) * 8, :], W=[t_xf[sl]])
        P.dma(SP, Af[sl], a_s[:, ob * 8:(ob + 1) * 8, :], W=[t_xf[sl]])
        P.dma(SP, Ykf[sl], yk_s[:, ob * 8:(ob + 1) * 8, :], W=[t_xf[sl]])
        for gl in range(8):
            g = ob * 8 + gl
            pk = PS(g % 4, 128)
            MM(pk, Xf[sl][:, gl, :], Ykf[sl][:, gl, :], True, True, R=[t_xf[sl]], W=[pst[g % 4]])
            TT(DVE, Xb3[g % 2], pk, tri, ALU.mult, R=[pst[g % 4], t_sg], W=[t_xb[g % 2]])
            STT(DVE, K0[:, g, :], identf, dsd[:, g:g + 1], Xb3[g % 2], ALU.mult, ALU.add, R=[t_sg, t_ident, t_xb[g % 2]], W=[t_k0[g]])
            pw_ = PS(4 + g % 4, 128)
            TRN(pw_, Af[sl][:, gl, :], identf, R=[t_xf[sl], t_ident], W=[pst[4 + g % 4]])
            CP(ACT, W1[:, g, :], pw_, R=[pst[4 + g % 4]], W=[t_w1g[g]])
    P.dma(SP, RV, rv_s, W=[t_rv])
    wg32 = A([128, 4, 512], F32)
    P.dma(SP, wg32, w_glu.rearrange("(k p) n -> p k n", p=128), W=[t_sg2])
    CP(DVE, w_glu_b, wg32, R=[t_sg2], W=[t_wsm])
    P.barrier()
    A.off = mark3
    NW = 4
    Uo = [A([128, NW, 1024]), A([128, NW, 1024])]; t_Uo = [Tok(), Tok()]
    Uown = [A([128, NW, 256]), A([128, NW, 256])]; t_Uown = [Tok(), Tok()]
    Rg = [A([128, 15, 128]) for _ in range(NW)]; t_Rg = [Tok() for _ in range(NW)]
    XaA = [A([128, 1024]) for _ in range(NW)]; XaB = [A([128, 256]) for _ in range(NW)]
    t_XaA = [Tok() for _ in range(NW)]; t_XaB = [Tok() for _ in range(NW)]
    Sx = [A([128, 80]) for _ in range(NW)]; t_Sx = [Tok() for _ in range(NW)]
    S2o = [A([128, 16], F32) for _ in range(NW)]; t_S2o = [Tok() for _ in range(NW)]
    XoA = [A([128, 256]) for _ in range(NW)]; XoB = [A([128, 256]) for _ in range(NW)]
    t_XoA = [Tok() for _ in range(NW)]; t_XoB = [Tok() for _ in range(NW)]
    Yall = A([128, 8, 256]); t_Yall = Tok()
    t_ys = [Tok() for _ in range(4)]
    t_yg = Tok()
    _save = A.off
    A.off = KB(14)
    ygT = A([128, 4, 8, 256])
    w_out_b = A([128, 8, D]); t_wout = Tok()
    wo32 = [A([128, D], F32), A([128, D], F32)]; t_wo32 = [Tok(), Tok()]
    xr = [A([128, D], F32), A([128, D], F32)]; t_xr = [Tok(), Tok()]
    assert A.off <= KB(62), A.off
    A.off = _save
    ot = wo32; t_ot = t_wo32
    gate_sb = A([128, D], F32); t_gsb = Tok()
    gate_bc = [PS(6), PS(7)]
    for hf in range(2):
        MM(gate_bc[hf], ones_f[0:1, 0:128], gate_row[0:1, hf * 512:(hf + 1) * 512], True, True,
           R=[t_ones, t_grow], W=[pst[6 + hf]])
        CP(DVE, gate_sb[:, hf * 512:(hf + 1) * 512], gate_bc[hf], R=[pst[6 + hf]], W=[t_gsb])

    def wout_chunk(kc):
        sl = kc % 2
        P.dma(SP, wo32[sl], w_out[kc * 128:(kc + 1) * 128, :], W=[t_wo32[sl]])
        TT(DVE, w_out_b[:, kc, :], wo32[sl], gate_sb, ALU.mult, R=[t_wo32[sl], t_gsb], W=[t_wout])
    ident2 = A([128, 128]); t_id2 = Tok()
    CP(DVE, ident2, ident, R=[t_ident], W=[t_id2])
    TT(DVE, ident2[0:64, 64:128], ident2[0:64, 64:128], ident[0:64, 0:64], ALU.add, R=[t_ident, t_id2], W=[t_id2])
    TT(DVE, ident2[64:128, 0:64], ident2[64:128, 0:64], ident[64:128, 64:128], ALU.add, R=[t_ident, t_id2], W=[t_id2])
    for w in range(NW):
        MEMSET(DVE, Sx[w], 0.0, W=[t_Sx[w]])

    def evac(i, out, in_, R, W):
        CP(ACT, out, in_, R=R, W=W)

    pending_reload = []
    t_ygl = [[Tok() for _ in range(8)] for _ in range(4)]

    def do_reload(oc_):
        for gl_ in range(8):
            P.dma(SP, ygT[gl_ * 16:(gl_ + 1) * 16, oc_, :, :], ys_s[oc_ * 8 + gl_].rearrange("t h c -> h t c"),
                  R=[t_ys[oc_]], W=[t_ygl[oc_][gl_]])

    for wave in range(32 // NW):
        wsl = wave % 2
        g0 = wave * NW
        ch0 = g0 * 16
        for s in range(8):
            P.dma(SP, Uo[wsl][s * 16:(s + 1) * 16, :, :],
                  uT_s[ch0:ch0 + NW * 16, s, :].rearrange("(g h) c -> h g c", h=16), W=[t_Uo[wsl]])
            P.dma(SP, Uown[wsl][s * 16:(s + 1) * 16, :, :],
                  uTo_s[ch0:ch0 + NW * 16, s, :].rearrange("(g h) c -> h g c", h=16), W=[t_Uown[wsl]])
        wout_chunk(wave)
        while pending_reload and pending_reload[0] * 2 + 1 < wave:
            do_reload(pending_reload.pop(0))
        for w in range(NW):
            g = g0 + w
            for hf in range(2):
                TT(DVE if hf == 0 else POOL, Rg[w][:, :, hf * 64:(hf + 1) * 64],
                   bc(ident2[:, hf * 64:(hf + 1) * 64].rearrange("p (o q) -> p o q", o=1), [128, 15, 64]),
                   bc(RV[:, hf, g, :].rearrange("p (k o) -> p k o", o=1), [128, 15, 64]), ALU.mult,
                   R=[t_id2, t_rv], W=[t_Rg[w]])
            pz = PS(2 * w, 1024, nb=2)
            for hf in range(2):
                MM(pz[:, hf * 512:(hf + 1) * 512], W1[:, g, :], Uo[wsl][:, w, hf * 512:(hf + 1) * 512], True, True,
                   R=[t_w1g[g], t_Uo[wsl]], W=[pst[2 * w], pst[2 * w + 1]])
            evac(w, XaA[w], pz, [pst[2 * w], pst[2 * w + 1]], [t_XaA[w]])
        for lv, (n, idxs) in enumerate(((256, (2, 1, 0)), (64, (5, 4, 3)))):
            for w in range(NW):
                bk = 2 * w + lv % 2
                src, t_src = (XaA[w], t_XaA[w]) if lv == 0 else (XaB[w], t_XaB[w])
                dst, t_dst = (XaB[w], t_XaB[w]) if lv == 0 else (XaA[w], t_XaA[w])
                pzz = PS(bk, n)
                ev = src[:, 0:4 * n].rearrange("p (c four) -> p c four", four=4)
                for j in range(3):
                    MM(pzz, Rg[w][:, idxs[j], :], ev[:, :, j], j == 0, False, R=[t_Rg[w], t_src], W=[pst[bk]])
                MM(pzz, ident, ev[:, :, 3], False, True, R=[t_ident, t_src], W=[pst[bk]])
                evac(w + lv, dst[:, 0:n], pzz, [pst[bk]], [t_dst])
        for lv, (shs, idxs) in enumerate((((1, 2, 3), (6, 7, 8)), ((4, 8, 12), (9, 10, 11)), ((16, 32, 48), (12, 13, 14)))):
            for w in range(NW):
                bk = 2 * w + lv % 2
                src, t_src = (XaA[w], t_XaA[w]) if lv % 2 == 0 else (XaB[w], t_XaB[w])
                dst, t_dst = (XaB[w], t_XaB[w]) if lv % 2 == 0 else (XaA[w], t_XaA[w])
                pzz = PS(bk, 64)
                MM(pzz, ident, src[:, 0:64], True, False, R=[t_ident, t_src], W=[pst[bk]])
                for j in range(3):
                    shf = shs[j]
                    MM(pzz[:, shf:64], Rg[w][:, idxs[j], :], src[:, 0:64 - shf], False, j == 2,
                       R=[t_Rg[w], t_src], W=[pst[bk]])
                if lv < 2:
                    evac(w + lv, dst[:, 0:64], pzz, [pst[bk]], [t_dst])
                else:
                    evac(w + lv, Sx[w][:, 1:65], pzz, [pst[bk]], [t_Sx[w]])
        for w in range(NW):
            sx4 = Sx[w][:, 0:64].rearrange("p (m i) -> p m i", i=4)
            TS(DVE, S2o[w], sx4[:, :, 0], s_sel[:, 0:1], None, ALU.mult, R=[t_Sx[w], t_small], W=[t_S2o[w]])
            for i in range(1, 4):
                STT(DVE, S2o[w], sx4[:, :, i], s_sel[:, i:i + 1], S2o[w], ALU.mult, ALU.add,
                    R=[t_Sx[w], t_small, t_S2o[w]], W=[t_S2o[w]])
        for w in range(NW):
            g = g0 + w
            bk = 2 * w
            pzo = PS(bk, 256)
            MM(pzo, W1[:, g, :], Uown[wsl][:, w, :], True, True, R=[t_w1g[g], t_Uown[wsl]], W=[pst[bk]])
            xo3 = XoA[w].rearrange("p (i m) -> p i m", i=16)
            evac(w, xo3[:, 1:16, :], pzo.rearrange("p (m i) -> p i m", i=16)[:, 0:15, :], [pst[bk]], [t_XoA[w]])
            CP(DVE, xo3[:, 0, :], S2o[w], R=[t_S2o[w]], W=[t_XoA[w]])
        for lv, (shs, idxs) in enumerate((((1, 2, 3), (0, 1, 2)), ((4, 8, 12), (3, 4, 5)))):
            for w in range(NW):
                bk = 2 * w + (lv + 1) % 2
                src, t_src = (XoA[w], t_XoA[w]) if lv == 0 else (XoB[w], t_XoB[w])
                dst, t_dst = (XoB[w], t_XoB[w]) if lv == 0 else (XoA[w], t_XoA[w])
                pzz = PS(bk, 256)
                MM(pzz, ident, src, True, False, R=[t_ident, t_src], W=[pst[bk]])
                for j in range(3):
                    shf = shs[j]
                    MM(pzz[:, shf * 16:256], Rg[w][:, idxs[j], :], src[:, 0:(16 - shf) * 16], False, j == 2,
                       R=[t_Rg[w], t_src], W=[pst[bk]])
                if lv == 0:
                    evac(w + lv, dst, pzz, [pst[bk]], [t_dst])
                else:
                    evac(w + lv, dst.rearrange("p (m i) -> p i m", i=16), pzz.rearrange("p (i m) -> p i m", i=16),
                         [pst[bk]], [t_dst])
        for w in range(NW):
            g = g0 + w
            gl = g % 8
            bk = 2 * w
            py = PS(bk, 256)
            MM(py, K0[:, g, :], Uown[wsl][:, w, :], True, False, R=[t_k0[g], t_Uown[wsl]], W=[pst[bk]])
            MM(py, Ymat[:, g, :], XoA[w], False, True, R=[t_ssmw, t_XoA[w]], W=[pst[bk]])
            ACTV(Yall[:, gl, :], py, AF.Gelu_apprx_tanh, R=[pst[bk]], W=[t_Yall])
            if gl == 7:
                oc = g // 8
                P.dma(ACT, ys_s[oc * 8:(oc + 1) * 8].rearrange("g t h c -> (t h) g c"), Yall, R=[t_Yall], W=[t_ys[oc]])
                pending_reload.append(oc)
    while pending_reload:
        do_reload(pending_reload.pop(0))
    sg = A([128, 512]); t_sg_ = Tok()
    ymq = [ymix[:, 0:4, q4_ * 512:(q4_ + 1) * 512] for q4_ in range(4)]
    t_ymq = [Tok() for _ in range(4)]
    ygf = ygT.rearrange("p a s c -> p a (s c)")
    zsf = zs.rearrange("p a s c -> p a (s c)")
    zmf = zm.rearrange("p a s c -> p a (s c)")
    xo_v = xo.rearrange("(c s) d -> s c d", s=8)
    yo_v = y_out.rearrange("(c s) d -> s c d", s=8)
    t_out = Tok()
    for q4 in range(4):
        for co in range(4):
            b_ = (co * 4 + q4) % 2
            pg = PS(b_)
            for cc in range(4):
                MM(pg, w_glu_b[:, cc, co * 128:(co + 1) * 128], ygf[:, cc, q4 * 512:(q4 + 1) * 512], cc == 0, cc == 3,
                   R=[t_wsm] + t_ygl[cc], W=[pst[b_]])
            ACTV(sg, pg, AF.Sigmoid, R=[pst[b_], t_small], W=[t_sg_], bias=s_bglu[:, co:co + 1])
            TT(DVE, sg, sg, ygf[:, co, q4 * 512:(q4 + 1) * 512], ALU.mult, R=[t_sg_] + t_ygl[co], W=[t_sg_])
            TT(DVE, ymq[q4][:, co, :], sg, zsf[:, co, q4 * 512:(q4 + 1) * 512], ALU.mult,
               R=[t_sg_, t_z], W=[t_ymq[q4]])
        for st in range(q4 * 4, q4 * 4 + 4):
            sl = st % 2
            s_ = st // 2
            c0 = (st % 2) * 128
            if st == 0:
                P.dma(SP, xr[0], xo_v[0, 0:128, :], W=[t_xr[0]])
            if st + 1 < 16:
                P.dma(SP, xr[(st + 1) % 2], xo_v[(st + 1) // 2, ((st + 1) % 2) * 128:((st + 1) % 2) * 128 + 128, :],
                      W=[t_xr[(st + 1) % 2]])
            pf = PS(2 + 2 * sl, 1024, nb=2)
            for hf in range(2):
                for kc in range(8):
                    if kc < 4:
                        lh = ymq[q4][:, kc, (st % 4) * 128:(st % 4 + 1) * 128]
                        rr = [t_ymq[q4], t_wout]
                    else:
                        lh = ymix[:, kc, st * 128:(st + 1) * 128]
                        rr = [t_ymix[kc], t_wout]
                    MM(pf[:, hf * 512:(hf + 1) * 512], lh, w_out_b[:, kc, hf * 512:(hf + 1) * 512],
                       kc == 0, kc == 7, R=rr, W=[pst[2 + 2 * sl], pst[3 + 2 * sl]], inc=(hf == 1 and kc == 7))
            TT(DVE, ot[sl], pf, xr[sl], ALU.add, R=[pst[2 + 2 * sl], pst[3 + 2 * sl], t_xr[sl]], W=[t_ot[sl]])
            P.dma(SP, yo_v[s_, c0:c0 + 128, :], ot[sl], R=[t_ot[sl]], W=[t_out])
    P.barrier()

    with nc.Block() as block:
        def run(E):
            def f(e):
                for waits, fn, inc in E.ops:
                    for key, val in waits:
                        e.wait_ge(sems[key], val)
                    if fn is not None:
                        ins_ = fn(e)
                        if inc is not None:
                            ins_.then_inc(sems[inc[0]], inc[1])
            return f
        block.tensor(run(PE))
        block.scalar(run(ACT))
        block.vector(run(DVE))
        block.gpsimd(run(POOL))
        block.sync(run(SP))
    es.close()
    return nc


_NC = [None]


def _prep(c, x, cvec, positions, w_ada, b_ada, norm_g, w_in, log_dt, lam_re, lam_im, b_re, b_im, c_re, c_im,
          d_skip, w_glu, b_glu, q_a_g, w_q_b, kv_a_g, w_kv_b, q_norm_g, k_norm_g, w_out):
    b = c // 4
    j = c % 4
    f = np.float32
    ac = np.ascontiguousarray
    own_blocks = [4 * m + j for m in range(NO)]
    xbv = ac(x[b])
    xov = ac(np.concatenate([x[b, blk * 128:(blk + 1) * 128] for blk in own_blocks], axis=0))
    pos = positions[b].astype(np.int32)
    posb = ac(pos.reshape(NT, 128).T)
    poso = ac(np.stack([pos[blk * 128:(blk + 1) * 128] for blk in own_blocks], axis=1))

    def T128(v):
        return ac(v.reshape(-1, 128).T.astype(f))

    lamP = np.zeros((128, 2, 32), f)
    for hf in range(2):
        lamP[hf * 64:(hf + 1) * 64, 0, :] = lam_re[0].T
        lamP[hf * 64:(hf + 1) * 64, 1, :] = lam_im[0].T
    bPv = np.zeros((128, 2, 32, 16), f)
    cPv = np.zeros((128, 2, 32, 16), f)
    for hf in range(2):
        bPv[hf * 64:(hf + 1) * 64, 0] = b_re[0].transpose(1, 0, 2)
        bPv[hf * 64:(hf + 1) * 64, 1] = b_im[0].transpose(1, 0, 2)
        cPv[hf * 64:(hf + 1) * 64, 0] = c_re[0].transpose(2, 0, 1)
        cPv[hf * 64:(hf + 1) * 64, 1] = c_im[0].transpose(2, 0, 1)
    lamS = np.zeros((128, 2, 32, 64), f)
    lamS[:, 0] = lam_re[0][None]
    lamS[:, 1] = lam_im[0][None]
    bSv = np.zeros((128, 2, 32, 64), f)
    dSv = np.zeros((128, 32), f)
    for s in range(8):
        bSv[s * 16:(s + 1) * 16, 0] = b_re[0].transpose(2, 0, 1)
        bSv[s * 16:(s + 1) * 16, 1] = b_im[0].transpose(2, 0, 1)
        dSv[s * 16:(s + 1) * 16, :] = d_skip[0].T
    kk = np.arange(128)[:, None, None]
    ii = np.arange(4)[None, :, None]
    qq = np.arange(128)[None, None, :]
    mdiag = ((128 * ii + kk) <= (128 * j + qq)).astype(f)
    sel = np.zeros((128, 4), f)
    sel[:, j] = 1.0
    return {
        "xb": xbv, "xo": xov, "posb": posb, "poso": poso,
        "cT": T128(cvec[b]), "w_ada": ac(w_ada[0]), "b_adaT": T128(b_ada[0]), "b_gate": ac(b_ada[0][None, 2 * D:3 * D]),
        "norm_gT": T128(norm_g[0]), "w_in": ac(w_in[0]), "w_q_b": ac(w_q_b[0]), "q_a_gT": T128(q_a_g[0]),
        "w_kv_b": ac(w_kv_b[0]), "kv_a_gT": T128(kv_a_g[0]),
        "qg_rep": ac(np.broadcast_to(q_norm_g[0][None], (128, 96)).astype(f)),
        "kg_rep": ac(np.broadcast_to(k_norm_g[0][None], (128, 96)).astype(f)),
        "w_glu": ac(w_glu[0]), "b_gluT": T128(b_glu[0]), "w_out": ac(w_out[0]),
        "ldt_rep": ac(np.broadcast_to(log_dt[0][None], (128, 32)).astype(f)),
        "lamP": lamP, "bP": bPv, "cP": cPv, "lamS": lamS, "bS": bSv, "dS": dSv,
        "mdiag": ac(mdiag), "sel4": sel,
    }


def kernel(**inputs):
    inp = {k: np.asarray(v) for k, v in inputs.items()}
    if _NC[0] is None:
        _NC[0] = build()
    nc = _NC[0]
    args = (inp["x"], inp["c"], inp["positions"], inp["w_ada"], inp["b_ada"], inp["norm_g"], inp["w_in"],
            inp["log_dt"], inp["lam_re"], inp["lam_im"], inp["b_re"], inp["b_im"], inp["c_re"], inp["c_im"],
            inp["d_skip"], inp["w_glu"], inp["b_glu"], inp["q_a_g"], inp["w_q_b"], inp["kv_a_g"], inp["w_kv_b"],
            inp["q_norm_g"], inp["k_norm_g"], inp["w_out"])
    in_maps = [_prep(c, *args) for c in range(8)]
    res = run_bass_kernel_spmd(nc, in_maps, core_ids=list(range(8)))
    out = np.zeros((2, L, D), np.float32)
    for c in range(8):
        b, j = c // 4, c % 4
        y = res.results[c]["y"]
        for m in range(NO):
            blk = 4 * m + j
            out[b, blk * 128:(blk + 1) * 128] = y[m * 128:(m + 1) * 128]
    return out
```

```python
import math
from contextlib import ExitStack
import numpy as np
import concourse.bass as bass
import concourse.mybir as mybir
from concourse.bass_utils import run_bass_kernel_spmd

F32 = mybir.dt.float32
BF16 = mybir.dt.bfloat16
I32 = mybir.dt.int32
ALU = mybir.AluOpType
AF = mybir.ActivationFunctionType
AX = mybir.AxisListType

D = 1024
L = 8192
NT = 64
NO = 16
EPS = 1e-6
TWO_PI = 2.0 * math.pi
TWO_PI_HI = float(np.float32(TWO_PI))
TWO_PI_LO = TWO_PI - TWO_PI_HI
IFS = -math.log(10000.0) / 16.0
IFS_HI = float(np.float32(IFS))
IFS_LO = IFS - IFS_HI
DEBUG = False


class Tok:
    __slots__ = ("w", "r", "wdma")

    def __init__(self):
        self.w = []
        self.r = {}
        self.wdma = False


class Eng:
    def __init__(self, name, nsem=0):
        self.name = name
        self.key = (name, "c")
        self.ops = []
        self.known = {}
        self.count = 0
        self.nsem = nsem
        self.dnext = 0
        self.duses = [0] * nsem


class Prog:
    def __init__(self):
        self.pe = Eng("pe")
        self.act = Eng("act", nsem=8)
        self.dve = Eng("dve")
        self.pool = Eng("pool", nsem=14)
        self.sp = Eng("sp", nsem=24)
        self.engs = [self.pe, self.act, self.dve, self.pool, self.sp]

    def _deps(self, E, R, W, is_dma=False):
        deps = []
        for t in R:
            deps.extend(t.w)
        for t in W:
            if not (is_dma and t.wdma and not t.r):
                deps.extend(t.w)
            deps.extend(t.r.items())
        return deps

    def _waits(self, E, deps):
        waits = {}
        for key, val in deps:
            if key == E.key and E.name == "pe":
                continue
            if E.known.get(key, 0) >= val:
                continue
            if waits.get(key, 0) < val:
                waits[key] = val
        for k, v in waits.items():
            E.known[k] = v
        return list(waits.items())

    def _mark(self, ev, R, W, is_dma=False):
        for t in R:
            if t.r.get(ev[0], 0) < ev[1]:
                t.r[ev[0]] = ev[1]
        for t in W:
            if is_dma and t.wdma and not t.r:
                t.w = t.w + [ev]
            else:
                t.w = [ev]
            t.wdma = is_dma
            t.r = {}

    def emit(self, E, fn, R=(), W=(), inc=True):
        waits = self._waits(E, self._deps(E, R, W))
        if inc:
            E.count += 1
            ev = (E.key, E.count)
            E.ops.append((waits, fn, (E.key, 1)))
        else:
            ev = (E.key, E.count + 1)
            E.ops.append((waits, fn, None))
        self._mark(ev, R, W)

    def dma(self, Q, out, in_, R=(), W=()):
        i = Q.dnext % Q.nsem
        Q.dnext += 1
        key = (Q.name, "d", i)
        n = Q.duses[i]
        deps = self._deps(Q, R, W, is_dma=True)
        if n > 0:
            deps.append((key, 16 * n))
        Q.duses[i] = n + 1
        waits = self._waits(Q, deps)
        ev = (key, 16 * (n + 1))
        Q.ops.append((waits, (lambda e, o=out, s=in_: e.dma_start(out=o, in_=s)), (key, 16)))
        self._mark(ev, R, W, is_dma=True)

    def barrier(self):
        evs = []
        for E in self.engs:
            if E.count:
                evs.append((E.key, E.count))
            for i in range(E.nsem):
                if E.duses[i]:
                    evs.append(((E.name, "d", i), 16 * E.duses[i]))
        for E in self.engs:
            w = self._waits(E, [e for e in evs if not (e[0] == E.key)])
            if w:
                E.ops.append((w, None, None))

    def barrier_on(self, E):
        evs = []
        for X in self.engs:
            if X.count and X is not E:
                evs.append((X.key, X.count))
            for i in range(X.nsem):
                if X.duses[i]:
                    evs.append(((X.name, "d", i), 16 * X.duses[i]))
        w = self._waits(E, evs)
        if w:
            E.ops.append((w, None, None))

    def all_keys(self):
        ks = []
        for E in self.engs:
            ks.append(E.key)
            for i in range(E.nsem):
                ks.append((E.name, "d", i))
        return ks


def build():
    nc = bass.Bass("TRN2", target_bir_lowering=False)
    P = Prog()
    PE, ACT, DVE, POOL, SP = P.pe, P.act, P.dve, P.pool, P.sp

    def din(name, shape, dt=F32):
        return nc.dram_tensor(name, list(shape), dt, kind="ExternalInput").ap()

    xb = din("xb", [L, D])
    xo = din("xo", [NO * 128, D])
    posb = din("posb", [128, NT], I32)
    poso = din("poso", [128, NO], I32)
    cT = din("cT", [128, 8])
    w_ada = din("w_ada", [D, 3 * D])
    b_adaT = din("b_adaT", [128, 24])
    b_gate = din("b_gate", [1, D])
    norm_gT = din("norm_gT", [128, 8])
    w_in = din("w_in", [D, 2208])
    w_q_b = din("w_q_b", [384, 768])
    q_a_gT = din("q_a_gT", [128, 3])
    w_kv_b = din("w_kv_b", [256, 1024])
    kv_a_gT = din("kv_a_gT", [128, 2])
    qg_rep = din("qg_rep", [128, 96])
    kg_rep = din("kg_rep", [128, 96])
    w_glu = din("w_glu", [512, 512])
    b_gluT = din("b_gluT", [128, 4])
    w_out = din("w_out", [D, D])
    ldt_rep = din("ldt_rep", [128, 32])
    lamP = din("lamP", [128, 2, 32])
    bP = din("bP", [128, 2, 32, 16])
    cP = din("cP", [128, 2, 32, 16])
    lamS = din("lamS", [128, 2, 32, 64])
    bS = din("bS", [128, 2, 32, 64])
    dS = din("dS", [128, 32])
    mdiag = din("mdiag", [128, 4, 128])
    sel4 = din("sel4", [128, 4])
    y_out = nc.dram_tensor("y", [NO * 128, D], F32, kind="ExternalOutput").ap()
    kT_s = nc.dram_tensor("kT_s", [8, 96, L], BF16).ap()
    v_s = nc.dram_tensor("v_s", [L, 512], BF16).ap()
    uT_s = nc.dram_tensor("uT_s", [512, 8, 1024], BF16).ap()
    uTo_s = nc.dram_tensor("uTo_s", [512, 8, 256], BF16).ap()
    ys_s = nc.dram_tensor("ys_s", [32, 8, 16, 256], BF16).ap()
    w1_s = nc.dram_tensor("w1_s", [128, 32, 128], BF16).ap()
    ymat_s = nc.dram_tensor("ymat_s", [128, 32, 128], BF16).ap()
    x_s = nc.dram_tensor("x_s", [128, 32, 128], F32).ap()
    yk_s = nc.dram_tensor("yk_s", [128, 32, 128], F32).ap()
    a_s = nc.dram_tensor("a_s", [128, 32, 128], F32).ap()
    rv_s = nc.dram_tensor("rv_s", [128, 2, 32, 15], F32).ap()

    es = ExitStack()
    ARENA = 105184
    IAR = 520
    arena = es.enter_context(nc.sbuf_tensor("arena", [128, ARENA], BF16))
    iarena = es.enter_context(nc.sbuf_tensor("iarena", [128, IAR], I32))
    psum = es.enter_context(nc.psum_tensor("psum", [128, 4096], F32))
    sems = {}
    for k in P.all_keys():
        sems[k] = es.enter_context(nc.semaphore("s_" + "_".join(str(x) for x in k)))

    class Alloc:
        def __init__(self):
            self.off = 0
            self.hole = None

        def __call__(self, shape, dt=BF16):
            n = 1
            for s in shape[1:]:
                n *= s
            nb = n * (4 if dt in (F32, I32) else 2)
            nb = (nb + 63) // 64 * 64
            o = self.off
            if self.hole is not None and o < self.hole[1] and o + nb // 2 > self.hole[0]:
                o = self.hole[1]
            self.off = o + nb // 2
            assert self.off <= ARENA, ("arena overflow", self.off)
            v = arena[0:shape[0], o:o + nb // 2]
            if dt != BF16:
                v = v.bitcast(dt)
            v = v[:, 0:n]
            if len(shape) == 3:
                v = v.rearrange("p (a b) -> p a b", a=shape[1])
            elif len(shape) == 4:
                v = v.rearrange("p (a b c) -> p a b c", a=shape[1], b=shape[2])
            return v

    A = Alloc()
    ioff = [0]

    def AI(shape):
        n = 1
        for d_ in shape[1:]:
            n *= d_
        o = ioff[0]
        ioff[0] += n
        assert ioff[0] <= IAR, ioff[0]
        v = iarena[0:shape[0], o:o + n]
        if len(shape) == 3:
            v = v.rearrange("p (a b) -> p a b", a=shape[1])
        return v

    def KB(k):
        return int(k * 512)

    def PS(bank, n=512, dt=F32, parts=128, nb=1):
        v = psum[0:parts, bank * 512:(bank + nb) * 512]
        if dt == BF16:
            v = v.bitcast(BF16)
        return v[:, 0:n]

    pst = [Tok() for _ in range(8)]

    def MM(out, lhsT, rhs, start, stop, R=(), W=(), inc=True):
        P.emit(PE, lambda e: e.matmul(out, lhsT=lhsT, rhs=rhs, start=start, stop=stop), R, W, inc=inc)

    def TRN(out, in_, ident_ap, R=(), W=(), inc=True):
        P.emit(PE, lambda e: e.transpose(out=out, in_=in_, identity=ident_ap), R, W, inc=inc)

    def ACTV(out, in_, func, R=(), W=(), bias=None, scale=None, accum=None):
        kw = {}
        if bias is not None:
            kw["bias"] = bias
        if scale is not None:
            kw["scale"] = scale
        if accum is not None:
            kw["accum_out"] = accum
        P.emit(ACT, lambda e: e.activation(out=out, in_=in_, func=func, **kw), R, W)

    def TS(E, out, in0, s1, s2, op0, op1=None, R=(), W=()):
        if op1 is None:
            P.emit(E, lambda e: e.tensor_scalar(out=out, in0=in0, scalar1=s1, scalar2=None, op0=op0), R, W)
        else:
            P.emit(E, lambda e: e.tensor_scalar(out=out, in0=in0, scalar1=s1, scalar2=s2, op0=op0, op1=op1), R, W)

    def TT(E, out, in0, in1, op, R=(), W=()):
        P.emit(E, lambda e: e.tensor_tensor(out=out, in0=in0, in1=in1, op=op), R, W)

    def STT(E, out, in0, scalar, in1, op0, op1, R=(), W=()):
        P.emit(E, lambda e: e.scalar_tensor_tensor(out=out, in0=in0, scalar=scalar, in1=in1, op0=op0, op1=op1), R, W)

    def CP(E, out, in_, R=(), W=()):
        if E is ACT:
            P.emit(E, lambda e: e.activation(out=out, in_=in_, func=AF.Copy), R, W)
        else:
            P.emit(E, lambda e: e.tensor_copy(out=out, in_=in_), R, W)

    def RED(E, out, in_, op, R=(), W=()):
        P.emit(E, lambda e: e.tensor_reduce(out=out, in_=in_, axis=AX.X, op=op), R, W)

    def MEMSET(E, ap, val, W=()):
        P.emit(E, lambda e: e.memset(ap, val), (), W)

    def bc(ap, shape):
        return ap.to_broadcast(list(shape))

    tk = Tok
    ident = A([128, 128]); t_ident = Tok()
    identf = A([128, 128], F32)
    ones_bf = A([1, 512]); ones_f = A([1, 128], F32); t_ones = Tok()
    mhalf = A([128, 8], F32)
    c_act = A([128, 8], F32); t_cact = Tok()
    modT = A([128, 24], F32); t_mod = Tok()
    gs = A([128, 8], F32)
    gate_row = A([1, D], F32); t_grow = Tok()
    biasrow = A([1, 2208]); t_brow = Tok()
    biasT = A([128, 12], F32)
    small = A([128, 64], F32); t_small = Tok()
    qg = A([128, 96], F32); kg = A([128, 96], F32); t_g = Tok()
    nbias = A([128, 1], F32); t_nb = Tok()
    invf = A([128, 16], F32)
    md = A([128, 4, 128]); t_md = Tok()
    assert A.off <= KB(14), A.off
    A.off = KB(14)
    QT = A([128, 8, NO * 128]); t_qt = Tok()
    zm = A([128, 4, 8, 256]); zs = A([128, 4, 8, 256]); t_z = Tok()
    w_in_b = A([128, 8, 2208]); t_win = Tok()
    w_qb_b = A([128, 3, 768]); w_kvb_b = A([128, 2, 1024]); t_wsm = Tok()
    cosb = A([128, NT, 16], F32); sinb = A([128, NT, 16], F32); t_ropeb = Tok()
    coso = A([128, NO, 16], F32); sino = A([128, NO, 16], F32); t_ropeo = Tok()
    assert A.off <= KB(131), A.off
    A.off = KB(78)
    ymix = A([128, 8, NO * 128]); t_ymix = [Tok() for _ in range(8)]
    W1 = A([128, 32, 128]); Ymat = A([128, 32, 128]); K0 = A([128, 32, 128]); t_ssmw = Tok()
    t_rv = Tok()
    w_glu_b = A([128, 4, 512])
    assert A.off <= KB(141), A.off
    base_off = KB(131)
    A.off = base_off

    sb_adaT = small[:, 0:24]; s_normg = small[:, 24:32]; s_qag = small[:, 32:35]; s_kvag = small[:, 35:37]
    s_bglu = small[:, 37:41]; s_sel = small[:, 41:45]

    P.dma(SP, small[:, 0:24], b_adaT, W=[t_small])
    P.dma(SP, small[:, 24:32], norm_gT, W=[t_small])
    P.dma(SP, small[:, 32:35], q_a_gT, W=[t_small])
    P.dma(SP, small[:, 35:37], kv_a_gT, W=[t_small])
    P.dma(SP, small[:, 37:41], b_gluT, W=[t_small])
    P.dma(SP, small[:, 41:45], sel4, W=[t_small])
    P.dma(SP, qg, qg_rep, W=[t_g])
    P.dma(SP, kg, kg_rep, W=[t_g])
    P.dma(SP, c_act, cT, W=[t_cact])
    P.dma(SP, gate_row, b_gate, W=[t_grow])
    MEMSET(POOL, ident, 1.0, W=[t_ident])
    P.emit(POOL, lambda e: e.affine_select(out=ident, in_=ident, pattern=[[1, 128]], compare_op=ALU.is_equal,
                                           fill=0.0, base=0, channel_multiplier=-1), [t_ident], [t_ident])
    MEMSET(POOL, identf, 1.0, W=[t_ident])
    P.emit(POOL, lambda e: e.affine_select(out=identf, in_=identf, pattern=[[1, 128]], compare_op=ALU.is_equal,
                                           fill=0.0, base=0, channel_multiplier=-1), [t_ident], [t_ident])
    MEMSET(POOL, ones_bf, 1.0, W=[t_ones])
    MEMSET(POOL, ones_f, 1.0, W=[t_ones])
    MEMSET(POOL, mhalf, -0.5, W=[t_ones])

    def rsqrt_mean(out, ssq, n, width, R, W):
        TS(POOL, out, ssq, 1.0 / n, EPS, ALU.mult, ALU.add, R=R, W=W)
        TT(POOL, out, out, mhalf[:, 0:width], ALU.pow, R=list(W) + [t_ones], W=W)

    def sincos(ang, sin_out, cos_out, tmp, tmpi, toks, E=None):
        E = DVE if E is None else E

        def STT(E_, out, in0, scalar, in1, op0, op1, R=(), W=()):
            if E_ is DVE:
                P.emit(E_, lambda e: e.scalar_tensor_tensor(out=out, in0=in0, scalar=scalar, in1=in1, op0=op0, op1=op1), R, W)
            else:
                TS(E_, in0, in0, scalar, None, op0, R=R, W=W)
                TT(E_, out, in0, in1, op1, R=R, W=W)

        def reduce_into(dst, src, shift):
            TS(E, tmp, src, shift, 1.0 / TWO_PI, ALU.add, ALU.mult, R=toks, W=toks)
            CP(E, tmpi, tmp, R=toks, W=toks)
            CP(E, tmp, tmpi, R=toks, W=toks)
            STT(E, dst, tmp, -TWO_PI_HI, src, ALU.mult, ALU.add, R=toks, W=toks)
            STT(E, dst, tmp, -TWO_PI_LO, dst, ALU.mult, ALU.add, R=toks, W=toks)
            if shift != 0.0:
                TS(E, dst, dst, shift, None, ALU.add, R=toks, W=toks)
            TS(E, tmp, dst, math.pi, None, ALU.is_gt, R=toks, W=toks)
            STT(E, dst, tmp, -TWO_PI, dst, ALU.mult, ALU.add, R=toks, W=toks)
            TS(E, tmp, dst, -1.0, None, ALU.mult, R=toks, W=toks)
            TS(E, tmp, tmp, math.pi, None, ALU.is_gt, R=toks, W=toks)
            STT(E, dst, tmp, TWO_PI, dst, ALU.mult, ALU.add, R=toks, W=toks)
            TS(E, dst, dst, math.pi, -math.pi, ALU.min, ALU.max, R=toks, W=toks)
        reduce_into(sin_out, ang, 0.0)
        TS(E, cos_out, sin_out, math.pi / 2.0, None, ALU.add, R=toks, W=toks)
        TS(E, tmp, cos_out, math.pi, None, ALU.is_gt, R=toks, W=toks)
        STT(E, cos_out, tmp, -TWO_PI, cos_out, ALU.mult, ALU.add, R=toks, W=toks)
        TS(E, cos_out, cos_out, math.pi, -math.pi, ALU.min, ALU.max, R=toks, W=toks)
        ACTV(cos_out, cos_out, AF.Sin, R=toks, W=toks)
        ACTV(sin_out, sin_out, AF.Sin, R=toks, W=toks)

    mark0 = KB(14)
    A.off = mark0
    ACTV(c_act, c_act, AF.Silu, R=[t_cact], W=[t_cact])
    wst = [A([128, 8, 512], F32), A([128, 8, 512], F32)]
    t_wst = [Tok(), Tok()]
    ps_mod = PS(0, 24)
    ps_grow = [PS(1, 512, parts=1), PS(2, 512, parts=1)]
    ada_n = [0]

    def ada_chunk(ch):
        sl = ada_n[0] % 2
        ada_n[0] += 1
        P.dma(SP, wst[sl], w_ada[:, ch * 512:(ch + 1) * 512].rearrange("(k p) n -> p k n", p=128), W=[t_wst[sl]])
        for c4 in range(4 if ch < 4 else 0):
            cc = ch * 4 + c4
            for kc in range(8):
                MM(ps_mod[:, cc:cc + 1], wst[sl][:, kc, c4 * 128:(c4 + 1) * 128], c_act[:, kc:kc + 1],
                   kc == 0, kc == 7, R=[t_wst[sl], t_cact], W=[pst[0]])
        if ch >= 4:
            for kc in range(8):
                MM(ps_grow[ch - 4], c_act[:, kc:kc + 1], wst[sl][:, kc, :], kc == 0, kc == 7,
                   R=[t_wst[sl], t_cact], W=[pst[1 + ch - 4]])

    ada_chunk(2)
    ada_chunk(3)
    t_gs = Tok()
    TT(DVE, modT[:, 8:16], ps_mod[:, 8:16], sb_adaT[:, 8:16], ALU.add, R=[pst[0], t_small], W=[t_gs])
    STT(DVE, gs, modT[:, 8:16], 1.0, s_normg, ALU.add, ALU.mult, R=[t_gs, t_small], W=[t_gs])
    ada_chunk(0)
    ada_chunk(1)
    TT(DVE, modT[:, 0:8], ps_mod[:, 0:8], sb_adaT[:, 0:8], ALU.add, R=[pst[0], t_small], W=[t_mod])
    wst2 = [A([128, 2208], F32), A([128, 2208], F32)]
    t_wst2 = [Tok(), Tok()]
    ps_brow = [PS(1 + i, 512, parts=1) for i in range(5)]
    for kc in range(8):
        sl = kc % 2
        P.dma(POOL, wst2[sl], w_in[kc * 128:(kc + 1) * 128, :], W=[t_wst2[sl]])
        if kc % 2 == 0:
            TS(DVE, w_in_b[:, kc, :], wst2[sl], gs[:, kc:kc + 1], None, ALU.mult, R=[t_wst2[sl], t_gs], W=[t_win])
        else:
            ACTV(w_in_b[:, kc, :], wst2[sl], AF.Copy, R=[t_wst2[sl], t_gs], W=[t_win], scale=gs[:, kc:kc + 1])
        for i in range(5):
            n0 = i * 512
            n1 = min(2208, n0 + 512)
            MM(ps_brow[i][:, 0:n1 - n0], modT[:, kc:kc + 1], wst2[sl][:, n0:n1], kc == 0, kc == 7,
               R=[t_wst2[sl], t_mod], W=[pst[1 + i]])
    for i in range(5):
        n0 = i * 512
        n1 = min(2208, n0 + 512)
        CP(DVE, biasrow[:, n0:n1], ps_brow[i][:, 0:n1 - n0], R=[pst[1 + i]], W=[t_brow])
    ada_chunk(4)
    ada_chunk(5)
    for hf in range(2):
        TT(DVE, gate_row[:, hf * 512:(hf + 1) * 512], ps_grow[hf], gate_row[:, hf * 512:(hf + 1) * 512], ALU.add,
           R=[pst[1 + hf], t_grow], W=[t_grow])
    FM_COLS = [0, 128, 256, 384, 512, 640, 768, 896, 1696, 1824, 1952, 2080]
    ps_bt = PS(6, 16)
    for j_, c0_ in enumerate(FM_COLS):
        MM(ps_bt[:, j_:j_ + 1], biasrow[0:1, c0_:c0_ + 128], ones_bf[0:1, 0:1], True, True, R=[t_brow, t_ones], W=[pst[6]])
    CP(DVE, biasT, ps_bt[:, 0:12], R=[pst[6]], W=[t_brow])
    for kc in range(3):
        sl = kc % 2
        P.dma(SP, wst2[sl][:, 0:768], w_q_b[kc * 128:(kc + 1) * 128, :], W=[t_wst2[sl]])
        ACTV(w_qb_b[:, kc, :], wst2[sl][:, 0:768], AF.Copy, R=[t_wst2[sl], t_small], W=[t_wsm], scale=s_qag[:, kc:kc + 1])
    for kc in range(2):
        sl = (kc + 1) % 2
        P.dma(SP, wst2[sl][:, 0:1024], w_kv_b[kc * 128:(kc + 1) * 128, :], W=[t_wst2[sl]])
        TS(DVE, w_kvb_b[:, kc, :], wst2[sl][:, 0:1024], s_kvag[:, kc:kc + 1], None, ALU.mult, R=[t_wst2[sl], t_small], W=[t_wsm])
    P.dma(SP, wst2[0][:, 0:512].rearrange("p (a b) -> p a b", a=4), mdiag, W=[t_wst2[0]])
    CP(DVE, md, wst2[0][:, 0:512].rearrange("p (a b) -> p a b", a=4), R=[t_wst2[0]], W=[t_md])
    tq = A([128, 96], F32); tmx = A([128, 2], F32); t_tmp0 = Tok()
    TS(DVE, tq, qg, -1.0, None, ALU.mult, R=[t_g], W=[t_tmp0])
    TT(DVE, tq, tq, qg, ALU.max, R=[t_g, t_tmp0], W=[t_tmp0])
    RED(DVE, tmx[:, 0:1], tq, ALU.max, R=[t_tmp0], W=[t_tmp0])
    TS(DVE, tq, kg, -1.0, None, ALU.mult, R=[t_g, t_tmp0], W=[t_tmp0])
    TT(DVE, tq, tq, kg, ALU.max, R=[t_g, t_tmp0], W=[t_tmp0])
    RED(DVE, tmx[:, 1:2], tq, ALU.max, R=[t_tmp0], W=[t_tmp0])
    TT(DVE, nbias, tmx[:, 0:1], tmx[:, 1:2], ALU.mult, R=[t_tmp0], W=[t_nb])
    TS(DVE, nbias, nbias, -math.sqrt(96.0), None, ALU.mult, R=[t_nb], W=[t_nb])
    t_rp = Tok()
    ii = AI([128, 16])
    P.emit(POOL, lambda e: e.iota(ii, pattern=[[1, 16]], base=0, channel_multiplier=0), (), [t_rp])
    CP(DVE, invf, ii, R=[t_rp], W=[t_rp])
    invc = A([128, 16], F32)
    TS(DVE, invc, invf, IFS_LO, 1.0, ALU.mult, ALU.add, R=[t_rp], W=[t_rp])
    ACTV(invf, invf, AF.Exp, R=[t_rp], W=[t_rp], scale=IFS_HI)
    TT(DVE, invf, invf, invc, ALU.mult, R=[t_rp], W=[t_rp])
    pbi = AI([128, NT]); pbf = A([128, NT], F32); poi = AI([128, NO]); pof = A([128, NO], F32)
    P.dma(SP, pbi, posb, W=[t_rp])
    P.dma(SP, poi, poso, W=[t_rp])
    CP(DVE, pbf, pbi, R=[t_rp], W=[t_rp])
    CP(DVE, pof, poi, R=[t_rp], W=[t_rp])
    angb = A([128, NO, 16], F32); tmpb = A([128, NO, 16], F32); tmpbi = AI([128, NO, 16])
    rope_ops = []
    _re, _rd = P.emit, P.dma
    for ch_ in range(NT // NO):
        if ch_ == 1:
            P.emit = lambda E, fn, R=(), W=(): rope_ops.append((E, fn, tuple(R), tuple(W)))
        sl_ = slice(ch_ * NO, (ch_ + 1) * NO)
        TT(DVE, angb, bc(pbf[:, sl_].rearrange("p (t o) -> p t o", o=1), [128, NO, 16]),
           bc(invf.rearrange("p (o i) -> p o i", o=1), [128, NO, 16]), ALU.mult, R=[t_rp], W=[t_rp])
        sincos(angb, sinb[:, sl_, :], cosb[:, sl_, :], tmpb, tmpbi, [t_rp, t_ropeb])
    TT(DVE, angb, bc(pof.rearrange("p (t o) -> p t o", o=1), [128, NO, 16]),
       bc(invf.rearrange("p (o i) -> p o i", o=1), [128, NO, 16]), ALU.mult, R=[t_rp], W=[t_rp])
    sincos(angb, sino, coso, tmpb, tmpbi, [t_rp, t_ropeo])
    P.emit, P.dma = _re, _rd
    rope_pos = [0]

    def rope_pump(n):
        for _ in range(n):
            if rope_pos[0] >= len(rope_ops):
                return
            E_, fn_, R_, W_ = rope_ops[rope_pos[0]]
            rope_pos[0] += 1
            P.emit(E_, fn_, R_, W_)

    assert A.off <= KB(78), A.off
    A.off = base_off
    own_sync = [False]

    def own_region_sync():
        if not own_sync[0]:
            own_sync[0] = True
            P.barrier_on(ACT)
    mark1 = A.off
    NX = 3
    xt = [A([128, D], F32) for _ in range(NX)]; t_xt = [Tok() for _ in range(NX)]
    xs = [A([128, D]), A([128, D])]; t_xs = [Tok(), Tok()]
    junk = A([128, 384]); t_junk = Tok()
    NSTAT = 5
    st8 = A([128, NSTAT, 32], F32); t_st = [Tok() for _ in range(NSTAT)]
    hT = [A([128, 8, 512]), A([128, 8, 512])]; t_hT = [Tok(), Tok()]
    uTp = A([128, 4, 8, 256]); t_uTp = Tok()
    kTq = [A([128, 8, 256]), A([128, 8, 256])]; t_kTq = [Tok(), Tok()]
    ckvn = [A([128, 384]), A([128, 384])]; t_ckv = [Tok(), Tok()]
    ckvnT = [A([128, 3, 128]), A([128, 3, 128])]; t_ckvT = [Tok(), Tok()]
    sqt = [A([128, 8, 96], F32), A([128, 8, 96], F32)]; t_sqt = [Tok(), Tok()]
    Kt = [A([128, 8, 96]), A([128, 8, 96])]; t_Kt = [Tok(), Tok()]
    krg = [A([128, 32], F32) for _ in range(3)]; krot = [A([128, 32], F32) for _ in range(3)]
    kr4 = A([128, 4, 16], F32); t_kr = [Tok() for _ in range(3)]; t_kr4 = Tok()
    qr4 = A([128, 4, 8, 16], F32)
    vt = [A([128, 512]), A([128, 512])]; t_vt = [Tok(), Tok()]
    kg64 = bc(kg[:, 0:64].rearrange("p (o d) -> p o d", o=1), [128, 8, 64])
    qg96 = bc(qg.rearrange("p (o d) -> p o d", o=1), [128, 8, 96])

    def rope32(dst_re, dst_im, x1, x2, cos_t, sin_t, tmp4, toks):
        TT(DVE, tmp4[0], x1, cos_t, ALU.mult, R=toks, W=toks)
        TT(DVE, tmp4[1], x2, sin_t, ALU.mult, R=toks, W=toks)
        TT(DVE, tmp4[2], x2, cos_t, ALU.mult, R=toks, W=toks)
        TT(DVE, tmp4[3], x1, sin_t, ALU.mult, R=toks, W=toks)
        TT(DVE, dst_re, tmp4[0], tmp4[1], ALU.subtract, R=toks, W=toks)
        TT(DVE, dst_im, tmp4[2], tmp4[3], ALU.add, R=toks, W=toks)

    def stA(t, src_rows):
        sl = t % NX
        s2 = t % 2
        ss = t % NSTAT
        P.dma(SP, xt[sl], src_rows, W=[t_xt[sl]])
        ACTV(xs[s2], xt[sl], AF.Square, R=[t_xt[sl]], W=[t_xs[s2], t_st[ss]], accum=st8[:, ss, 0:1])
        rsqrt_mean(st8[:, ss, 1:2], st8[:, ss, 0:1], float(D), 1, R=[t_st[ss]], W=[t_st[ss]])
        TS(DVE, xs[s2], xt[sl], st8[:, ss, 1:2], None, ALU.mult, R=[t_xt[sl], t_st[ss]], W=[t_xs[s2]])

    def stB(t, gbase=0):
        s2 = t % 2
        hs = ((t + gbase) // 4) % 2
        tq = t % 4
        pT = PS(0, 1024, BF16).rearrange("p (k t) -> p k t", k=8)
        for kc in range(8):
            TRN(pT[:, kc, :], xs[s2][:, kc * 128:(kc + 1) * 128], ident, R=[t_xs[s2], t_ident], W=[pst[0]], inc=(kc == 7))
        CP(ACT, hT[hs][:, :, tq * 128:(tq + 1) * 128], pT, R=[pst[0]], W=[t_hT[hs]])

    def proj_fm(ps_ap, pstok, col0, hslot, n=512):
        for kc in range(8):
            MM(ps_ap, w_in_b[:, kc, col0:col0 + 128], hT[hslot][:, kc, 0:n], kc == 0, kc == 7,
               R=[t_win, t_hT[hslot]], W=[pstok], inc=(kc == 7))

    def proj_tm(ps_ap, pstok, col0, ncol, hslot, tq):
        for kc in range(8):
            MM(ps_ap, hT[hslot][:, kc, tq * 128:(tq + 1) * 128], w_in_b[:, kc, col0:col0 + ncol], kc == 0, False,
               R=[t_win, t_hT[hslot]], W=[pstok], inc=False)
        MM(ps_ap, ones_bf[0:1, 0:128], biasrow[0:1, col0:col0 + ncol], False, True, R=[t_brow, t_ones], W=[pstok])

    def run_pipeline(ntiles, stages, after_step):
        nst = len(stages)
        for step in range(ntiles + nst - 1):
            if 0 <= step < ntiles:
                stages[0](step)
            for si in range(nst - 1, 0, -1):
                t = step - si
                if 0 <= t < ntiles:
                    stages[si](t)
            after_step(step)

    def p1A(t):
        stA(t, xb[t * 128:(t + 1) * 128, :])

    def p1B(t):
        stB(t)

    def p1C(t):
        hs = (t // 4) % 2
        tq = t % 4
        s2 = t % 2
        s3 = t % 3
        ss = t % NSTAT
        pkv = PS(3, 288)
        proj_tm(pkv, pst[3], 1408, 288, hs, tq)
        ACTV(junk[:, 0:256], pkv[:, 0:256], AF.Square, R=[pst[3]], W=[t_st[ss]], accum=st8[:, ss, 2:3])
        rsqrt_mean(st8[:, ss, 3:4], st8[:, ss, 2:3], 256.0, 1, R=[t_st[ss]], W=[t_st[ss]])
        TS(DVE, ckvn[s2][:, 0:256], pkv[:, 0:256], st8[:, ss, 3:4], None, ALU.mult, R=[pst[3], t_st[ss]], W=[t_ckv[s2]])
        ACTV(junk[:, 256:288], pkv[:, 256:288], AF.Square, R=[pst[3]], W=[t_st[ss]], accum=st8[:, ss, 4:5])
        TT(DVE, krg[s3], pkv[:, 256:288], kg[:, 64:96], ALU.mult, R=[pst[3], t_g], W=[t_kr[s3]])
        rope32(krot[s3][:, 0:16], krot[s3][:, 16:32], krg[s3][:, 0:16], krg[s3][:, 16:32], cosb[:, t, :], sinb[:, t, :],
               [kr4[:, i, :] for i in range(4)], [t_kr[s3], t_ropeb, t_kr4])

    def p1D(t):
        s2 = t % 2
        pT2 = PS(4, 256, BF16).rearrange("p (k t) -> p k t", k=2)
        for k2 in range(2):
            TRN(pT2[:, k2, :], ckvn[s2][:, k2 * 128:(k2 + 1) * 128], ident, R=[t_ckv[s2], t_ident], W=[pst[4]], inc=(k2 == 1))
        CP(ACT, ckvnT[s2][:, 0:2, :], pT2, R=[pst[4]], W=[t_ckvT[s2]])

    def p1E(t):
        s2 = t % 2
        s3 = t % 3
        ss = t % NSTAT
        pkvf = PS(5, 1024, nb=2)
        for hf in range(2):
            for k2 in range(2):
                MM(pkvf[:, hf * 512:(hf + 1) * 512], ckvnT[s2][:, k2, :], w_kvb_b[:, k2, hf * 512:(hf + 1) * 512],
                   k2 == 0, k2 == 1, R=[t_ckvT[s2], t_wsm], W=[pst[5], pst[6]], inc=(hf == 1 and k2 == 1))
        kv3 = pkvf.rearrange("p (h d) -> p h d", h=8)
        ACTV(sqt[s2][:, :, 0:64], kv3[:, :, 0:64], AF.Square, R=[pst[5], pst[6]], W=[t_sqt[s2]])
        ssh = st8[:, ss, 8:16]
        RED(DVE, ssh, sqt[s2][:, :, 0:64], ALU.add, R=[t_sqt[s2]], W=[t_st[ss]])
        TS(POOL, ssh, ssh, st8[:, ss, 4:5], None, ALU.add, R=[t_st[ss]], W=[t_st[ss]])
        rk = st8[:, ss, 16:24]
        rsqrt_mean(rk, ssh, 96.0, 8, R=[t_st[ss]], W=[t_st[ss]])
        rk64 = bc(rk.rearrange("p (h o) -> p h o", o=1), [128, 8, 64])
        TT(DVE, sqt[s2][:, :, 0:64], kv3[:, :, 0:64], rk64, ALU.mult, R=[pst[5], pst[6], t_st[ss]], W=[t_sqt[s2]])
        TT(DVE, Kt[s2][:, :, 0:64], sqt[s2][:, :, 0:64], kg64, ALU.mult, R=[t_sqt[s2], t_g], W=[t_Kt[s2]])
        TT(DVE, Kt[s2][:, :, 64:96], bc(krot[s3].rearrange("p (o d) -> p o d", o=1), [128, 8, 32]),
           bc(rk.rearrange("p (h o) -> p h o", o=1), [128, 8, 32]), ALU.mult, R=[t_kr[s3], t_st[ss]], W=[t_Kt[s2]])
        CP(ACT, vt[s2].rearrange("p (h d) -> p h d", h=8), kv3[:, :, 64:128], R=[pst[5], pst[6]], W=[t_vt[s2]])
        P.dma(ACT, v_s[t * 128:(t + 1) * 128, :], vt[s2], R=[t_vt[s2]])

    def p1F(t):
        s2 = t % 2
        ks = (t // 2) % 2
        pKT = PS(7, 1024, BF16, parts=96).rearrange("p (h t) -> p h t", h=8)
        for h in range(8):
            TRN(pKT[:, h, :], Kt[s2][:, h, :], ident, R=[t_Kt[s2], t_ident], W=[pst[7]], inc=(h == 7))
        CP(ACT, kTq[ks][0:96, :, (t % 2) * 128:(t % 2 + 1) * 128], pKT, R=[pst[7]], W=[t_kTq[ks]])
        if t % 2 == 1:
            P.dma(ACT, kT_s.rearrange("h d t -> d h t")[:, :, (t - 1) * 128:(t + 1) * 128], kTq[ks][0:96], R=[t_kTq[ks]])

    def p1after(step):
        u_ = step - 4
        if u_ < 0:
            return
        qd = u_ // 4
        cc = u_ % 4
        if qd >= NT // 4:
            return
        hs = qd % 2
        b_ = 1 + cc % 2
        pu = PS(b_)
        proj_fm(pu, pst[b_], cc * 128, hs)
        TS(DVE, uTp[:, cc, :, (qd % 4) * 64:(qd % 4) * 64 + 64], pu.rearrange("p (c s) -> p s c", s=8),
           biasT[:, cc:cc + 1], None, ALU.add, R=[pst[b_], t_brow], W=[t_uTp])
        if qd % 4 == 3 and cc == 3:
            sq_ = qd // 4
            for c2 in range(4):
                P.dma(POOL, uT_s[c2 * 128:(c2 + 1) * 128, :, sq_ * 256:(sq_ + 1) * 256], uTp[:, c2], R=[t_uTp])

    wg_ops = []
    real_emit, real_dma = P.emit, P.dma

    def rec_emit(E, fn, R=(), W=()):
        wg_ops.append(("e", E, fn, tuple(R), tuple(W)))

    def rec_dma(Q, out, in_, R=(), W=()):
        wg_ops.append(("d", Q, out, in_, tuple(R), tuple(W)))

    saved_off = A.off
    A.off = KB(78)
    A.hole = (KB(94), KB(110))
    P.emit, P.dma = rec_emit, rec_dma
    t_sg = Tok()
    lp = A([128, 2, 32], F32); dtp = A([128, 32], F32)
    P.dma(SP, lp, lamP, W=[t_sg])
    P.dma(SP, dtp, ldt_rep, W=[t_sg])
    ACTV(dtp, dtp, AF.Exp, R=[t_sg], W=[t_sg])
    mag = A([128, 32], F32); ang = A([128, 32], F32); sn = A([128, 32], F32); cs = A([128, 32], F32)
    tmpa = A([128, 32], F32); tmpai = AI([128, 32])
    TT(DVE, mag, lp[:, 0, :], dtp, ALU.mult, R=[t_sg], W=[t_sg])
    ACTV(mag, mag, AF.Exp, R=[t_sg], W=[t_sg])
    TT(DVE, ang, lp[:, 1, :], dtp, ALU.mult, R=[t_sg], W=[t_sg])
    sincos(ang, sn, cs, tmpa, tmpai, [t_sg])
    PW = A([128, 2, 32, 13], F32)
    RVt = A([128, 2, 32, 15], F32)
    pr_ = sn; pi_ = cs
    TT(DVE, PW[:, 0, :, 0], mag, cs, ALU.mult, R=[t_sg], W=[t_sg])
    TT(DVE, PW[:, 1, :, 0], mag, sn, ALU.mult, R=[t_sg], W=[t_sg])
    for k in range(12):
        TT(DVE, tmpa, PW[:, 0, :, k], PW[:, 0, :, k], ALU.mult, R=[t_sg], W=[t_sg])
        TT(DVE, ang, PW[:, 1, :, k], PW[:, 1, :, k], ALU.mult, R=[t_sg], W=[t_sg])
        TT(DVE, PW[:, 0, :, k + 1], tmpa, ang, ALU.subtract, R=[t_sg], W=[t_sg])
        TT(DVE, tmpa, PW[:, 0, :, k], PW[:, 1, :, k], ALU.mult, R=[t_sg], W=[t_sg])
        TS(DVE, PW[:, 1, :, k + 1], tmpa, 2.0, None, ALU.mult, R=[t_sg], W=[t_sg])
    RV_SPEC = [(3, None), (4, None), (3, 4), (5, None), (6, None), (5, 6),
               (7, None), (8, None), (7, 8), (9, None), (10, None), (9, 10), (11, None), (12, None), (11, 12)]
    for k, (ka, kb_) in enumerate(RV_SPEC):
        if kb_ is None:
            sre, sim = PW[:, 0, :, ka], PW[:, 1, :, ka]
        else:
            TT(DVE, tmpa, PW[:, 0, :, ka], PW[:, 0, :, kb_], ALU.mult, R=[t_sg], W=[t_sg])
            TT(DVE, ang, PW[:, 1, :, ka], PW[:, 1, :, kb_], ALU.mult, R=[t_sg], W=[t_sg])
            TT(DVE, pr_, tmpa, ang, ALU.subtract, R=[t_sg], W=[t_sg])
            TT(DVE, tmpa, PW[:, 0, :, ka], PW[:, 1, :, kb_], ALU.mult, R=[t_sg], W=[t_sg])
            TT(DVE, ang, PW[:, 1, :, ka], PW[:, 0, :, kb_], ALU.mult, R=[t_sg], W=[t_sg])
            TT(DVE, pi_, tmpa, ang, ALU.add, R=[t_sg], W=[t_sg])
            sre, sim = pr_, pi_
        CP(DVE, RVt[0:64, 0, :, k], sre[0:64], R=[t_sg], W=[t_sg])
        TS(DVE, RVt[64:128, 0, :, k], sim[64:128], -1.0, None, ALU.mult, R=[t_sg], W=[t_sg])
        CP(DVE, RVt[0:64, 1, :, k], sim[0:64], R=[t_sg], W=[t_sg])
        CP(DVE, RVt[64:128, 1, :, k], sre[64:128], R=[t_sg], W=[t_sg])
    P.dma(POOL, rv_s, RVt, R=[t_sg])
    PT = A([128, 2, 32, 9], F32)
    MEMSET(DVE, PT[:, 0, :, 0], 1.0, W=[t_sg])
    MEMSET(DVE, PT[:, 1, :, 0], 0.0, W=[t_sg])

    def cmul(E, ore, oim, are, aim, bre, bim, t1, t2, toks):
        TT(E, t1, are, bre, ALU.mult, R=toks, W=toks)
        TT(E, t2, aim, bim, ALU.mult, R=toks, W=toks)
        TT(E, ore, t1, t2, ALU.subtract, R=toks, W=toks)
        TT(E, t1, are, bim, ALU.mult, R=toks, W=toks)
        TT(E, t2, aim, bre, ALU.mult, R=toks, W=toks)
        TT(E, oim, t1, t2, ALU.add, R=toks, W=toks)

    for t in range(8):
        cmul(DVE, PT[:, 0, :, t + 1], PT[:, 1, :, t + 1], PT[:, 0, :, t], PT[:, 1, :, t], PW[:, 0, :, 0], PW[:, 1, :, 0],
             tmpa, ang, [t_sg])
    PI_ = A([128, 2, 32, 8], F32)
    xa_raw = A([128, 512], F32); xb_raw = A([128, 512], F32)
    n2 = xa_raw[:, 0:256].rearrange("p (g s) -> p g s", g=32); n2b = xb_raw[:, 0:256].rearrange("p (g s) -> p g s", g=32)
    TT(DVE, n2, PT[:, 0, :, 0:8], PT[:, 0, :, 0:8], ALU.mult, R=[t_sg], W=[t_sg])
    TT(DVE, n2b, PT[:, 1, :, 0:8], PT[:, 1, :, 0:8], ALU.mult, R=[t_sg], W=[t_sg])
    TT(DVE, n2, n2, n2b, ALU.add, R=[t_sg], W=[t_sg])
    P.emit(DVE, lambda e: e.reciprocal(out=n2, in_=n2), [t_sg], [t_sg])
    TT(DVE, PI_[:, 0, :, :], PT[:, 0, :, 0:8], n2, ALU.mult, R=[t_sg], W=[t_sg])
    TT(DVE, n2b, PT[:, 1, :, 0:8], n2, ALU.mult, R=[t_sg], W=[t_sg])
    TS(DVE, PI_[:, 1, :, :], n2b, -1.0, None, ALU.mult, R=[t_sg], W=[t_sg])
    PTr = A([128, 2, 32, 8], F32)
    for s_ in range(8):
        CP(DVE, PTr[:, :, :, s_], PT[:, :, :, 7 - s_], R=[t_sg], W=[t_sg])
    cf = A([128, 2, 32], F32); l2 = A([128, 32], F32); l2b = A([128, 32], F32)
    TT(DVE, l2, lp[:, 0, :], lp[:, 0, :], ALU.mult, R=[t_sg], W=[t_sg])
    TT(DVE, l2b, lp[:, 1, :], lp[:, 1, :], ALU.mult, R=[t_sg], W=[t_sg])
    TT(DVE, l2, l2, l2b, ALU.add, R=[t_sg], W=[t_sg])
    P.emit(DVE, lambda e: e.reciprocal(out=l2, in_=l2), [t_sg], [t_sg])
    lb1 = A([128, 32], F32)
    TS(DVE, lb1, PW[:, 0, :, 0], -1.0, None, ALU.add, R=[t_sg], W=[t_sg])
    TT(DVE, tmpa, lb1, lp[:, 0, :], ALU.mult, R=[t_sg], W=[t_sg])
    TT(DVE, ang, PW[:, 1, :, 0], lp[:, 1, :], ALU.mult, R=[t_sg], W=[t_sg])
    TT(DVE, tmpa, tmpa, ang, ALU.add, R=[t_sg], W=[t_sg])
    TT(DVE, cf[:, 0, :], tmpa, l2, ALU.mult, R=[t_sg], W=[t_sg])
    TT(DVE, tmpa, PW[:, 1, :, 0], lp[:, 0, :], ALU.mult, R=[t_sg], W=[t_sg])
    TT(DVE, ang, lb1, lp[:, 1, :], ALU.mult, R=[t_sg], W=[t_sg])
    TT(DVE, tmpa, tmpa, ang, ALU.subtract, R=[t_sg], W=[t_sg])
    TT(DVE, cf[:, 1, :], tmpa, l2, ALU.mult, R=[t_sg], W=[t_sg])
    GB = 4
    bp2 = [A([128, 2, GB, 16], F32), A([128, 2, GB, 16], F32)]; BB = A([128, 2, GB, 16], F32)
    cp2 = [A([128, 2, GB, 16], F32), A([128, 2, GB, 16], F32)]
    t16a = A([128, GB, 16], F32); t16b = A([128, GB, 16], F32)
    X = A([128, GB, 8, 16], F32)
    xa = xa_raw.rearrange("p (g s h) -> p g s h", g=GB, s=8); xb_ = xb_raw.rearrange("p (g s h) -> p g s h", g=GB, s=8)
    YY = A([128, GB, 9, 16], F32); A7 = A([128, GB, 8, 16], F32)
    xa9 = A([128, GB, 9, 16], F32); xb9 = A([128, GB, 9, 16], F32)
    YmS = A([128, GB, 128])

    t_bpin = [Tok(), Tok()]

    def b4(ap3, axis):
        shp = [ap3.shape[0], GB, 8, 16]
        if axis == 3:
            return bc(ap3.rearrange("p g (s o) -> p g s o", o=1), shp)
        return bc(ap3.rearrange("p g (o h) -> p g o h", o=1), shp)

    def cstack(dst, pw_re, pw_im, c_re_, c_im_, neg_im, xtok=None, n=8):
        RR = [t_sg] + ([xtok] if xtok is not None else [])
        ta, tb = (xa, xb_) if n == 8 else (xa9, xb9)

        def bb(ap3, axis, lo, hi):
            shp = [hi - lo, GB, n, 16]
            if axis == 3:
                return bc(ap3[lo:hi].rearrange("p g (s o) -> p g s o", o=1), shp)
            return bc(ap3[lo:hi].rearrange("p g (o h) -> p g o h", o=1), shp)
        TT(DVE, ta[0:64], bb(pw_re, 3, 0, 64), bb(c_re_, 2, 0, 64), ALU.mult, R=RR, W=[t_sg])
        TT(DVE, tb[0:64], bb(pw_im, 3, 0, 64), bb(c_im_, 2, 0, 64), ALU.mult, R=RR, W=[t_sg])
        TT(DVE, dst[0:64], ta[0:64], tb[0:64], ALU.subtract, R=RR, W=[t_sg])
        TT(DVE, ta[64:128], bb(pw_re, 3, 64, 128), bb(c_im_, 2, 64, 128), ALU.mult, R=RR, W=[t_sg])
        TT(DVE, tb[64:128], bb(pw_im, 3, 64, 128), bb(c_re_, 2, 64, 128), ALU.mult, R=RR, W=[t_sg])
        TT(DVE, dst[64:128], ta[64:128], tb[64:128], ALU.add, R=RR, W=[t_sg])
        if neg_im:
            TS(DVE, dst[64:128], dst[64:128], -1.0, None, ALU.mult, R=RR, W=[t_sg])

    for gb in range(32 // GB):
        g0 = gb * GB
        bp = bp2[gb % 2]
        cp_ = cp2[gb % 2]
        P.dma(POOL, bp, bP[:, :, g0:g0 + GB, :], W=[t_bpin[gb % 2]])
        P.dma(POOL, cp_, cP[:, :, g0:g0 + GB, :], W=[t_bpin[gb % 2]])
        cfre = bc(cf[:, 0, g0:g0 + GB].rearrange("p (g o) -> p g o", o=1), [128, GB, 16])
        cfim = bc(cf[:, 1, g0:g0 + GB].rearrange("p (g o) -> p g o", o=1), [128, GB, 16])
        cmul(DVE, BB[:, 0], BB[:, 1], cfre, cfim, bp[:, 0], bp[:, 1], t16a, t16b, [t_sg, t_bpin[gb % 2]])
        cstack(X, PI_[:, 0, g0:g0 + GB, :], PI_[:, 1, g0:g0 + GB, :], BB[:, 0], BB[:, 1], False)
        cstack(A7, PTr[:, 0, g0:g0 + GB, :], PTr[:, 1, g0:g0 + GB, :], BB[:, 0], BB[:, 1], False)
        cstack(YY, PT[:, 0, g0:g0 + GB, :], PT[:, 1, g0:g0 + GB, :], cp_[:, 0], cp_[:, 1], True, t_bpin[gb % 2], n=9)
        CP(DVE, YmS, YY[:, :, 1:9, :].rearrange("p g t h -> p g (t h)"), R=[t_sg], W=[t_sg])
        P.dma(POOL, ymat_s[:, g0:g0 + GB, :], YmS, R=[t_sg])
        P.dma(POOL, x_s[:, g0:g0 + GB, :], X.rearrange("p g s h -> p g (s h)"), R=[t_sg])
        P.dma(POOL, a_s[:, g0:g0 + GB, :], A7.rearrange("p g s h -> p g (s h)"), R=[t_sg])
        P.dma(POOL, yk_s[:, g0:g0 + GB, :], YY[:, :, 0:8, :].rearrange("p g t h -> p g (t h)"), R=[t_sg])
    ops_a = wg_ops
    ops_b = []
    P.emit, P.dma = real_emit, real_dma
    assert A.off <= KB(131), A.off
    A.hole = None
    A.off = saved_off
    merged = []
    ia = ib = 0
    while ia < len(ops_a) or ib < len(ops_b):
        if ia < len(ops_a):
            merged.append(ops_a[ia]); ia += 1
        if ib < len(ops_b):
            merged.append(ops_b[ib]); ib += 1
    wg_pos = [0]
    wg_per_step = (len(merged) + 39) // 40

    def wg_pump(n):
        for _ in range(n):
            if wg_pos[0] >= len(merged):
                return
            op = merged[wg_pos[0]]
            wg_pos[0] += 1
            if op[0] == "e":
                P.emit(op[1], op[2], op[3], op[4])
            else:
                P.dma(op[1], op[2], op[3], op[4], op[5])


    Qt = Kt
    t_Qt = t_Kt
    uTo = uTp
    xo_t = xo.rearrange("(t p) d -> t p d", p=128)

    def p2A(t):
        stA(t, xo_t[t - NT])

    def p2B(t):
        stB(t)

    def p2C(t):
        hs = (t // 4) % 2
        tq = t % 4
        s2 = t % 2
        ss = t % NSTAT
        pq = PS(3, 384)
        proj_tm(pq, pst[3], 1024, 384, hs, tq)
        ACTV(junk[:, 0:384], pq, AF.Square, R=[pst[3]], W=[t_st[ss]], accum=st8[:, ss, 2:3])
        rsqrt_mean(st8[:, ss, 3:4], st8[:, ss, 2:3], 384.0, 1, R=[t_st[ss]], W=[t_st[ss]])
        TS(DVE, ckvn[s2], pq, st8[:, ss, 3:4], None, ALU.mult, R=[pst[3], t_st[ss]], W=[t_ckv[s2]])

    def p2D(t):
        s2 = t % 2
        pT3 = PS(4, 384, BF16).rearrange("p (k t) -> p k t", k=3)
        for k3 in range(3):
            TRN(pT3[:, k3, :], ckvn[s2][:, k3 * 128:(k3 + 1) * 128], ident, R=[t_ckv[s2], t_ident], W=[pst[4]], inc=(k3 == 2))
        CP(ACT, ckvnT[s2], pT3, R=[pst[4]], W=[t_ckvT[s2]])

    def p2E(t):
        s2 = t % 2
        ss = t % NSTAT
        pqf = PS(5, 768, nb=2)
        for (n0, n1) in ((0, 512), (512, 768)):
            for k3 in range(3):
                MM(pqf[:, n0:n1], ckvnT[s2][:, k3, :], w_qb_b[:, k3, n0:n1], k3 == 0, k3 == 2,
                   R=[t_ckvT[s2], t_wsm], W=[pst[5], pst[6]], inc=(n0 == 512 and k3 == 2))
        q3 = pqf.rearrange("p (h d) -> p h d", h=8)
        ACTV(sqt[s2], q3, AF.Square, R=[pst[5], pst[6]], W=[t_sqt[s2]])
        ssh = st8[:, ss, 8:16]
        RED(DVE, ssh, sqt[s2], ALU.add, R=[t_sqt[s2]], W=[t_st[ss]])
        rk = st8[:, ss, 16:24]
        rsqrt_mean(rk, ssh, 96.0, 8, R=[t_st[ss]], W=[t_st[ss]])
        TT(DVE, sqt[s2], q3, bc(rk.rearrange("p (h o) -> p h o", o=1), [128, 8, 96]), ALU.mult,
           R=[pst[5], pst[6], t_st[ss]], W=[t_sqt[s2]])
        TT(DVE, sqt[s2], sqt[s2], qg96, ALU.mult, R=[t_sqt[s2], t_g], W=[t_sqt[s2]])
        CP(DVE, Qt[s2][:, :, 0:64], sqt[s2][:, :, 0:64], R=[t_sqt[s2]], W=[t_Qt[s2]])
        cos8 = bc(coso[:, t - NT, :].rearrange("p (o d) -> p o d", o=1), [128, 8, 16])
        sin8 = bc(sino[:, t - NT, :].rearrange("p (o d) -> p o d", o=1), [128, 8, 16])
        rope32(Qt[s2][:, :, 64:80], Qt[s2][:, :, 80:96], sqt[s2][:, :, 64:80], sqt[s2][:, :, 80:96], cos8, sin8,
               [qr4[:, i] for i in range(4)], [t_sqt[s2], t_Qt[s2], t_ropeo, t_kr4])

    def p2F(t):
        own_region_sync()
        s2 = t % 2
        pQT = PS(7, 1024, BF16, parts=96).rearrange("p (h t) -> p h t", h=8)
        for h in range(8):
            TRN(pQT[:, h, :], Qt[s2][:, h, :], ident, R=[t_Qt[s2], t_ident], W=[pst[7]], inc=(h == 7))
        CP(ACT, QT[0:96, :, (t - NT) * 128:(t - NT + 1) * 128], pQT, R=[pst[7], t_sg], W=[t_qt])

    def p2after(step):
        u_ = step - 4
        if u_ < 0:
            return
        qg = u_ // 4
        part = u_ % 4
        if qg < NT // 4 or qg >= (NT + NO) // 4:
            return
        own_region_sync()
        qd = qg - NT // 4
        hs = qg % 2
        for cc in range(3 * part, 3 * part + 3):
            b_ = 1 + cc % 2
            pu = PS(b_)
            col0 = [0, 128, 256, 384, 512, 640, 768, 896, 1696, 1824, 1952, 2080][cc]
            proj_fm(pu, pst[b_], col0, hs)
            src = pu.rearrange("p (c s) -> p s c", s=8)
            if cc < 4:
                TS(DVE, uTo[:, cc, :, qd * 64:(qd + 1) * 64], src, biasT[:, cc:cc + 1], None, ALU.add, R=[pst[b_], t_brow], W=[t_uTp])
            elif cc < 8:
                ACTV(zs[:, cc - 4, :, qd * 64:(qd + 1) * 64], src, AF.Silu, R=[pst[b_], t_sg, t_brow], W=[t_z], bias=biasT[:, cc:cc + 1])
            else:
                ACTV(zm[:, cc - 8, :, qd * 64:(qd + 1) * 64], src, AF.Silu, R=[pst[b_], t_sg, t_brow], W=[t_z], bias=biasT[:, cc:cc + 1])

    def mk(f1, f2):
        return lambda t: f1(t) if t < NT else f2(t)

    def both_after(step):
        rope_pump(24)
        p1after(step)
        p2after(step)

    run_pipeline(NT + NO, [mk(p1A, p2A), mk(p1B, p2B), mk(p1C, p2C), mk(p1D, p2D), mk(p1E, p2E), mk(p1F, p2F)], both_after)
    for cc in range(4):
        P.dma(POOL, uTo_s[cc * 128:(cc + 1) * 128, :, :], uTo[:, cc], R=[t_uTp])

    P.barrier()
    A.off = mark1
    kTh = [A([128, L]), A([128, L])]; t_kTh = [Tok(), Tok()]
    Va = [A([128, NT, 128]), A([128, NT, 128])]; t_Va = [Tok(), Tok()]
    PT3 = [A([128, 2, 512]) for _ in range(3)]; t_PT = [Tok() for _ in range(3)]
    rden = A([128, 512], F32); t_rd = Tok()
    scale = 1.0 / math.sqrt(96.0)
    MEMSET(POOL, Va[0][:, :, 64:128], 1.0, W=[t_Va[0]])
    MEMSET(POOL, Va[1][:, :, 0:64], 1.0, W=[t_Va[1]])
    ymv = ymix.rearrange("p a (s c) -> p a s c", s=8)

    def load_head(h):
        hsl = h % 2
        P.dma(SP, kTh[hsl][0:96, :], kT_s[h], W=[t_kTh[hsl]])
        voff = 0 if hsl == 0 else 64
        for part in range(4):
            P.dma(SP, Va[hsl][:, part * 16:(part + 1) * 16, voff:voff + 64],
                  v_s[part * 2048:(part + 1) * 2048, h * 64:(h + 1) * 64].rearrange("(b p) d -> p b d", p=128),
                  W=[t_Va[hsl]])

    items = []
    for h in range(8):
        for gq in range(4):
            nkb = 4 * (4 * gq + 3) + 4
            for kp in range(nkb // 2):
                items.append((h, gq, kp, nkb))

    def geom(it):
        h, gq, kp, nkb = it
        m0 = gq * 4
        kb0 = 2 * kp
        mk = kb0 // 4
        first = max(mk, m0) - m0
        return h, gq, m0, kb0, mk, first * 128, nkb

    def emit_qk(idx):
        h, gq, m0, kb0, mk, c0, nkb = geom(items[idx])
        hsl = h % 2
        sb_ = idx % 3
        b0 = sb_ * 2
        for u in range(2):
            MM(PS(b0 + u)[:, c0:512], kTh[hsl][0:96, (kb0 + u) * 128:(kb0 + u + 1) * 128],
               QT[0:96, h, m0 * 128 + c0:(m0 + 4) * 128], True, True, R=[t_kTh[hsl], t_qt], W=[pst[b0 + u]], inc=(u == 1))
        S2 = psum[:, b0 * 512:(b0 + 2) * 512].rearrange("p (u n) -> p u n", u=2)
        ACTV(PT3[sb_][:, :, c0:512], S2[:, :, c0:512], AF.Exp, R=[pst[b0], pst[b0 + 1], t_nb], W=[t_PT[sb_]],
             bias=nbias[:, 0:1], scale=scale)
        if mk >= m0:
            for u in range(2):
                TT(DVE, PT3[sb_][:, u, c0:c0 + 128], PT3[sb_][:, u, c0:c0 + 128], md[:, (kb0 + u) % 4, :], ALU.mult,
                   R=[t_PT[sb_], t_md], W=[t_PT[sb_]])

    def emit_pv(idx):
        h, gq, m0, kb0, mk, c0, nkb = geom(items[idx])
        hsl = h % 2
        sb_ = idx % 3
        ob = 6 + (h * 4 + gq) % 2
        po = PS(ob)
        if kb0 == 0 and gq == 0 and h + 1 < 8:
            load_head(h + 1)
        for u in range(2):
            kb = kb0 + u
            MM(po[:, c0:512], Va[hsl][:, kb, :], PT3[sb_][:, u, c0:512], kb == 0, kb == nkb - 1,
               R=[t_Va[hsl], t_PT[sb_]], W=[pst[ob]], inc=(u == 1))
        if kb0 + 2 == nkb:
            hp = h // 2
            if hsl == 0:
                nr, dr = slice(0, 64), slice(64, 128)
            else:
                nr, dr = slice(64, 128), slice(0, 64)
            P.emit(DVE, lambda e, o=rden[dr, :], i=po[dr, :]: e.reciprocal(out=o, in_=i), [pst[ob]], [t_rd])
            TT(DVE, rden[nr, :], po[nr, :], rden[dr, :], ALU.mult, R=[pst[ob], t_rd], W=[t_rd])
            dst = ymv[nr, 4 + hp, :, m0 * 16:m0 * 16 + 64]
            TT(DVE, dst, rden[nr, :].rearrange("p (c s) -> p s c", s=8), zm[nr, hp, :, m0 * 16:m0 * 16 + 64], ALU.mult,
               R=[t_rd, t_z], W=[t_ymix[4 + hp]])

    load_head(0)
    LOOK = 2
    for idx in range(min(LOOK, len(items))):
        emit_qk(idx)
    for idx in range(len(items)):
        if idx + LOOK < len(items):
            emit_qk(idx + LOOK)
        emit_pv(idx)
        tgt = (len(merged) * (idx + 1)) // max(1, len(items) - 8)
        if tgt > wg_pos[0]:
            wg_pump(tgt - wg_pos[0])
    wg_pump(len(merged))

    P.barrier()
    A.off = KB(141)
    RV = A([128, 2, 32, 15], F32)
    mark3 = A.off
    t_w1 = Tok()
    t_sg2 = Tok()
    P.dma(SP, Ymat, ymat_s, W=[t_ssmw])
    t_sg = Tok()
    tri = A([128, 128], F32)
    rci = AI([128, 1]); rcf = A([128, 1], F32); colf = A([128, 128], F32); coli = AI([128, 128])
    P.emit(POOL, lambda e: e.iota(rci, pattern=[[0, 1]], base=0, channel_multiplier=1), (), [t_sg])
    P.emit(DVE, lambda e: e.tensor_scalar(out=rci, in0=rci, scalar1=4, scalar2=4, op0=ALU.arith_shift_right,
                                          op1=ALU.logical_shift_left), [t_sg], [t_sg])
    CP(DVE, rcf, rci, R=[t_sg], W=[t_sg])
    P.emit(POOL, lambda e: e.iota(coli, pattern=[[1, 128]], base=0, channel_multiplier=0), (), [t_sg])
    CP(DVE, colf, coli, R=[t_sg], W=[t_sg])
    TS(DVE, tri, colf, rcf[:, 0:1], None, ALU.subtract, R=[t_sg], W=[t_sg])
    TS(DVE, tri, tri, -0.5, None, ALU.is_gt, R=[t_sg], W=[t_sg])
    dsd = A([128, 32], F32)
    P.dma(SP, dsd, dS, W=[t_sg])
    Xf = [A([128, 8, 128], F32), A([128, 8, 128], F32)]; Ykf = [A([128, 8, 128], F32), A([128, 8, 128], F32)]
    t_xf = [Tok(), Tok()]
    Xb3 = [A([128, 128], F32), A([128, 128], F32)]; t_xb = [Tok(), Tok()]
    t_k0 = [Tok() for _ in range(32)]; t_w1g = [Tok() for _ in range(32)]
    Af = [A([128, 8, 128], F32), A([128, 8, 128], F32)]
    for ob in range(4):
        sl = ob % 2
        P.dma(SP, Xf[sl], x_s[:, ob * 8:(ob + 1) * 8, :], W=[t_xf[sl]])
        P.dma(SP, Af[sl], a_s[:, ob * 8:(ob + 1) * 8, :], W=[t_xf[sl]])
        P.dma(SP, Ykf[sl], yk_s[:, ob * 8:(ob + 1) * 8, :], W=[t_xf[sl]])
        for gl in range(8):
            g = ob * 8 + gl
            pk = PS(g % 4, 128)
            MM(pk, Xf[sl][:, gl, :], Ykf[sl][:, gl, :], True, True, R=[t_xf[sl]], W=[pst[g % 4]])
            TT(DVE, Xb3[g % 2], pk, tri, ALU.mult, R=[pst[g % 4], t_sg], W=[t_xb[g % 2]])
            STT(DVE, K0[:, g, :], identf, dsd[:, g:g + 1], Xb3[g % 2], ALU.mult, ALU.add, R=[t_sg, t_ident, t_xb[g % 2]], W=[t_k0[g]])
            pw_ = PS(4 + g % 4, 128)
            TRN(pw_, Af[sl][:, gl, :], identf, R=[t_xf[sl], t_ident], W=[pst[4 + g % 4]])
            CP(ACT, W1[:, g, :], pw_, R=[pst[4 + g % 4]], W=[t_w1g[g]])
    P.dma(SP, RV, rv_s, W=[t_rv])
    wg32 = A([128, 4, 512], F32)
    P.dma(SP, wg32, w_glu.rearrange("(k p) n -> p k n", p=128), W=[t_sg2])
    CP(DVE, w_glu_b, wg32, R=[t_sg2], W=[t_wsm])
    P.barrier()
    A.off = mark3
    NW = 4
    Uo = [A([128, NW, 1024]), A([128, NW, 1024])]; t_Uo = [Tok(), Tok()]
    Uown = [A([128, NW, 256]), A([128, NW, 256])]; t_Uown = [Tok(), Tok()]
    Rg = [A([128, 15, 128]) for _ in range(NW)]; t_Rg = [Tok() for _ in range(NW)]
    XaA = [A([128, 1024]) for _ in range(NW)]; XaB = [A([128, 256]) for _ in range(NW)]
    t_XaA = [Tok() for _ in range(NW)]; t_XaB = [Tok() for _ in range(NW)]
    Sx = [A([128, 80]) for _ in range(NW)]; t_Sx = [Tok() for _ in range(NW)]
    S2o = [A([128, 16], F32) for _ in range(NW)]; t_S2o = [Tok() for _ in range(NW)]
    XoA = [A([128, 256]) for _ in range(NW)]; XoB = [A([128, 256]) for _ in range(NW)]
    t_XoA = [Tok() for _ in range(NW)]; t_XoB = [Tok() for _ in range(NW)]
    Yall = A([128, 8, 256]); t_Yall = Tok()
    t_ys = [Tok() for _ in range(4)]
    t_yg = Tok()
    _save = A.off
    A.off = KB(14)
    ygT = A([128, 4, 8, 256])
    w_out_b = A([128, 8, D]); t_wout = Tok()
    wo32 = [A([128, D], F32), A([128, D], F32)]; t_wo32 = [Tok(), Tok()]
    xr = [A([128, D], F32), A([128, D], F32)]; t_xr = [Tok(), Tok()]
    assert A.off <= KB(62), A.off
    A.off = _save
    ot = wo32; t_ot = t_wo32
    gate_sb = A([128, D], F32); t_gsb = Tok()
    gate_bc = [PS(6), PS(7)]
    for hf in range(2):
        MM(gate_bc[hf], ones_f[0:1, 0:128], gate_row[0:1, hf * 512:(hf + 1) * 512], True, True,
           R=[t_ones, t_grow], W=[pst[6 + hf]])
        CP(DVE, gate_sb[:, hf * 512:(hf + 1) * 512], gate_bc[hf], R=[pst[6 + hf]], W=[t_gsb])

    def wout_chunk(kc):
        sl = kc % 2
        P.dma(SP, wo32[sl], w_out[kc * 128:(kc + 1) * 128, :], W=[t_wo32[sl]])
        TT(DVE, w_out_b[:, kc, :], wo32[sl], gate_sb, ALU.mult, R=[t_wo32[sl], t_gsb], W=[t_wout])
    ident2 = A([128, 128]); t_id2 = Tok()
    CP(DVE, ident2, ident, R=[t_ident], W=[t_id2])
    TT(DVE, ident2[0:64, 64:128], ident2[0:64, 64:128], ident[0:64, 0:64], ALU.add, R=[t_ident, t_id2], W=[t_id2])
    TT(DVE, ident2[64:128, 0:64], ident2[64:128, 0:64], ident[64:128, 64:128], ALU.add, R=[t_ident, t_id2], W=[t_id2])
    for w in range(NW):
        MEMSET(DVE, Sx[w], 0.0, W=[t_Sx[w]])

    def evac(i, out, in_, R, W):
        CP(ACT, out, in_, R=R, W=W)

    pending_reload = []
    t_ygl = [[Tok() for _ in range(8)] for _ in range(4)]

    def do_reload(oc_):
        for gl_ in range(8):
            P.dma(SP, ygT[gl_ * 16:(gl_ + 1) * 16, oc_, :, :], ys_s[oc_ * 8 + gl_].rearrange("t h c -> h t c"),
                  R=[t_ys[oc_]], W=[t_ygl[oc_][gl_]])

    for wave in range(32 // NW):
        wsl = wave % 2
        g0 = wave * NW
        ch0 = g0 * 16
        for s in range(8):
            P.dma(SP, Uo[wsl][s * 16:(s + 1) * 16, :, :],
                  uT_s[ch0:ch0 + NW * 16, s, :].rearrange("(g h) c -> h g c", h=16), W=[t_Uo[wsl]])
            P.dma(SP, Uown[wsl][s * 16:(s + 1) * 16, :, :],
                  uTo_s[ch0:ch0 + NW * 16, s, :].rearrange("(g h) c -> h g c", h=16), W=[t_Uown[wsl]])
        wout_chunk(wave)
        while pending_reload and pending_reload[0] * 2 + 1 < wave:
            do_reload(pending_reload.pop(0))
        for w in range(NW):
            g = g0 + w
            for hf in range(2):
                TT(DVE if hf == 0 else POOL, Rg[w][:, :, hf * 64:(hf + 1) * 64],
                   bc(ident2[:, hf * 64:(hf + 1) * 64].rearrange("p (o q) -> p o q", o=1), [128, 15, 64]),
                   bc(RV[:, hf, g, :].rearrange("p (k o) -> p k o", o=1), [128, 15, 64]), ALU.mult,
                   R=[t_id2, t_rv], W=[t_Rg[w]])
            pz = PS(2 * w, 1024, nb=2)
            for hf in range(2):
                MM(pz[:, hf * 512:(hf + 1) * 512], W1[:, g, :], Uo[wsl][:, w, hf * 512:(hf + 1) * 512], True, True,
                   R=[t_w1g[g], t_Uo[wsl]], W=[pst[2 * w], pst[2 * w + 1]])
            evac(w, XaA[w], pz, [pst[2 * w], pst[2 * w + 1]], [t_XaA[w]])
        for lv, (n, idxs) in enumerate(((256, (2, 1, 0)), (64, (5, 4, 3)))):
            for w in range(NW):
                bk = 2 * w + lv % 2
                src, t_src = (XaA[w], t_XaA[w]) if lv == 0 else (XaB[w], t_XaB[w])
                dst, t_dst = (XaB[w], t_XaB[w]) if lv == 0 else (XaA[w], t_XaA[w])
                pzz = PS(bk, n)
                ev = src[:, 0:4 * n].rearrange("p (c four) -> p c four", four=4)
                for j in range(3):
                    MM(pzz, Rg[w][:, idxs[j], :], ev[:, :, j], j == 0, False, R=[t_Rg[w], t_src], W=[pst[bk]])
                MM(pzz, ident, ev[:, :, 3], False, True, R=[t_ident, t_src], W=[pst[bk]])
                evac(w + lv, dst[:, 0:n], pzz, [pst[bk]], [t_dst])
        for lv, (shs, idxs) in enumerate((((1, 2, 3), (6, 7, 8)), ((4, 8, 12), (9, 10, 11)), ((16, 32, 48), (12, 13, 14)))):
            for w in range(NW):
                bk = 2 * w + lv % 2
                src, t_src = (XaA[w], t_XaA[w]) if lv % 2 == 0 else (XaB[w], t_XaB[w])
                dst, t_dst = (XaB[w], t_XaB[w]) if lv % 2 == 0 else (XaA[w], t_XaA[w])
                pzz = PS(bk, 64)
                MM(pzz, ident, src[:, 0:64], True, False, R=[t_ident, t_src], W=[pst[bk]])
                for j in range(3):
                    shf = shs[j]
                    MM(pzz[:, shf:64], Rg[w][:, idxs[j], :], src[:, 0:64 - shf], False, j == 2,
                       R=[t_Rg[w], t_src], W=[pst[bk]])
                if lv < 2:
                    evac(w + lv, dst[:, 0:64], pzz, [pst[bk]], [t_dst])
                else:
                    evac(w + lv, Sx[w][:, 1:65], pzz, [pst[bk]], [t_Sx[w]])
        for w in range(NW):
            sx4 = Sx[w][:, 0:64].rearrange("p (m i) -> p m i", i=4)
            TS(DVE, S2o[w], sx4[:, :, 0], s_sel[:, 0:1], None, ALU.mult, R=[t_Sx[w], t_small], W=[t_S2o[w]])
            for i in range(1, 4):
                STT(DVE, S2o[w], sx4[:, :, i], s_sel[:, i:i + 1], S2o[w], ALU.mult, ALU.add,
                    R=[t_Sx[w], t_small, t_S2o[w]], W=[t_S2o[w]])
        for w in range(NW):
            g = g0 + w
            bk = 2 * w
            pzo = PS(bk, 256)
            MM(pzo, W1[:, g, :], Uown[wsl][:, w, :], True, True, R=[t_w1g[g], t_Uown[wsl]], W=[pst[bk]])
            xo3 = XoA[w].rearrange("p (i m) -> p i m", i=16)
            evac(w, xo3[:, 1:16, :], pzo.rearrange("p (m i) -> p i m", i=16)[:, 0:15, :], [pst[bk]], [t_XoA[w]])
            CP(DVE, xo3[:, 0, :], S2o[w], R=[t_S2o[w]], W=[t_XoA[w]])
        for lv, (shs, idxs) in enumerate((((1, 2, 3), (0, 1, 2)), ((4, 8, 12), (3, 4, 5)))):
            for w in range(NW):
                bk = 2 * w + (lv + 1) % 2
                src, t_src = (XoA[w], t_XoA[w]) if lv == 0 else (XoB[w], t_XoB[w])
                dst, t_dst = (XoB[w], t_XoB[w]) if lv == 0 else (XoA[w], t_XoA[w])
                pzz = PS(bk, 256)
                MM(pzz, ident, src, True, False, R=[t_ident, t_src], W=[pst[bk]])
                for j in range(3):
                    shf = shs[j]
                    MM(pzz[:, shf * 16:256], Rg[w][:, idxs[j], :], src[:, 0:(16 - shf) * 16], False, j == 2,
                       R=[t_Rg[w], t_src], W=[pst[bk]])
                if lv == 0:
                    evac(w + lv, dst, pzz, [pst[bk]], [t_dst])
                else:
                    evac(w + lv, dst.rearrange("p (m i) -> p i m", i=16), pzz.rearrange("p (i m) -> p i m", i=16),
                         [pst[bk]], [t_dst])
        for w in range(NW):
            g = g0 + w
            gl = g % 8
            bk = 2 * w
            py = PS(bk, 256)
            MM(py, K0[:, g, :], Uown[wsl][:, w, :], True, False, R=[t_k0[g], t_Uown[wsl]], W=[pst[bk]])
            MM(py, Ymat[:, g, :], XoA[w], False, True, R=[t_ssmw, t_XoA[w]], W=[pst[bk]])
            ACTV(Yall[:, gl, :], py, AF.Gelu_apprx_tanh, R=[pst[bk]], W=[t_Yall])
            if gl == 7:
                oc = g // 8
                P.dma(ACT, ys_s[oc * 8:(oc + 1) * 8].rearrange("g t h c -> (t h) g c"), Yall, R=[t_Yall], W=[t_ys[oc]])
                pending_reload.append(oc)
    while pending_reload:
        do_reload(pending_reload.pop(0))
    sg = A([128, 512]); t_sg_ = Tok()
    ymq = [ymix[:, 0:4, q4_ * 512:(q4_ + 1) * 512] for q4_ in range(4)]
    t_ymq = [Tok() for _ in range(4)]
    ygf = ygT.rearrange("p a s c -> p a (s c)")
    zsf = zs.rearrange("p a s c -> p a (s c)")
    zmf = zm.rearrange("p a s c -> p a (s c)")
    xo_v = xo.rearrange("(c s) d -> s c d", s=8)
    yo_v = y_out.rearrange("(c s) d -> s c d", s=8)
    t_out = Tok()
    for q4 in range(4):
        for co in range(4):
            b_ = (co * 4 + q4) % 2
            pg = PS(b_)
            for cc in range(4):
                MM(pg, w_glu_b[:, cc, co * 128:(co + 1) * 128], ygf[:, cc, q4 * 512:(q4 + 1) * 512], cc == 0, cc == 3,
                   R=[t_wsm] + t_ygl[cc], W=[pst[b_]])
            ACTV(sg, pg, AF.Sigmoid, R=[pst[b_], t_small], W=[t_sg_], bias=s_bglu[:, co:co + 1])
            TT(DVE, sg, sg, ygf[:, co, q4 * 512:(q4 + 1) * 512], ALU.mult, R=[t_sg_] + t_ygl[co], W=[t_sg_])
            TT(DVE, ymq[q4][:, co, :], sg, zsf[:, co, q4 * 512:(q4 + 1) * 512], ALU.mult,
               R=[t_sg_, t_z], W=[t_ymq[q4]])
        for st in range(q4 * 4, q4 * 4 + 4):
            sl = st % 2
            s_ = st // 2
            c0 = (st % 2) * 128
            if st == 0:
                P.dma(SP, xr[0], xo_v[0, 0:128, :], W=[t_xr[0]])
            if st + 1 < 16:
                P.dma(SP, xr[(st + 1) % 2], xo_v[(st + 1) // 2, ((st + 1) % 2) * 128:((st + 1) % 2) * 128 + 128, :],
                      W=[t_xr[(st + 1) % 2]])
            pf = PS(2 + 2 * sl, 1024, nb=2)
            for hf in range(2):
                for kc in range(8):
                    if kc < 4:
                        lh = ymq[q4][:, kc, (st % 4) * 128:(st % 4 + 1) * 128]
                        rr = [t_ymq[q4], t_wout]
                    else:
                        lh = ymix[:, kc, st * 128:(st + 1) * 128]
                        rr = [t_ymix[kc], t_wout]
                    MM(pf[:, hf * 512:(hf + 1) * 512], lh, w_out_b[:, kc, hf * 512:(hf + 1) * 512],
                       kc == 0, kc == 7, R=rr, W=[pst[2 + 2 * sl], pst[3 + 2 * sl]], inc=(hf == 1 and kc == 7))
            TT(DVE, ot[sl], pf, xr[sl], ALU.add, R=[pst[2 + 2 * sl], pst[3 + 2 * sl], t_xr[sl]], W=[t_ot[sl]])
            P.dma(SP, yo_v[s_, c0:c0 + 128, :], ot[sl], R=[t_ot[sl]], W=[t_out])
    P.barrier()

    with nc.Block() as block:
        def run(E):
            def f(e):
                for waits, fn, inc in E.ops:
                    for key, val in waits:
                        e.wait_ge(sems[key], val)
                    if fn is not None:
                        ins_ = fn(e)
                        if inc is not None:
                            ins_.then_inc(sems[inc[0]], inc[1])
            return f
        block.tensor(run(PE))
        block.scalar(run(ACT))
        block.vector(run(DVE))
        block.gpsimd(run(POOL))
        block.sync(run(SP))
    es.close()
    return nc


_NC = [None]


def _prep(c, x, cvec, positions, w_ada, b_ada, norm_g, w_in, log_dt, lam_re, lam_im, b_re, b_im, c_re, c_im,
          d_skip, w_glu, b_glu, q_a_g, w_q_b, kv_a_g, w_kv_b, q_norm_g, k_norm_g, w_out):
    b = c // 4
    j = c % 4
    f = np.float32
    ac = np.ascontiguousarray
    own_blocks = [4 * m + j for m in range(NO)]
    xbv = ac(x[b])
    xov = ac(np.concatenate([x[b, blk * 128:(blk + 1) * 128] for blk in own_blocks], axis=0))
    pos = positions[b].astype(np.int32)
    posb = ac(pos.reshape(NT, 128).T)
    poso = ac(np.stack([pos[blk * 128:(blk + 1) * 128] for blk in own_blocks], axis=1))

    def T128(v):
        return ac(v.reshape(-1, 128).T.astype(f))

    lamP = np.zeros((128, 2, 32), f)
    for hf in range(2):
        lamP[hf * 64:(hf + 1) * 64, 0, :] = lam_re[0].T
        lamP[hf * 64:(hf + 1) * 64, 1, :] = lam_im[0].T
    bPv = np.zeros((128, 2, 32, 16), f)
    cPv = np.zeros((128, 2, 32, 16), f)
    for hf in range(2):
        bPv[hf * 64:(hf + 1) * 64, 0] = b_re[0].transpose(1, 0, 2)
        bPv[hf * 64:(hf + 1) * 64, 1] = b_im[0].transpose(1, 0, 2)
        cPv[hf * 64:(hf + 1) * 64, 0] = c_re[0].transpose(2, 0, 1)
        cPv[hf * 64:(hf + 1) * 64, 1] = c_im[0].transpose(2, 0, 1)
    lamS = np.zeros((128, 2, 32, 64), f)
    lamS[:, 0] = lam_re[0][None]
    lamS[:, 1] = lam_im[0][None]
    bSv = np.zeros((128, 2, 32, 64), f)
    dSv = np.zeros((128, 32), f)
    for s in range(8):
        bSv[s * 16:(s + 1) * 16, 0] = b_re[0].transpose(2, 0, 1)
        bSv[s * 16:(s + 1) * 16, 1] = b_im[0].transpose(2, 0, 1)
        dSv[s * 16:(s + 1) * 16, :] = d_skip[0].T
    kk = np.arange(128)[:, None, None]
    ii = np.arange(4)[None, :, None]
    qq = np.arange(128)[None, None, :]
    mdiag = ((128 * ii + kk) <= (128 * j + qq)).astype(f)
    sel = np.zeros((128, 4), f)
    sel[:, j] = 1.0
    return {
        "xb": xbv, "xo": xov, "posb": posb, "poso": poso,
        "cT": T128(cvec[b]), "w_ada": ac(w_ada[0]), "b_adaT": T128(b_ada[0]), "b_gate": ac(b_ada[0][None, 2 * D:3 * D]),
        "norm_gT": T128(norm_g[0]), "w_in": ac(w_in[0]), "w_q_b": ac(w_q_b[0]), "q_a_gT": T128(q_a_g[0]),
        "w_kv_b": ac(w_kv_b[0]), "kv_a_gT": T128(kv_a_g[0]),
        "qg_rep": ac(np.broadcast_to(q_norm_g[0][None], (128, 96)).astype(f)),
        "kg_rep": ac(np.broadcast_to(k_norm_g[0][None], (128, 96)).astype(f)),
        "w_glu": ac(w_glu[0]), "b_gluT": T128(b_glu[0]), "w_out": ac(w_out[0]),
        "ldt_rep": ac(np.broadcast_to(log_dt[0][None], (128, 32)).astype(f)),
        "lamP": lamP, "bP": bPv, "cP": cPv, "lamS": lamS, "bS": bSv, "dS": dSv,
        "mdiag": ac(mdiag), "sel4": sel,
    }


def kernel(**inputs):
    inp = {k: np.asarray(v) for k, v in inputs.items()}
    if _NC[0] is None:
        _NC[0] = build()
    nc = _NC[0]
    args = (inp["x"], inp["c"], inp["positions"], inp["w_ada"], inp["b_ada"], inp["norm_g"], inp["w_in"],
            inp["log_dt"], inp["lam_re"], inp["lam_im"], inp["b_re"], inp["b_im"], inp["c_re"], inp["c_im"],
            inp["d_skip"], inp["w_glu"], inp["b_glu"], inp["q_a_g"], inp["w_q_b"], inp["kv_a_g"], inp["w_kv_b"],
            inp["q_norm_g"], inp["k_norm_g"], inp["w_out"])
    in_maps = [_prep(c, *args) for c in range(8)]
    res = run_bass_kernel_spmd(nc, in_maps, core_ids=list(range(8)))
    out = np.zeros((2, L, D), np.float32)
    for c in range(8):
        b, j = c // 4, c % 4
        y = res.results[c]["y"]
        for m in range(NO):
            blk = 4 * m + j
            out[b, blk * 128:(blk + 1) * 128] = y[m * 128:(m + 1) * 128]
    return out
```

```python
import math
from contextlib import ExitStack
import numpy as np
import concourse.bass as bass
import concourse.mybir as mybir
from concourse.bass_utils import run_bass_kernel_spmd

F32 = mybir.dt.float32
BF16 = mybir.dt.bfloat16
I32 = mybir.dt.int32
ALU = mybir.AluOpType
AF = mybir.ActivationFunctionType
AX = mybir.AxisListType

D = 1024
L = 8192
NT = 64
NO = 16
EPS = 1e-6
TWO_PI = 2.0 * math.pi
TWO_PI_HI = float(np.float32(TWO_PI))
TWO_PI_LO = TWO_PI - TWO_PI_HI
IFS = -math.log(10000.0) / 16.0
IFS_HI = float(np.float32(IFS))
IFS_LO = IFS - IFS_HI
DEBUG = False


class Tok:
    __slots__ = ("w", "r", "wdma")

    def __init__(self):
        self.w = []
        self.r = {}
        self.wdma = False


class Eng:
    def __init__(self, name, nsem=0):
        self.name = name
        self.key = (name, "c")
        self.ops = []
        self.known = {}
        self.count = 0
        self.nsem = nsem
        self.dnext = 0
        self.duses = [0] * nsem


class Prog:
    def __init__(self):
        self.pe = Eng("pe")
        self.act = Eng("act", nsem=8)
        self.dve = Eng("dve")
        self.pool = Eng("pool", nsem=14)
        self.sp = Eng("sp", nsem=24)
        self.engs = [self.pe, self.act, self.dve, self.pool, self.sp]

    def _deps(self, E, R, W, is_dma=False):
        deps = []
        for t in R:
            deps.extend(t.w)
        for t in W:
            if not (is_dma and t.wdma and not t.r):
                deps.extend(t.w)
            deps.extend(t.r.items())
        return deps

    def _waits(self, E, deps):
        waits = {}
        for key, val in deps:
            if key == E.key and E.name == "pe":
                continue
            if E.known.get(key, 0) >= val:
                continue
            if waits.get(key, 0) < val:
                waits[key] = val
        for k, v in waits.items():
            E.known[k] = v
        return list(waits.items())

    def _mark(self, ev, R, W, is_dma=False):
        for t in R:
            if t.r.get(ev[0], 0) < ev[1]:
                t.r[ev[0]] = ev[1]
        for t in W:
            if is_dma and t.wdma and not t.r:
                t.w = t.w + [ev]
            else:
                t.w = [ev]
            t.wdma = is_dma
            t.r = {}

    def emit(self, E, fn, R=(), W=(), inc=True):
        waits = self._waits(E, self._deps(E, R, W))
        if inc:
            E.count += 1
            ev = (E.key, E.count)
            E.ops.append((waits, fn, (E.key, 1)))
        else:
            ev = (E.key, E.count + 1)
            E.ops.append((waits, fn, None))
        self._mark(ev, R, W)

    def dma(self, Q, out, in_, R=(), W=()):
        i = Q.dnext % Q.nsem
        Q.dnext += 1
        key = (Q.name, "d", i)
        n = Q.duses[i]
        deps = self._deps(Q, R, W, is_dma=True)
        if n > 0:
            deps.append((key, 16 * n))
        Q.duses[i] = n + 1
        waits = self._waits(Q, deps)
        ev = (key, 16 * (n + 1))
        Q.ops.append((waits, (lambda e, o=out, s=in_: e.dma_start(out=o, in_=s)), (key, 16)))
        self._mark(ev, R, W, is_dma=True)

    def barrier(self):
        evs = []
        for E in self.engs:
            if E.count:
                evs.append((E.key, E.count))
            for i in range(E.nsem):
                if E.duses[i]:
                    evs.append(((E.name, "d", i), 16 * E.duses[i]))
        for E in self.engs:
            w = self._waits(E, [e for e in evs if not (e[0] == E.key)])
            if w:
                E.ops.append((w, None, None))

    def barrier_on(self, E):
        evs = []
        for X in self.engs:
            if X.count and X is not E:
                evs.append((X.key, X.count))
            for i in range(X.nsem):
                if X.duses[i]:
                    evs.append(((X.name, "d", i), 16 * X.duses[i]))
        w = self._waits(E, evs)
        if w:
            E.ops.append((w, None, None))

    def all_keys(self):
        ks = []
        for E in self.engs:
            ks.append(E.key)
            for i in range(E.nsem):
                ks.append((E.name, "d", i))
        return ks


def build():
    nc = bass.Bass("TRN2", target_bir_lowering=False)
    P = Prog()
    PE, ACT, DVE, POOL, SP = P.pe, P.act, P.dve, P.pool, P.sp

    def din(name, shape, dt=F32):
        return nc.dram_tensor(name, list(shape), dt, kind="ExternalInput").ap()

    xb = din("xb", [L, D])
    xo = din("xo", [NO * 128, D])
    posb = din("posb", [128, NT], I32)
    poso = din("poso", [128, NO], I32)
    cT = din("cT", [128, 8])
    w_ada = din("w_ada", [D, 3 * D])
    b_adaT = din("b_adaT", [128, 24])
    b_gate = din("b_gate", [1, D])
    norm_gT = din("norm_gT", [128, 8])
    w_in = din("w_in", [D, 2208])
    w_q_b = din("w_q_b", [384, 768])
    q_a_gT = din("q_a_gT", [128, 3])
    w_kv_b = din("w_kv_b", [256, 1024])
    kv_a_gT = din("kv_a_gT", [128, 2])
    qg_rep = din("qg_rep", [128, 96])
    kg_rep = din("kg_rep", [128, 96])
    w_glu = din("w_glu", [512, 512])
    b_gluT = din("b_gluT", [128, 4])
    w_out = din("w_out", [D, D])
    ldt_rep = din("ldt_rep", [128, 32])
    lamP = din("lamP", [128, 2, 32])
    bP = din("bP", [128, 2, 32, 16])
    cP = din("cP", [128, 2, 32, 16])
    lamS = din("lamS", [128, 2, 32, 64])
    bS = din("bS", [128, 2, 32, 64])
    dS = din("dS", [128, 32])
    mdiag = din("mdiag", [128, 4, 128])
    sel4 = din("sel4", [128, 4])
    y_out = nc.dram_tensor("y", [NO * 128, D], F32, kind="ExternalOutput").ap()
    kT_s = nc.dram_tensor("kT_s", [8, 96, L], BF16).ap()
    v_s = nc.dram_tensor("v_s", [L, 512], BF16).ap()
    uT_s = nc.dram_tensor("uT_s", [512, 8, 1024], BF16).ap()
    uTo_s = nc.dram_tensor("uTo_s", [512, 8, 256], BF16).ap()
    ys_s = nc.dram_tensor("ys_s", [32, 8, 16, 256], BF16).ap()
    w1_s = nc.dram_tensor("w1_s", [128, 32, 128], BF16).ap()
    ymat_s = nc.dram_tensor("ymat_s", [128, 32, 128], BF16).ap()
    x_s = nc.dram_tensor("x_s", [128, 32, 128], F32).ap()
    yk_s = nc.dram_tensor("yk_s", [128, 32, 128], F32).ap()
    a_s = nc.dram_tensor("a_s", [128, 32, 128], F32).ap()
    rv_s = nc.dram_tensor("rv_s", [128, 2, 32, 15], F32).ap()

    es = ExitStack()
    ARENA = 105184
    IAR = 520
    arena = es.enter_context(nc.sbuf_tensor("arena", [128, ARENA], BF16))
    iarena = es.enter_context(nc.sbuf_tensor("iarena", [128, IAR], I32))
    psum = es.enter_context(nc.psum_tensor("psum", [128, 4096], F32))
    sems = {}
    for k in P.all_keys():
        sems[k] = es.enter_context(nc.semaphore("s_" + "_".join(str(x) for x in k)))

    class Alloc:
        def __init__(self):
            self.off = 0
            self.hole = None

        def __call__(self, shape, dt=BF16):
            n = 1
            for s in shape[1:]:
                n *= s
            nb = n * (4 if dt in (F32, I32) else 2)
            nb = (nb + 63) // 64 * 64
            o = self.off
            if self.hole is not None and o < self.hole[1] and o + nb // 2 > self.hole[0]:
                o = self.hole[1]
            self.off = o + nb // 2
            assert self.off <= ARENA, ("arena overflow", self.off)
            v = arena[0:shape[0], o:o + nb // 2]
            if dt != BF16:
                v = v.bitcast(dt)
            v = v[:, 0:n]
            if len(shape) == 3:
                v = v.rearrange("p (a b) -> p a b", a=shape[1])
            elif len(shape) == 4:
                v = v.rearrange("p (a b c) -> p a b c", a=shape[1], b=shape[2])
            return v

    A = Alloc()
    ioff = [0]

    def AI(shape):
        n = 1
        for d_ in shape[1:]:
            n *= d_
        o = ioff[0]
        ioff[0] += n
        assert ioff[0] <= IAR, ioff[0]
        v = iarena[0:shape[0], o:o + n]
        if len(shape) == 3:
            v = v.rearrange("p (a b) -> p a b", a=shape[1])
        return v

    def KB(k):
        return int(k * 512)

    def PS(bank, n=512, dt=F32, parts=128, nb=1):
        v = psum[0:parts, bank * 512:(bank + nb) * 512]
        if dt == BF16:
            v = v.bitcast(BF16)
        return v[:, 0:n]

    pst = [Tok() for _ in range(8)]

    def MM(out, lhsT, rhs, start, stop, R=(), W=(), inc=True):
        P.emit(PE, lambda e: e.matmul(out, lhsT=lhsT, rhs=rhs, start=start, stop=stop), R, W, inc=inc)

    def TRN(out, in_, ident_ap, R=(), W=(), inc=True):
        P.emit(PE, lambda e: e.transpose(out=out, in_=in_, identity=ident_ap), R, W, inc=inc)

    def ACTV(out, in_, func, R=(), W=(), bias=None, scale=None, accum=None):
        kw = {}
        if bias is not None:
            kw["bias"] = bias
        if scale is not None:
            kw["scale"] = scale
        if accum is not None:
            kw["accum_out"] = accum
        P.emit(ACT, lambda e: e.activation(out=out, in_=in_, func=func, **kw), R, W)

    def TS(E, out, in0, s1, s2, op0, op1=None, R=(), W=()):
        if op1 is None:
            P.emit(E, lambda e: e.tensor_scalar(out=out, in0=in0, scalar1=s1, scalar2=None, op0=op0), R, W)
        else:
            P.emit(E, lambda e: e.tensor_scalar(out=out, in0=in0, scalar1=s1, scalar2=s2, op0=op0, op1=op1), R, W)

    def TT(E, out, in0, in1, op, R=(), W=()):
        P.emit(E, lambda e: e.tensor_tensor(out=out, in0=in0, in1=in1, op=op), R, W)

    def STT(E, out, in0, scalar, in1, op0, op1, R=(), W=()):
        P.emit(E, lambda e: e.scalar_tensor_tensor(out=out, in0=in0, scalar=scalar, in1=in1, op0=op0, op1=op1), R, W)

    def CP(E, out, in_, R=(), W=()):
        if E is ACT:
            P.emit(E, lambda e: e.activation(out=out, in_=in_, func=AF.Copy), R, W)
        else:
            P.emit(E, lambda e: e.tensor_copy(out=out, in_=in_), R, W)

    def RED(E, out, in_, op, R=(), W=()):
        P.emit(E, lambda e: e.tensor_reduce(out=out, in_=in_, axis=AX.X, op=op), R, W)

    def MEMSET(E, ap, val, W=()):
        P.emit(E, lambda e: e.memset(ap, val), (), W)

    def bc(ap, shape):
        return ap.to_broadcast(list(shape))

    tk = Tok
    ident = A([128, 128]); t_ident = Tok()
    identf = A([128, 128], F32)
    ones_bf = A([1, 512]); ones_f = A([1, 128], F32); t_ones = Tok()
    mhalf = A([128, 8], F32)
    c_act = A([128, 8], F32); t_cact = Tok()
    modT = A([128, 24], F32); t_mod = Tok()
    gs = A([128, 8], F32)
    gate_row = A([1, D], F32); t_grow = Tok()
    biasrow = A([1, 2208]); t_brow = Tok()
    biasT = A([128, 12], F32)
    small = A([128, 64], F32); t_small = Tok()
    qg = A([128, 96], F32); kg = A([128, 96], F32); t_g = Tok()
    nbias = A([128, 1], F32); t_nb = Tok()
    invf = A([128, 16], F32)
    md = A([128, 4, 128]); t_md = Tok()
    assert A.off <= KB(14), A.off
    A.off = KB(14)
    QT = A([128, 8, NO * 128]); t_qt = Tok()
    zm = A([128, 4, 8, 256]); zs = A([128, 4, 8, 256]); t_z = Tok()
    w_in_b = A([128, 8, 2208]); t_win = Tok()
    w_qb_b = A([128, 3, 768]); w_kvb_b = A([128, 2, 1024]); t_wsm = Tok()
    cosb = A([128, NT, 16], F32); sinb = A([128, NT, 16], F32); t_ropeb = Tok()
    coso = A([128, NO, 16], F32); sino = A([128, NO, 16], F32); t_ropeo = Tok()
    assert A.off <= KB(131), A.off
    A.off = KB(78)
    ymix = A([128, 8, NO * 128]); t_ymix = [Tok() for _ in range(8)]
    W1 = A([128, 32, 128]); Ymat = A([128, 32, 128]); K0 = A([128, 32, 128]); t_ssmw = Tok()
    t_rv = Tok()
    w_glu_b = A([128, 4, 512])
    assert A.off <= KB(141), A.off
    base_off = KB(131)
    A.off = base_off

    sb_adaT = small[:, 0:24]; s_normg = small[:, 24:32]; s_qag = small[:, 32:35]; s_kvag = small[:, 35:37]
    s_bglu = small[:, 37:41]; s_sel = small[:, 41:45]

    P.dma(SP, small[:, 0:24], b_adaT, W=[t_small])
    P.dma(SP, small[:, 24:32], norm_gT, W=[t_small])
    P.dma(SP, small[:, 32:35], q_a_gT, W=[t_small])
    P.dma(SP, small[:, 35:37], kv_a_gT, W=[t_small])
    P.dma(SP, small[:, 37:41], b_gluT, W=[t_small])
    P.dma(SP, small[:, 41:45], sel4, W=[t_small])
    P.dma(SP, qg, qg_rep, W=[t_g])
    P.dma(SP, kg, kg_rep, W=[t_g])
    P.dma(SP, c_act, cT, W=[t_cact])
    P.dma(SP, gate_row, b_gate, W=[t_grow])
    MEMSET(POOL, ident, 1.0, W=[t_ident])
    P.emit(POOL, lambda e: e.affine_select(out=ident, in_=ident, pattern=[[1, 128]], compare_op=ALU.is_equal,
                                           fill=0.0, base=0, channel_multiplier=-1), [t_ident], [t_ident])
    MEMSET(POOL, identf, 1.0, W=[t_ident])
    P.emit(POOL, lambda e: e.affine_select(out=identf, in_=identf, pattern=[[1, 128]], compare_op=ALU.is_equal,
                                           fill=0.0, base=0, channel_multiplier=-1), [t_ident], [t_ident])
    MEMSET(POOL, ones_bf, 1.0, W=[t_ones])
    MEMSET(POOL, ones_f, 1.0, W=[t_ones])
    MEMSET(POOL, mhalf, -0.5, W=[t_ones])

    def rsqrt_mean(out, ssq, n, width, R, W):
        TS(POOL, out, ssq, 1.0 / n, EPS, ALU.mult, ALU.add, R=R, W=W)
        TT(POOL, out, out, mhalf[:, 0:width], ALU.pow, R=list(W) + [t_ones], W=W)

    def sincos(ang, sin_out, cos_out, tmp, tmpi, toks, E=None):
        E = DVE if E is None else E

        def STT(E_, out, in0, scalar, in1, op0, op1, R=(), W=()):
            if E_ is DVE:
                P.emit(E_, lambda e: e.scalar_tensor_tensor(out=out, in0=in0, scalar=scalar, in1=in1, op0=op0, op1=op1), R, W)
            else:
                TS(E_, in0, in0, scalar, None, op0, R=R, W=W)
                TT(E_, out, in0, in1, op1, R=R, W=W)

        def reduce_into(dst, src, shift):
            TS(E, tmp, src, shift, 1.0 / TWO_PI, ALU.add, ALU.mult, R=toks, W=toks)
            CP(E, tmpi, tmp, R=toks, W=toks)
            CP(E, tmp, tmpi, R=toks, W=toks)
            STT(E, dst, tmp, -TWO_PI_HI, src, ALU.mult, ALU.add, R=toks, W=toks)
            STT(E, dst, tmp, -TWO_PI_LO, dst, ALU.mult, ALU.add, R=toks, W=toks)
            if shift != 0.0:
                TS(E, dst, dst, shift, None, ALU.add, R=toks, W=toks)
            TS(E, tmp, dst, math.pi, None, ALU.is_gt, R=toks, W=toks)
            STT(E, dst, tmp, -TWO_PI, dst, ALU.mult, ALU.add, R=toks, W=toks)
            TS(E, tmp, dst, -1.0, None, ALU.mult, R=toks, W=toks)
            TS(E, tmp, tmp, math.pi, None, ALU.is_gt, R=toks, W=toks)
            STT(E, dst, tmp, TWO_PI, dst, ALU.mult, ALU.add, R=toks, W=toks)
            TS(E, dst, dst, math.pi, -math.pi, ALU.min, ALU.max, R=toks, W=toks)
        reduce_into(sin_out, ang, 0.0)
        TS(E, cos_out, sin_out, math.pi / 2.0, None, ALU.add, R=toks, W=toks)
        TS(E, tmp, cos_out, math.pi, None, ALU.is_gt, R=toks, W=toks)
        STT(E, cos_out, tmp, -TWO_PI, cos_out, ALU.mult, ALU.add, R=toks, W=toks)
        TS(E, cos_out, cos_out, math.pi, -math.pi, ALU.min, ALU.max, R=toks, W=toks)
        ACTV(cos_out, cos_out, AF.Sin, R=toks, W=toks)
        ACTV(sin_out, sin_out, AF.Sin, R=toks, W=toks)

    mark0 = KB(14)
    A.off = mark0
    ACTV(c_act, c_act, AF.Silu, R=[t_cact], W=[t_cact])
    wst = [A([128, 8, 512], F32), A([128, 8, 512], F32)]
    t_wst = [Tok(), Tok()]
    ps_mod = PS(0, 24)
    ps_grow = [PS(1, 512, parts=1), PS(2, 512, parts=1)]
    ada_n = [0]

    def ada_chunk(ch):
        sl = ada_n[0] % 2
        ada_n[0] += 1
        P.dma(SP, wst[sl], w_ada[:, ch * 512:(ch + 1) * 512].rearrange("(k p) n -> p k n", p=128), W=[t_wst[sl]])
        for c4 in range(4 if ch < 4 else 0):
            cc = ch * 4 + c4
            for kc in range(8):
                MM(ps_mod[:, cc:cc + 1], wst[sl][:, kc, c4 * 128:(c4 + 1) * 128], c_act[:, kc:kc + 1],
                   kc == 0, kc == 7, R=[t_wst[sl], t_cact], W=[pst[0]])
        if ch >= 4:
            for kc in range(8):
                MM(ps_grow[ch - 4], c_act[:, kc:kc + 1], wst[sl][:, kc, :], kc == 0, kc == 7,
                   R=[t_wst[sl], t_cact], W=[pst[1 + ch - 4]])

    ada_chunk(2)
    ada_chunk(3)
    t_gs = Tok()
    TT(DVE, modT[:, 8:16], ps_mod[:, 8:16], sb_adaT[:, 8:16], ALU.add, R=[pst[0], t_small], W=[t_gs])
    STT(DVE, gs, modT[:, 8:16], 1.0, s_normg, ALU.add, ALU.mult, R=[t_gs, t_small], W=[t_gs])
    ada_chunk(0)
    ada_chunk(1)
    TT(DVE, modT[:, 0:8], ps_mod[:, 0:8], sb_adaT[:, 0:8], ALU.add, R=[pst[0], t_small], W=[t_mod])
    wst2 = [A([128, 2208], F32), A([128, 2208], F32)]
    t_wst2 = [Tok(), Tok()]
    ps_brow = [PS(1 + i, 512, parts=1) for i in range(5)]
    for kc in range(8):
        sl = kc % 2
        P.dma(POOL, wst2[sl], w_in[kc * 128:(kc + 1) * 128, :], W=[t_wst2[sl]])
        if kc % 2 == 0:
            TS(DVE, w_in_b[:, kc, :], wst2[sl], gs[:, kc:kc + 1], None, ALU.mult, R=[t_wst2[sl], t_gs], W=[t_win])
        else:
            ACTV(w_in_b[:, kc, :], wst2[sl], AF.Copy, R=[t_wst2[sl], t_gs], W=[t_win], scale=gs[:, kc:kc + 1])
        for i in range(5):
            n0 = i * 512
            n1 = min(2208, n0 + 512)
            MM(ps_brow[i][:, 0:n1 - n0], modT[:, kc:kc + 1], wst2[sl][:, n0:n1], kc == 0, kc == 7,
               R=[t_wst2[sl], t_mod], W=[pst[1 + i]])
    for i in range(5):
        n0 = i * 512
        n1 = min(2208, n0 + 512)
        CP(DVE, biasrow[:, n0:n1], ps_brow[i][:, 0:n1 - n0], R=[pst[1 + i]], W=[t_brow])
    ada_chunk(4)
    ada_chunk(5)
    for hf in range(2):
        TT(DVE, gate_row[:, hf * 512:(hf + 1) * 512], ps_grow[hf], gate_row[:, hf * 512:(hf + 1) * 512], ALU.add,
           R=[pst[1 + hf], t_grow], W=[t_grow])
    FM_COLS = [0, 128, 256, 384, 512, 640, 768, 896, 1696, 1824, 1952, 2080]
    ps_bt = PS(6, 16)
    for j_, c0_ in enumerate(FM_COLS):
        MM(ps_bt[:, j_:j_ + 1], biasrow[0:1, c0_:c0_ + 128], ones_bf[0:1, 0:1], True, True, R=[t_brow, t_ones], W=[pst[6]])
    CP(DVE, biasT, ps_bt[:, 0:12], R=[pst[6]], W=[t_brow])
    for kc in range(3):
        sl = kc % 2
        P.dma(SP, wst2[sl][:, 0:768], w_q_b[kc * 128:(kc + 1) * 128, :], W=[t_wst2[sl]])
        ACTV(w_qb_b[:, kc, :], wst2[sl][:, 0:768], AF.Copy, R=[t_wst2[sl], t_small], W=[t_wsm], scale=s_qag[:, kc:kc + 1])
    for kc in range(2):
        sl = (kc + 1) % 2
        P.dma(SP, wst2[sl][:, 0:1024], w_kv_b[kc * 128:(kc + 1) * 128, :], W=[t_wst2[sl]])
        TS(DVE, w_kvb_b[:, kc, :], wst2[sl][:, 0:1024], s_kvag[:, kc:kc + 1], None, ALU.mult, R=[t_wst2[sl], t_small], W=[t_wsm])
    P.dma(SP, wst2[0][:, 0:512].rearrange("p (a b) -> p a b", a=4), mdiag, W=[t_wst2[0]])
    CP(DVE, md, wst2[0][:, 0:512].rearrange("p (a b) -> p a b", a=4), R=[t_wst2[0]], W=[t_md])
    tq = A([128, 96], F32); tmx = A([128, 2], F32); t_tmp0 = Tok()
    TS(DVE, tq, qg, -1.0, None, ALU.mult, R=[t_g], W=[t_tmp0])
    TT(DVE, tq, tq, qg, ALU.max, R=[t_g, t_tmp0], W=[t_tmp0])
    RED(DVE, tmx[:, 0:1], tq, ALU.max, R=[t_tmp0], W=[t_tmp0])
    TS(DVE, tq, kg, -1.0, None, ALU.mult, R=[t_g, t_tmp0], W=[t_tmp0])
    TT(DVE, tq, tq, kg, ALU.max, R=[t_g, t_tmp0], W=[t_tmp0])
    RED(DVE, tmx[:, 1:2], tq, ALU.max, R=[t_tmp0], W=[t_tmp0])
    TT(DVE, nbias, tmx[:, 0:1], tmx[:, 1:2], ALU.mult, R=[t_tmp0], W=[t_nb])
    TS(DVE, nbias, nbias, -math.sqrt(96.0), None, ALU.mult, R=[t_nb], W=[t_nb])
    t_rp = Tok()
    ii = AI([128, 16])
    P.emit(POOL, lambda e: e.iota(ii, pattern=[[1, 16]], base=0, channel_multiplier=0), (), [t_rp])
    CP(DVE, invf, ii, R=[t_rp], W=[t_rp])
    invc = A([128, 16], F32)
    TS(DVE, invc, invf, IFS_LO, 1.0, ALU.mult, ALU.add, R=[t_rp], W=[t_rp])
    ACTV(invf, invf, AF.Exp, R=[t_rp], W=[t_rp], scale=IFS_HI)
    TT(DVE, invf, invf, invc, ALU.mult, R=[t_rp], W=[t_rp])
    pbi = AI([128, NT]); pbf = A([128, NT], F32); poi = AI([128, NO]); pof = A([128, NO], F32)
    P.dma(SP, pbi, posb, W=[t_rp])
    P.dma(SP, poi, poso, W=[t_rp])
    CP(DVE, pbf, pbi, R=[t_rp], W=[t_rp])
    CP(DVE, pof, poi, R=[t_rp], W=[t_rp])
    angb = A([128, NO, 16], F32); tmpb = A([128, NO, 16], F32); tmpbi = AI([128, NO, 16])
    rope_ops = []
    _re, _rd = P.emit, P.dma
    for ch_ in range(NT // NO):
        if ch_ == 1:
            P.emit = lambda E, fn, R=(), W=(): rope_ops.append((E, fn, tuple(R), tuple(W)))
        sl_ = slice(ch_ * NO, (ch_ + 1) * NO)
        TT(DVE, angb, bc(pbf[:, sl_].rearrange("p (t o) -> p t o", o=1), [128, NO, 16]),
           bc(invf.rearrange("p (o i) -> p o i", o=1), [128, NO, 16]), ALU.mult, R=[t_rp], W=[t_rp])
        sincos(angb, sinb[:, sl_, :], cosb[:, sl_, :], tmpb, tmpbi, [t_rp, t_ropeb])
    TT(DVE, angb, bc(pof.rearrange("p (t o) -> p t o", o=1), [128, NO, 16]),
       bc(invf.rearrange("p (o i) -> p o i", o=1), [128, NO, 16]), ALU.mult, R=[t_rp], W=[t_rp])
    sincos(angb, sino, coso, tmpb, tmpbi, [t_rp, t_ropeo])
    P.emit, P.dma = _re, _rd
    rope_pos = [0]

    def rope_pump(n):
        for _ in range(n):
            if rope_pos[0] >= len(rope_ops):
                return
            E_, fn_, R_, W_ = rope_ops[rope_pos[0]]
            rope_pos[0] += 1
            P.emit(E_, fn_, R_, W_)

    assert A.off <= KB(78), A.off
    A.off = base_off
    own_sync = [False]

    def own_region_sync():
        if not own_sync[0]:
            own_sync[0] = True
            P.barrier_on(ACT)
            P.barrier_on(DVE)
    mark1 = A.off
    NX = 3
    xt = [A([128, D], F32) for _ in range(NX)]; t_xt = [Tok() for _ in range(NX)]
    xs = [A([128, D]), A([128, D])]; t_xs = [Tok(), Tok()]
    junk = A([128, 384]); t_junk = Tok()
    NSTAT = 5
    st8 = A([128, NSTAT, 32], F32); t_st = [Tok() for _ in range(NSTAT)]
    hT = [A([128, 8, 512]), A([128, 8, 512])]; t_hT = [Tok(), Tok()]
    uTp = A([128, 4, 8, 256]); t_uTp = Tok()
    kTq = [A([128, 8, 256]), A([128, 8, 256])]; t_kTq = [Tok(), Tok()]
    ckvn = [A([128, 384]), A([128, 384])]; t_ckv = [Tok(), Tok()]
    ckvnT = [A([128, 3, 128]), A([128, 3, 128])]; t_ckvT = [Tok(), Tok()]
    sqt = [A([128, 8, 96], F32), A([128, 8, 96], F32)]; t_sqt = [Tok(), Tok()]
    Kt = [A([128, 8, 96]), A([128, 8, 96])]; t_Kt = [Tok(), Tok()]
    krg = [A([128, 32], F32) for _ in range(3)]; krot = [A([128, 32], F32) for _ in range(3)]
    kr4 = A([128, 4, 16], F32); t_kr = [Tok() for _ in range(3)]; t_kr4 = Tok()
    qr4 = A([128, 4, 8, 16], F32)
    vt = [A([128, 512]), A([128, 512])]; t_vt = [Tok(), Tok()]
    kg64 = bc(kg[:, 0:64].rearrange("p (o d) -> p o d", o=1), [128, 8, 64])
    qg96 = bc(qg.rearrange("p (o d) -> p o d", o=1), [128, 8, 96])

    def rope32(dst_re, dst_im, x1, x2, cos_t, sin_t, tmp4, toks):
        TT(DVE, tmp4[0], x1, cos_t, ALU.mult, R=toks, W=toks)
        TT(DVE, tmp4[1], x2, sin_t, ALU.mult, R=toks, W=toks)
        TT(DVE, tmp4[2], x2, cos_t, ALU.mult, R=toks, W=toks)
        TT(DVE, tmp4[3], x1, sin_t, ALU.mult, R=toks, W=toks)
        TT(DVE, dst_re, tmp4[0], tmp4[1], ALU.subtract, R=toks, W=toks)
        TT(DVE, dst_im, tmp4[2], tmp4[3], ALU.add, R=toks, W=toks)

    def stA(t, src_rows):
        sl = t % NX
        s2 = t % 2
        ss = t % NSTAT
        P.dma(SP, xt[sl], src_rows, W=[t_xt[sl]])
        ACTV(xs[s2], xt[sl], AF.Square, R=[t_xt[sl]], W=[t_xs[s2], t_st[ss]], accum=st8[:, ss, 0:1])
        rsqrt_mean(st8[:, ss, 1:2], st8[:, ss, 0:1], float(D), 1, R=[t_st[ss]], W=[t_st[ss]])
        TS(DVE, xs[s2], xt[sl], st8[:, ss, 1:2], None, ALU.mult, R=[t_xt[sl], t_st[ss]], W=[t_xs[s2]])

    def stB(t, gbase=0):
        s2 = t % 2
        hs = ((t + gbase) // 4) % 2
        tq = t % 4
        pT = PS(0, 1024, BF16).rearrange("p (k t) -> p k t", k=8)
        for kc in range(8):
            TRN(pT[:, kc, :], xs[s2][:, kc * 128:(kc + 1) * 128], ident, R=[t_xs[s2], t_ident], W=[pst[0]], inc=(kc == 7))
        CP(ACT, hT[hs][:, :, tq * 128:(tq + 1) * 128], pT, R=[pst[0]], W=[t_hT[hs]])

    def proj_fm(ps_ap, pstok, col0, hslot, n=512):
        for kc in range(8):
            MM(ps_ap, w_in_b[:, kc, col0:col0 + 128], hT[hslot][:, kc, 0:n], kc == 0, kc == 7,
               R=[t_win, t_hT[hslot]], W=[pstok], inc=(kc == 7))

    def proj_tm(ps_ap, pstok, col0, ncol, hslot, tq):
        for kc in range(8):
            MM(ps_ap, hT[hslot][:, kc, tq * 128:(tq + 1) * 128], w_in_b[:, kc, col0:col0 + ncol], kc == 0, False,
               R=[t_win, t_hT[hslot]], W=[pstok], inc=False)
        MM(ps_ap, ones_bf[0:1, 0:128], biasrow[0:1, col0:col0 + ncol], False, True, R=[t_brow, t_ones], W=[pstok])

    def run_pipeline(ntiles, stages, after_step):
        nst = len(stages)
        for step in range(ntiles + nst - 1):
            if 0 <= step < ntiles:
                stages[0](step)
            for si in range(nst - 1, 0, -1):
                t = step - si
                if 0 <= t < ntiles:
                    stages[si](t)
            after_step(step)

    def p1A(t):
        stA(t, xb[t * 128:(t + 1) * 128, :])

    def p1B(t):
        stB(t)

    def p1C(t):
        hs = (t // 4) % 2
        tq = t % 4
        s2 = t % 2
        s3 = t % 3
        ss = t % NSTAT
        pkv = PS(3, 288)
        proj_tm(pkv, pst[3], 1408, 288, hs, tq)
        ACTV(junk[:, 0:256], pkv[:, 0:256], AF.Square, R=[pst[3]], W=[t_st[ss]], accum=st8[:, ss, 2:3])
        rsqrt_mean(st8[:, ss, 3:4], st8[:, ss, 2:3], 256.0, 1, R=[t_st[ss]], W=[t_st[ss]])
        TS(DVE, ckvn[s2][:, 0:256], pkv[:, 0:256], st8[:, ss, 3:4], None, ALU.mult, R=[pst[3], t_st[ss]], W=[t_ckv[s2]])
        ACTV(junk[:, 256:288], pkv[:, 256:288], AF.Square, R=[pst[3]], W=[t_st[ss]], accum=st8[:, ss, 4:5])
        TT(DVE, krg[s3], pkv[:, 256:288], kg[:, 64:96], ALU.mult, R=[pst[3], t_g], W=[t_kr[s3]])
        rope32(krot[s3][:, 0:16], krot[s3][:, 16:32], krg[s3][:, 0:16], krg[s3][:, 16:32], cosb[:, t, :], sinb[:, t, :],
               [kr4[:, i, :] for i in range(4)], [t_kr[s3], t_ropeb, t_kr4])

    def p1D(t):
        s2 = t % 2
        pT2 = PS(4, 256, BF16).rearrange("p (k t) -> p k t", k=2)
        for k2 in range(2):
            TRN(pT2[:, k2, :], ckvn[s2][:, k2 * 128:(k2 + 1) * 128], ident, R=[t_ckv[s2], t_ident], W=[pst[4]], inc=(k2 == 1))
        CP(ACT, ckvnT[s2][:, 0:2, :], pT2, R=[pst[4]], W=[t_ckvT[s2]])

    def p1E(t):
        s2 = t % 2
        s3 = t % 3
        ss = t % NSTAT
        pkvf = PS(5, 1024, nb=2)
        for hf in range(2):
            for k2 in range(2):
                MM(pkvf[:, hf * 512:(hf + 1) * 512], ckvnT[s2][:, k2, :], w_kvb_b[:, k2, hf * 512:(hf + 1) * 512],
                   k2 == 0, k2 == 1, R=[t_ckvT[s2], t_wsm], W=[pst[5], pst[6]], inc=(hf == 1 and k2 == 1))
        kv3 = pkvf.rearrange("p (h d) -> p h d", h=8)
        ACTV(sqt[s2][:, :, 0:64], kv3[:, :, 0:64], AF.Square, R=[pst[5], pst[6]], W=[t_sqt[s2]])
        ssh = st8[:, ss, 8:16]
        RED(DVE, ssh, sqt[s2][:, :, 0:64], ALU.add, R=[t_sqt[s2]], W=[t_st[ss]])
        TS(POOL, ssh, ssh, st8[:, ss, 4:5], None, ALU.add, R=[t_st[ss]], W=[t_st[ss]])
        rk = st8[:, ss, 16:24]
        rsqrt_mean(rk, ssh, 96.0, 8, R=[t_st[ss]], W=[t_st[ss]])
        rk64 = bc(rk.rearrange("p (h o) -> p h o", o=1), [128, 8, 64])
        TT(DVE, sqt[s2][:, :, 0:64], kv3[:, :, 0:64], rk64, ALU.mult, R=[pst[5], pst[6], t_st[ss]], W=[t_sqt[s2]])
        TT(DVE, Kt[s2][:, :, 0:64], sqt[s2][:, :, 0:64], kg64, ALU.mult, R=[t_sqt[s2], t_g], W=[t_Kt[s2]])
        TT(DVE, Kt[s2][:, :, 64:96], bc(krot[s3].rearrange("p (o d) -> p o d", o=1), [128, 8, 32]),
           bc(rk.rearrange("p (h o) -> p h o", o=1), [128, 8, 32]), ALU.mult, R=[t_kr[s3], t_st[ss]], W=[t_Kt[s2]])
        CP(ACT, vt[s2].rearrange("p (h d) -> p h d", h=8), kv3[:, :, 64:128], R=[pst[5], pst[6]], W=[t_vt[s2]])
        P.dma(ACT, v_s[t * 128:(t + 1) * 128, :], vt[s2], R=[t_vt[s2]])

    def p1F(t):
        s2 = t % 2
        ks = (t // 2) % 2
        pKT = PS(7, 1024, BF16, parts=96).rearrange("p (h t) -> p h t", h=8)
        for h in range(8):
            TRN(pKT[:, h, :], Kt[s2][:, h, :], ident, R=[t_Kt[s2], t_ident], W=[pst[7]], inc=(h == 7))
        CP(ACT, kTq[ks][0:96, :, (t % 2) * 128:(t % 2 + 1) * 128], pKT, R=[pst[7]], W=[t_kTq[ks]])
        if t % 2 == 1:
            P.dma(ACT, kT_s.rearrange("h d t -> d h t")[:, :, (t - 1) * 128:(t + 1) * 128], kTq[ks][0:96], R=[t_kTq[ks]])

    def p1after(step):
        u_ = step - 4
        if u_ < 0:
            return
        qd = u_ // 4
        cc = u_ % 4
        if qd >= NT // 4:
            return
        hs = qd % 2
        b_ = 1 + cc % 2
        pu = PS(b_)
        proj_fm(pu, pst[b_], cc * 128, hs)
        TS(DVE, uTp[:, cc, :, (qd % 4) * 64:(qd % 4) * 64 + 64], pu.rearrange("p (c s) -> p s c", s=8),
           biasT[:, cc:cc + 1], None, ALU.add, R=[pst[b_], t_brow], W=[t_uTp])
        if qd % 4 == 3 and cc == 3:
            sq_ = qd // 4
            for c2 in range(4):
                P.dma(POOL, uT_s[c2 * 128:(c2 + 1) * 128, :, sq_ * 256:(sq_ + 1) * 256], uTp[:, c2], R=[t_uTp])

    wg_ops = []
    real_emit, real_dma = P.emit, P.dma

    def rec_emit(E, fn, R=(), W=()):
        wg_ops.append(("e", E, fn, tuple(R), tuple(W)))

    def rec_dma(Q, out, in_, R=(), W=()):
        wg_ops.append(("d", Q, out, in_, tuple(R), tuple(W)))

    saved_off = A.off
    A.off = KB(78)
    A.hole = (KB(94), KB(110))
    P.emit, P.dma = rec_emit, rec_dma
    t_sg = Tok()
    lp = A([128, 2, 32], F32); dtp = A([128, 32], F32)
    P.dma(SP, lp, lamP, W=[t_sg])
    P.dma(SP, dtp, ldt_rep, W=[t_sg])
    ACTV(dtp, dtp, AF.Exp, R=[t_sg], W=[t_sg])
    mag = A([128, 32], F32); ang = A([128, 32], F32); sn = A([128, 32], F32); cs = A([128, 32], F32)
    tmpa = A([128, 32], F32); tmpai = AI([128, 32])
    TT(DVE, mag, lp[:, 0, :], dtp, ALU.mult, R=[t_sg], W=[t_sg])
    ACTV(mag, mag, AF.Exp, R=[t_sg], W=[t_sg])
    TT(DVE, ang, lp[:, 1, :], dtp, ALU.mult, R=[t_sg], W=[t_sg])
    sincos(ang, sn, cs, tmpa, tmpai, [t_sg])
    PW = A([128, 2, 32, 13], F32)
    RVt = A([128, 2, 32, 15], F32)
    pr_ = sn; pi_ = cs
    TT(DVE, PW[:, 0, :, 0], mag, cs, ALU.mult, R=[t_sg], W=[t_sg])
    TT(DVE, PW[:, 1, :, 0], mag, sn, ALU.mult, R=[t_sg], W=[t_sg])
    for k in range(12):
        TT(DVE, tmpa, PW[:, 0, :, k], PW[:, 0, :, k], ALU.mult, R=[t_sg], W=[t_sg])
        TT(DVE, ang, PW[:, 1, :, k], PW[:, 1, :, k], ALU.mult, R=[t_sg], W=[t_sg])
        TT(DVE, PW[:, 0, :, k + 1], tmpa, ang, ALU.subtract, R=[t_sg], W=[t_sg])
        TT(DVE, tmpa, PW[:, 0, :, k], PW[:, 1, :, k], ALU.mult, R=[t_sg], W=[t_sg])
        TS(DVE, PW[:, 1, :, k + 1], tmpa, 2.0, None, ALU.mult, R=[t_sg], W=[t_sg])
    RV_SPEC = [(3, None), (4, None), (3, 4), (5, None), (6, None), (5, 6),
               (7, None), (8, None), (7, 8), (9, None), (10, None), (9, 10), (11, None), (12, None), (11, 12)]
    for k, (ka, kb_) in enumerate(RV_SPEC):
        if kb_ is None:
            sre, sim = PW[:, 0, :, ka], PW[:, 1, :, ka]
        else:
            TT(DVE, tmpa, PW[:, 0, :, ka], PW[:, 0, :, kb_], ALU.mult, R=[t_sg], W=[t_sg])
            TT(DVE, ang, PW[:, 1, :, ka], PW[:, 1, :, kb_], ALU.mult, R=[t_sg], W=[t_sg])
            TT(DVE, pr_, tmpa, ang, ALU.subtract, R=[t_sg], W=[t_sg])
            TT(DVE, tmpa, PW[:, 0, :, ka], PW[:, 1, :, kb_], ALU.mult, R=[t_sg], W=[t_sg])
            TT(DVE, ang, PW[:, 1, :, ka], PW[:, 0, :, kb_], ALU.mult, R=[t_sg], W=[t_sg])
            TT(DVE, pi_, tmpa, ang, ALU.add, R=[t_sg], W=[t_sg])
            sre, sim = pr_, pi_
        CP(DVE, RVt[0:64, 0, :, k], sre[0:64], R=[t_sg], W=[t_sg])
        TS(DVE, RVt[64:128, 0, :, k], sim[64:128], -1.0, None, ALU.mult, R=[t_sg], W=[t_sg])
        CP(DVE, RVt[0:64, 1, :, k], sim[0:64], R=[t_sg], W=[t_sg])
        CP(DVE, RVt[64:128, 1, :, k], sre[64:128], R=[t_sg], W=[t_sg])
    P.dma(POOL, rv_s, RVt, R=[t_sg])
    PT = A([128, 2, 32, 9], F32)
    MEMSET(DVE, PT[:, 0, :, 0], 1.0, W=[t_sg])
    MEMSET(DVE, PT[:, 1, :, 0], 0.0, W=[t_sg])

    def cmul(E, ore, oim, are, aim, bre, bim, t1, t2, toks):
        TT(E, t1, are, bre, ALU.mult, R=toks, W=toks)
        TT(E, t2, aim, bim, ALU.mult, R=toks, W=toks)
        TT(E, ore, t1, t2, ALU.subtract, R=toks, W=toks)
        TT(E, t1, are, bim, ALU.mult, R=toks, W=toks)
        TT(E, t2, aim, bre, ALU.mult, R=toks, W=toks)
        TT(E, oim, t1, t2, ALU.add, R=toks, W=toks)

    for t in range(8):
        cmul(DVE, PT[:, 0, :, t + 1], PT[:, 1, :, t + 1], PT[:, 0, :, t], PT[:, 1, :, t], PW[:, 0, :, 0], PW[:, 1, :, 0],
             tmpa, ang, [t_sg])
    PI_ = A([128, 2, 32, 8], F32)
    xa_raw = A([128, 512], F32); xb_raw = A([128, 512], F32)
    n2 = xa_raw[:, 0:256].rearrange("p (g s) -> p g s", g=32); n2b = xb_raw[:, 0:256].rearrange("p (g s) -> p g s", g=32)
    TT(DVE, n2, PT[:, 0, :, 0:8], PT[:, 0, :, 0:8], ALU.mult, R=[t_sg], W=[t_sg])
    TT(DVE, n2b, PT[:, 1, :, 0:8], PT[:, 1, :, 0:8], ALU.mult, R=[t_sg], W=[t_sg])
    TT(DVE, n2, n2, n2b, ALU.add, R=[t_sg], W=[t_sg])
    P.emit(DVE, lambda e: e.reciprocal(out=n2, in_=n2), [t_sg], [t_sg])
    TT(DVE, PI_[:, 0, :, :], PT[:, 0, :, 0:8], n2, ALU.mult, R=[t_sg], W=[t_sg])
    TT(DVE, n2b, PT[:, 1, :, 0:8], n2, ALU.mult, R=[t_sg], W=[t_sg])
    TS(DVE, PI_[:, 1, :, :], n2b, -1.0, None, ALU.mult, R=[t_sg], W=[t_sg])
    PTr = A([128, 2, 32, 8], F32)
    for s_ in range(8):
        CP(DVE, PTr[:, :, :, s_], PT[:, :, :, 7 - s_], R=[t_sg], W=[t_sg])
    cf = A([128, 2, 32], F32); l2 = A([128, 32], F32); l2b = A([128, 32], F32)
    TT(DVE, l2, lp[:, 0, :], lp[:, 0, :], ALU.mult, R=[t_sg], W=[t_sg])
    TT(DVE, l2b, lp[:, 1, :], lp[:, 1, :], ALU.mult, R=[t_sg], W=[t_sg])
    TT(DVE, l2, l2, l2b, ALU.add, R=[t_sg], W=[t_sg])
    P.emit(DVE, lambda e: e.reciprocal(out=l2, in_=l2), [t_sg], [t_sg])
    lb1 = A([128, 32], F32)
    TS(DVE, lb1, PW[:, 0, :, 0], -1.0, None, ALU.add, R=[t_sg], W=[t_sg])
    TT(DVE, tmpa, lb1, lp[:, 0, :], ALU.mult, R=[t_sg], W=[t_sg])
    TT(DVE, ang, PW[:, 1, :, 0], lp[:, 1, :], ALU.mult, R=[t_sg], W=[t_sg])
    TT(DVE, tmpa, tmpa, ang, ALU.add, R=[t_sg], W=[t_sg])
    TT(DVE, cf[:, 0, :], tmpa, l2, ALU.mult, R=[t_sg], W=[t_sg])
    TT(DVE, tmpa, PW[:, 1, :, 0], lp[:, 0, :], ALU.mult, R=[t_sg], W=[t_sg])
    TT(DVE, ang, lb1, lp[:, 1, :], ALU.mult, R=[t_sg], W=[t_sg])
    TT(DVE, tmpa, tmpa, ang, ALU.subtract, R=[t_sg], W=[t_sg])
    TT(DVE, cf[:, 1, :], tmpa, l2, ALU.mult, R=[t_sg], W=[t_sg])
    GB = 4
    bp2 = [A([128, 2, GB, 16], F32), A([128, 2, GB, 16], F32)]; BB = A([128, 2, GB, 16], F32)
    cp2 = [A([128, 2, GB, 16], F32), A([128, 2, GB, 16], F32)]
    t16a = A([128, GB, 16], F32); t16b = A([128, GB, 16], F32)
    X = A([128, GB, 8, 16], F32)
    xa = xa_raw.rearrange("p (g s h) -> p g s h", g=GB, s=8); xb_ = xb_raw.rearrange("p (g s h) -> p g s h", g=GB, s=8)
    YY = A([128, GB, 9, 16], F32); A7 = A([128, GB, 8, 16], F32)
    xa9 = A([128, GB, 9, 16], F32); xb9 = A([128, GB, 9, 16], F32)
    YmS = A([128, GB, 128])

    t_bpin = [Tok(), Tok()]

    def b4(ap3, axis):
        shp = [ap3.shape[0], GB, 8, 16]
        if axis == 3:
            return bc(ap3.rearrange("p g (s o) -> p g s o", o=1), shp)
        return bc(ap3.rearrange("p g (o h) -> p g o h", o=1), shp)

    def cstack(dst, pw_re, pw_im, c_re_, c_im_, neg_im, xtok=None, n=8):
        RR = [t_sg] + ([xtok] if xtok is not None else [])
        ta, tb = (xa, xb_) if n == 8 else (xa9, xb9)

        def bb(ap3, axis, lo, hi):
            shp = [hi - lo, GB, n, 16]
            if axis == 3:
                return bc(ap3[lo:hi].rearrange("p g (s o) -> p g s o", o=1), shp)
            return bc(ap3[lo:hi].rearrange("p g (o h) -> p g o h", o=1), shp)
        TT(DVE, ta[0:64], bb(pw_re, 3, 0, 64), bb(c_re_, 2, 0, 64), ALU.mult, R=RR, W=[t_sg])
        TT(DVE, tb[0:64], bb(pw_im, 3, 0, 64), bb(c_im_, 2, 0, 64), ALU.mult, R=RR, W=[t_sg])
        TT(DVE, dst[0:64], ta[0:64], tb[0:64], ALU.subtract, R=RR, W=[t_sg])
        TT(DVE, ta[64:128], bb(pw_re, 3, 64, 128), bb(c_im_, 2, 64, 128), ALU.mult, R=RR, W=[t_sg])
        TT(DVE, tb[64:128], bb(pw_im, 3, 64, 128), bb(c_re_, 2, 64, 128), ALU.mult, R=RR, W=[t_sg])
        TT(DVE, dst[64:128], ta[64:128], tb[64:128], ALU.add, R=RR, W=[t_sg])
        if neg_im:
            TS(DVE, dst[64:128], dst[64:128], -1.0, None, ALU.mult, R=RR, W=[t_sg])

    for gb in range(32 // GB):
        g0 = gb * GB
        bp = bp2[gb % 2]
        cp_ = cp2[gb % 2]
        P.dma(POOL, bp, bP[:, :, g0:g0 + GB, :], W=[t_bpin[gb % 2]])
        P.dma(POOL, cp_, cP[:, :, g0:g0 + GB, :], W=[t_bpin[gb % 2]])
        cfre = bc(cf[:, 0, g0:g0 + GB].rearrange("p (g o) -> p g o", o=1), [128, GB, 16])
        cfim = bc(cf[:, 1, g0:g0 + GB].rearrange("p (g o) -> p g o", o=1), [128, GB, 16])
        cmul(DVE, BB[:, 0], BB[:, 1], cfre, cfim, bp[:, 0], bp[:, 1], t16a, t16b, [t_sg, t_bpin[gb % 2]])
        cstack(X, PI_[:, 0, g0:g0 + GB, :], PI_[:, 1, g0:g0 + GB, :], BB[:, 0], BB[:, 1], False)
        cstack(A7, PTr[:, 0, g0:g0 + GB, :], PTr[:, 1, g0:g0 + GB, :], BB[:, 0], BB[:, 1], False)
        cstack(YY, PT[:, 0, g0:g0 + GB, :], PT[:, 1, g0:g0 + GB, :], cp_[:, 0], cp_[:, 1], True, t_bpin[gb % 2], n=9)
        CP(DVE, YmS, YY[:, :, 1:9, :].rearrange("p g t h -> p g (t h)"), R=[t_sg], W=[t_sg])
        P.dma(POOL, ymat_s[:, g0:g0 + GB, :], YmS, R=[t_sg])
        P.dma(POOL, x_s[:, g0:g0 + GB, :], X.rearrange("p g s h -> p g (s h)"), R=[t_sg])
        P.dma(POOL, a_s[:, g0:g0 + GB, :], A7.rearrange("p g s h -> p g (s h)"), R=[t_sg])
        P.dma(POOL, yk_s[:, g0:g0 + GB, :], YY[:, :, 0:8, :].rearrange("p g t h -> p g (t h)"), R=[t_sg])
    ops_a = wg_ops
    ops_b = []
    P.emit, P.dma = real_emit, real_dma
    assert A.off <= KB(131), A.off
    A.hole = None
    A.off = saved_off
    merged = []
    ia = ib = 0
    while ia < len(ops_a) or ib < len(ops_b):
        if ia < len(ops_a):
            merged.append(ops_a[ia]); ia += 1
        if ib < len(ops_b):
            merged.append(ops_b[ib]); ib += 1
    wg_pos = [0]
    wg_per_step = (len(merged) + 39) // 40

    def wg_pump(n):
        for _ in range(n):
            if wg_pos[0] >= len(merged):
                return
            op = merged[wg_pos[0]]
            wg_pos[0] += 1
            if op[0] == "e":
                P.emit(op[1], op[2], op[3], op[4])
            else:
                P.dma(op[1], op[2], op[3], op[4], op[5])


    Qt = Kt
    t_Qt = t_Kt
    uTo = uTp
    xo_t = xo.rearrange("(t p) d -> t p d", p=128)

    def p2A(t):
        stA(t, xo_t[t - NT])

    def p2B(t):
        stB(t)

    def p2C(t):
        hs = (t // 4) % 2
        tq = t % 4
        s2 = t % 2
        ss = t % NSTAT
        pq = PS(3, 384)
        proj_tm(pq, pst[3], 1024, 384, hs, tq)
        ACTV(junk[:, 0:384], pq, AF.Square, R=[pst[3]], W=[t_st[ss]], accum=st8[:, ss, 2:3])
        rsqrt_mean(st8[:, ss, 3:4], st8[:, ss, 2:3], 384.0, 1, R=[t_st[ss]], W=[t_st[ss]])
        TS(DVE, ckvn[s2], pq, st8[:, ss, 3:4], None, ALU.mult, R=[pst[3], t_st[ss]], W=[t_ckv[s2]])

    def p2D(t):
        s2 = t % 2
        pT3 = PS(4, 384, BF16).rearrange("p (k t) -> p k t", k=3)
        for k3 in range(3):
            TRN(pT3[:, k3, :], ckvn[s2][:, k3 * 128:(k3 + 1) * 128], ident, R=[t_ckv[s2], t_ident], W=[pst[4]], inc=(k3 == 2))
        CP(ACT, ckvnT[s2], pT3, R=[pst[4]], W=[t_ckvT[s2]])

    def p2E(t):
        s2 = t % 2
        ss = t % NSTAT
        pqf = PS(5, 768, nb=2)
        for (n0, n1) in ((0, 512), (512, 768)):
            for k3 in range(3):
                MM(pqf[:, n0:n1], ckvnT[s2][:, k3, :], w_qb_b[:, k3, n0:n1], k3 == 0, k3 == 2,
                   R=[t_ckvT[s2], t_wsm], W=[pst[5], pst[6]], inc=(n0 == 512 and k3 == 2))
        q3 = pqf.rearrange("p (h d) -> p h d", h=8)
        ACTV(sqt[s2], q3, AF.Square, R=[pst[5], pst[6]], W=[t_sqt[s2]])
        ssh = st8[:, ss, 8:16]
        RED(DVE, ssh, sqt[s2], ALU.add, R=[t_sqt[s2]], W=[t_st[ss]])
        rk = st8[:, ss, 16:24]
        rsqrt_mean(rk, ssh, 96.0, 8, R=[t_st[ss]], W=[t_st[ss]])
        TT(DVE, sqt[s2], q3, bc(rk.rearrange("p (h o) -> p h o", o=1), [128, 8, 96]), ALU.mult,
           R=[pst[5], pst[6], t_st[ss]], W=[t_sqt[s2]])
        TT(DVE, sqt[s2], sqt[s2], qg96, ALU.mult, R=[t_sqt[s2], t_g], W=[t_sqt[s2]])
        CP(DVE, Qt[s2][:, :, 0:64], sqt[s2][:, :, 0:64], R=[t_sqt[s2]], W=[t_Qt[s2]])
        cos8 = bc(coso[:, t - NT, :].rearrange("p (o d) -> p o d", o=1), [128, 8, 16])
        sin8 = bc(sino[:, t - NT, :].rearrange("p (o d) -> p o d", o=1), [128, 8, 16])
        rope32(Qt[s2][:, :, 64:80], Qt[s2][:, :, 80:96], sqt[s2][:, :, 64:80], sqt[s2][:, :, 80:96], cos8, sin8,
               [qr4[:, i] for i in range(4)], [t_sqt[s2], t_Qt[s2], t_ropeo, t_kr4])

    def p2F(t):
        own_region_sync()
        s2 = t % 2
        pQT = PS(7, 1024, BF16, parts=96).rearrange("p (h t) -> p h t", h=8)
        for h in range(8):
            TRN(pQT[:, h, :], Qt[s2][:, h, :], ident, R=[t_Qt[s2], t_ident], W=[pst[7]], inc=(h == 7))
        CP(ACT, QT[0:96, :, (t - NT) * 128:(t - NT + 1) * 128], pQT, R=[pst[7], t_sg], W=[t_qt])

    def p2after(step):
        u_ = step - 4
        if u_ < 0:
            return
        qg = u_ // 4
        part = u_ % 4
        if qg < NT // 4 or qg >= (NT + NO) // 4:
            return
        own_region_sync()
        qd = qg - NT // 4
        hs = qg % 2
        for cc in range(3 * part, 3 * part + 3):
            b_ = 1 + cc % 2
            pu = PS(b_)
            col0 = [0, 128, 256, 384, 512, 640, 768, 896, 1696, 1824, 1952, 2080][cc]
            proj_fm(pu, pst[b_], col0, hs)
            src = pu.rearrange("p (c s) -> p s c", s=8)
            if cc < 4:
                TS(DVE, uTo[:, cc, :, qd * 64:(qd + 1) * 64], src, biasT[:, cc:cc + 1], None, ALU.add, R=[pst[b_], t_brow], W=[t_uTp])
            elif cc < 8:
                ACTV(zs[:, cc - 4, :, qd * 64:(qd + 1) * 64], src, AF.Silu, R=[pst[b_], t_sg, t_brow], W=[t_z], bias=biasT[:, cc:cc + 1])
            else:
                ACTV(zm[:, cc - 8, :, qd * 64:(qd + 1) * 64], src, AF.Silu, R=[pst[b_], t_sg, t_brow], W=[t_z], bias=biasT[:, cc:cc + 1])

    def mk(f1, f2):
        return lambda t: f1(t) if t < NT else f2(t)

    def both_after(step):
        rope_pump(8)
        p1after(step)
        p2after(step)

    run_pipeline(NT + NO, [mk(p1A, p2A), mk(p1B, p2B), mk(p1C, p2C), mk(p1D, p2D), mk(p1E, p2E), mk(p1F, p2F)], both_after)
    for cc in range(4):
        P.dma(POOL, uTo_s[cc * 128:(cc + 1) * 128, :, :], uTo[:, cc], R=[t_uTp])

    P.barrier()
    A.off = mark1
    kTh = [A([128, L]), A([128, L])]; t_kTh = [Tok(), Tok()]
    Va = [A([128, NT, 128]), A([128, NT, 128])]; t_Va = [Tok(), Tok()]
    PT3 = [A([128, 2, 512]) for _ in range(3)]; t_PT = [Tok() for _ in range(3)]
    rden = A([128, 512], F32); t_rd = Tok()
    scale = 1.0 / math.sqrt(96.0)
    MEMSET(POOL, Va[0][:, :, 64:128], 1.0, W=[t_Va[0]])
    MEMSET(POOL, Va[1][:, :, 0:64], 1.0, W=[t_Va[1]])
    ymv = ymix.rearrange("p a (s c) -> p a s c", s=8)

    def load_head(h):
        hsl = h % 2
        P.dma(SP, kTh[hsl][0:96, :], kT_s[h], W=[t_kTh[hsl]])
        voff = 0 if hsl == 0 else 64
        for part in range(4):
            P.dma(SP, Va[hsl][:, part * 16:(part + 1) * 16, voff:voff + 64],
                  v_s[part * 2048:(part + 1) * 2048, h * 64:(h + 1) * 64].rearrange("(b p) d -> p b d", p=128),
                  W=[t_Va[hsl]])

    items = []
    for h in range(8):
        for gq in range(4):
            nkb = 4 * (4 * gq + 3) + 4
            for kp in range(nkb // 2):
                items.append((h, gq, kp, nkb))

    def geom(it):
        h, gq, kp, nkb = it
        m0 = gq * 4
        kb0 = 2 * kp
        mk = kb0 // 4
        first = max(mk, m0) - m0
        return h, gq, m0, kb0, mk, first * 128, nkb

    def emit_qk(idx):
        h, gq, m0, kb0, mk, c0, nkb = geom(items[idx])
        hsl = h % 2
        sb_ = idx % 3
        b0 = sb_ * 2
        for u in range(2):
            MM(PS(b0 + u)[:, c0:512], kTh[hsl][0:96, (kb0 + u) * 128:(kb0 + u + 1) * 128],
               QT[0:96, h, m0 * 128 + c0:(m0 + 4) * 128], True, True, R=[t_kTh[hsl], t_qt], W=[pst[b0 + u]], inc=(u == 1))
        S2 = psum[:, b0 * 512:(b0 + 2) * 512].rearrange("p (u n) -> p u n", u=2)
        ACTV(PT3[sb_][:, :, c0:512], S2[:, :, c0:512], AF.Exp, R=[pst[b0], pst[b0 + 1], t_nb], W=[t_PT[sb_]],
             bias=nbias[:, 0:1], scale=scale)
        if mk >= m0:
            for u in range(2):
                TT(DVE, PT3[sb_][:, u, c0:c0 + 128], PT3[sb_][:, u, c0:c0 + 128], md[:, (kb0 + u) % 4, :], ALU.mult,
                   R=[t_PT[sb_], t_md], W=[t_PT[sb_]])

    def emit_pv(idx):
        h, gq, m0, kb0, mk, c0, nkb = geom(items[idx])
        hsl = h % 2
        sb_ = idx % 3
        ob = 6 + (h * 4 + gq) % 2
        po = PS(ob)
        if kb0 == 0 and gq == 0 and h + 1 < 8:
            load_head(h + 1)
        for u in range(2):
            kb = kb0 + u
            MM(po[:, c0:512], Va[hsl][:, kb, :], PT3[sb_][:, u, c0:512], kb == 0, kb == nkb - 1,
               R=[t_Va[hsl], t_PT[sb_]], W=[pst[ob]], inc=(u == 1))
        if kb0 + 2 == nkb:
            hp = h // 2
            if hsl == 0:
                nr, dr = slice(0, 64), slice(64, 128)
            else:
                nr, dr = slice(64, 128), slice(0, 64)
            P.emit(DVE, lambda e, o=rden[dr, :], i=po[dr, :]: e.reciprocal(out=o, in_=i), [pst[ob]], [t_rd])
            TT(DVE, rden[nr, :], po[nr, :], rden[dr, :], ALU.mult, R=[pst[ob], t_rd], W=[t_rd])
            dst = ymv[nr, 4 + hp, :, m0 * 16:m0 * 16 + 64]
            TT(DVE, dst, rden[nr, :].rearrange("p (c s) -> p s c", s=8), zm[nr, hp, :, m0 * 16:m0 * 16 + 64], ALU.mult,
               R=[t_rd, t_z], W=[t_ymix[4 + hp]])

    load_head(0)
    LOOK = 2
    for idx in range(min(LOOK, len(items))):
        emit_qk(idx)
    for idx in range(len(items)):
        if idx + LOOK < len(items):
            emit_qk(idx + LOOK)
        emit_pv(idx)
        tgt = (len(merged) * (idx + 1)) // max(1, len(items) - 8)
        if tgt > wg_pos[0]:
            wg_pump(tgt - wg_pos[0])
    wg_pump(len(merged))

    P.barrier()
    A.off = KB(141)
    RV = A([128, 2, 32, 15], F32)
    mark3 = A.off
    t_w1 = Tok()
    t_sg2 = Tok()
    P.dma(SP, Ymat, ymat_s, W=[t_ssmw])
    t_sg = Tok()
    tri = A([128, 128], F32)
    rci = AI([128, 1]); rcf = A([128, 1], F32); colf = A([128, 128], F32); coli = AI([128, 128])
    P.emit(POOL, lambda e: e.iota(rci, pattern=[[0, 1]], base=0, channel_multiplier=1), (), [t_sg])
    P.emit(DVE, lambda e: e.tensor_scalar(out=rci, in0=rci, scalar1=4, scalar2=4, op0=ALU.arith_shift_right,
                                          op1=ALU.logical_shift_left), [t_sg], [t_sg])
    CP(DVE, rcf, rci, R=[t_sg], W=[t_sg])
    P.emit(POOL, lambda e: e.iota(coli, pattern=[[1, 128]], base=0, channel_multiplier=0), (), [t_sg])
    CP(DVE, colf, coli, R=[t_sg], W=[t_sg])
    TS(DVE, tri, colf, rcf[:, 0:1], None, ALU.subtract, R=[t_sg], W=[t_sg])
    TS(DVE, tri, tri, -0.5, None, ALU.is_gt, R=[t_sg], W=[t_sg])
    dsd = A([128, 32], F32)
    P.dma(SP, dsd, dS, W=[t_sg])
    Xf = [A([128, 8, 128], F32), A([128, 8, 128], F32)]; Ykf = [A([128, 8, 128], F32), A([128, 8, 128], F32)]
    t_xf = [Tok(), Tok()]
    Xb3 = [A([128, 128], F32), A([128, 128], F32)]; t_xb = [Tok(), Tok()]
    t_k0 = [Tok() for _ in range(32)]; t_w1g = [Tok() for _ in range(32)]
    Af = [A([128, 8, 128], F32), A([128, 8, 128], F32)]
    for ob in range(4):
        sl = ob % 2
        P.dma(SP, Xf[sl], x_s[:, ob * 8:(ob + 1) * 8, :], W=[t_xf[sl]])
        P.dma(SP, Af[sl], a_s[:, ob * 8:(ob + 1) * 8, :], W=[t_xf[sl]])
        P.dma(SP, Ykf[sl], yk_s[:, ob * 8:(ob + 1) * 8, :], W=[t_xf[sl]])
        for gl in range(8):
            g = ob * 8 + gl
            pk = PS(g % 4, 128)
            MM(pk, Xf[sl][:, gl, :], Ykf[sl][:, gl, :], True, True, R=[t_xf[sl]], W=[pst[g % 4]])
            TT(DVE, Xb3[g % 2], pk, tri, ALU.mult, R=[pst[g % 4], t_sg], W=[t_xb[g % 2]])
            STT(DVE, K0[:, g, :], identf, dsd[:, g:g + 1], Xb3[g % 2], ALU.mult, ALU.add, R=[t_sg, t_ident, t_xb[g % 2]], W=[t_k0[g]])
            pw_ = PS(4 + g % 4, 128)
            TRN(pw_, Af[sl][:, gl, :], identf, R=[t_xf[sl], t_ident], W=[pst[4 + g % 4]])
            CP(ACT, W1[:, g, :], pw_, R=[pst[4 + g % 4]], W=[t_w1g[g]])
    P.dma(SP, RV, rv_s, W=[t_rv])
    wg32 = A([128, 4, 512], F32)
    P.dma(SP, wg32, w_glu.rearrange("(k p) n -> p k n", p=128), W=[t_sg2])
    CP(DVE, w_glu_b, wg32, R=[t_sg2], W=[t_wsm])
    P.barrier()
    A.off = mark3
    NW = 4
    Uo = [A([128, NW, 1024]), A([128, NW, 1024])]; t_Uo = [Tok(), Tok()]
    Uown = [A([128, NW, 256]), A([128, NW, 256])]; t_Uown = [Tok(), Tok()]
    Rg = [A([128, 15, 128]) for _ in range(NW)]; t_Rg = [Tok() for _ in range(NW)]
    XaA = [A([128, 1024]) for _ in range(NW)]; XaB = [A([128, 256]) for _ in range(NW)]
    t_XaA = [Tok() for _ in range(NW)]; t_XaB = [Tok() for _ in range(NW)]
    Sx = [A([128, 80]) for _ in range(NW)]; t_Sx = [Tok() for _ in range(NW)]
    S2o = [A([128, 16], F32) for _ in range(NW)]; t_S2o = [Tok() for _ in range(NW)]
    XoA = [A([128, 256]) for _ in range(NW)]; XoB = [A([128, 256]) for _ in range(NW)]
    t_XoA = [Tok() for _ in range(NW)]; t_XoB = [Tok() for _ in range(NW)]
    Yall = A([128, 8, 256]); t_Yall = Tok()
    t_ys = [Tok() for _ in range(4)]
    t_yg = Tok()
    _save = A.off
    A.off = KB(14)
    ygT = A([128, 4, 8, 256])
    w_out_b = A([128, 8, D]); t_wout = Tok()
    wo32 = [A([128, D], F32), A([128, D], F32)]; t_wo32 = [Tok(), Tok()]
    xr = [A([128, D], F32), A([128, D], F32)]; t_xr = [Tok(), Tok()]
    assert A.off <= KB(62), A.off
    A.off = _save
    ot = wo32; t_ot = t_wo32
    gate_sb = A([128, D], F32); t_gsb = Tok()
    gate_bc = [PS(6), PS(7)]
    for hf in range(2):
        MM(gate_bc[hf], ones_f[0:1, 0:128], gate_row[0:1, hf * 512:(hf + 1) * 512], True, True,
           R=[t_ones, t_grow], W=[pst[6 + hf]])
        CP(DVE, gate_sb[:, hf * 512:(hf + 1) * 512], gate_bc[hf], R=[pst[6 + hf]], W=[t_gsb])

    def wout_chunk(kc):
        sl = kc % 2
        P.dma(SP, wo32[sl], w_out[kc * 128:(kc + 1) * 128, :], W=[t_wo32[sl]])
        TT(DVE, w_out_b[:, kc, :], wo32[sl], gate_sb, ALU.mult, R=[t_wo32[sl], t_gsb], W=[t_wout])
    ident2 = A([128, 128]); t_id2 = Tok()
    CP(DVE, ident2, ident, R=[t_ident], W=[t_id2])
    TT(DVE, ident2[0:64, 64:128], ident2[0:64, 64:128], ident[0:64, 0:64], ALU.add, R=[t_ident, t_id2], W=[t_id2])
    TT(DVE, ident2[64:128, 0:64], ident2[64:128, 0:64], ident[64:128, 64:128], ALU.add, R=[t_ident, t_id2], W=[t_id2])
    for w in range(NW):
        MEMSET(DVE, Sx[w], 0.0, W=[t_Sx[w]])

    def evac(i, out, in_, R, W):
        CP(ACT, out, in_, R=R, W=W)

    pending_reload = []
    t_ygl = [[Tok() for _ in range(8)] for _ in range(4)]

    def do_reload(oc_):
        for gl_ in range(8):
            P.dma(SP, ygT[gl_ * 16:(gl_ + 1) * 16, oc_, :, :], ys_s[oc_ * 8 + gl_].rearrange("t h c -> h t c"),
                  R=[t_ys[oc_]], W=[t_ygl[oc_][gl_]])

    for wave in range(32 // NW):
        wsl = wave % 2
        g0 = wave * NW
        ch0 = g0 * 16
        for s in range(8):
            P.dma(SP, Uo[wsl][s * 16:(s + 1) * 16, :, :],
                  uT_s[ch0:ch0 + NW * 16, s, :].rearrange("(g h) c -> h g c", h=16), W=[t_Uo[wsl]])
            P.dma(SP, Uown[wsl][s * 16:(s + 1) * 16, :, :],
                  uTo_s[ch0:ch0 + NW * 16, s, :].rearrange("(g h) c -> h g c", h=16), W=[t_Uown[wsl]])
        wout_chunk(wave)
        while pending_reload and pending_reload[0] * 2 + 1 < wave:
            do_reload(pending_reload.pop(0))
        for w in range(NW):
            g = g0 + w
            for hf in range(2):
                TT(DVE if hf == 0 else POOL, Rg[w][:, :, hf * 64:(hf + 1) * 64],
                   bc(ident2[:, hf * 64:(hf + 1) * 64].rearrange("p (o q) -> p o q", o=1), [128, 15, 64]),
                   bc(RV[:, hf, g, :].rearrange("p (k o) -> p k o", o=1), [128, 15, 64]), ALU.mult,
                   R=[t_id2, t_rv], W=[t_Rg[w]])
            pz = PS(2 * w, 1024, nb=2)
            for hf in range(2):
                MM(pz[:, hf * 512:(hf + 1) * 512], W1[:, g, :], Uo[wsl][:, w, hf * 512:(hf + 1) * 512], True, True,
                   R=[t_w1g[g], t_Uo[wsl]], W=[pst[2 * w], pst[2 * w + 1]])
            evac(w, XaA[w], pz, [pst[2 * w], pst[2 * w + 1]], [t_XaA[w]])
        for lv, (n, idxs) in enumerate(((256, (2, 1, 0)), (64, (5, 4, 3)))):
            for w in range(NW):
                bk = 2 * w + lv % 2
                src, t_src = (XaA[w], t_XaA[w]) if lv == 0 else (XaB[w], t_XaB[w])
                dst, t_dst = (XaB[w], t_XaB[w]) if lv == 0 else (XaA[w], t_XaA[w])
                pzz = PS(bk, n)
                ev = src[:, 0:4 * n].rearrange("p (c four) -> p c four", four=4)
                for j in range(3):
                    MM(pzz, Rg[w][:, idxs[j], :], ev[:, :, j], j == 0, False, R=[t_Rg[w], t_src], W=[pst[bk]])
                MM(pzz, ident, ev[:, :, 3], False, True, R=[t_ident, t_src], W=[pst[bk]])
                evac(w + lv, dst[:, 0:n], pzz, [pst[bk]], [t_dst])
        for lv, (shs, idxs) in enumerate((((1, 2, 3), (6, 7, 8)), ((4, 8, 12), (9, 10, 11)), ((16, 32, 48), (12, 13, 14)))):
            for w in range(NW):
                bk = 2 * w + lv % 2
                src, t_src = (XaA[w], t_XaA[w]) if lv % 2 == 0 else (XaB[w], t_XaB[w])
                dst, t_dst = (XaB[w], t_XaB[w]) if lv % 2 == 0 else (XaA[w], t_XaA[w])
                pzz = PS(bk, 64)
                MM(pzz, ident, src[:, 0:64], True, False, R=[t_ident, t_src], W=[pst[bk]])
                for j in range(3):
                    shf = shs[j]
                    MM(pzz[:, shf:64], Rg[w][:, idxs[j], :], src[:, 0:64 - shf], False, j == 2,
                       R=[t_Rg[w], t_src], W=[pst[bk]])
                if lv < 2:
                    evac(w + lv, dst[:, 0:64], pzz, [pst[bk]], [t_dst])
                else:
                    evac(w + lv, Sx[w][:, 1:65], pzz, [pst[bk]], [t_Sx[w]])
        for w in range(NW):
            sx4 = Sx[w][:, 0:64].rearrange("p (m i) -> p m i", i=4)
            TS(DVE, S2o[w], sx4[:, :, 0], s_sel[:, 0:1], None, ALU.mult, R=[t_Sx[w], t_small], W=[t_S2o[w]])
            for i in range(1, 4):
                STT(DVE, S2o[w], sx4[:, :, i], s_sel[:, i:i + 1], S2o[w], ALU.mult, ALU.add,
                    R=[t_Sx[w], t_small, t_S2o[w]], W=[t_S2o[w]])
        for w in range(NW):
            g = g0 + w
            bk = 2 * w
            pzo = PS(bk, 256)
            MM(pzo, W1[:, g, :], Uown[wsl][:, w, :], True, True, R=[t_w1g[g], t_Uown[wsl]], W=[pst[bk]])
            xo3 = XoA[w].rearrange("p (i m) -> p i m", i=16)
            evac(w, xo3[:, 1:16, :], pzo.rearrange("p (m i) -> p i m", i=16)[:, 0:15, :], [pst[bk]], [t_XoA[w]])
            CP(DVE, xo3[:, 0, :], S2o[w], R=[t_S2o[w]], W=[t_XoA[w]])
        for lv, (shs, idxs) in enumerate((((1, 2, 3), (0, 1, 2)), ((4, 8, 12), (3, 4, 5)))):
            for w in range(NW):
                bk = 2 * w + (lv + 1) % 2
                src, t_src = (XoA[w], t_XoA[w]) if lv == 0 else (XoB[w], t_XoB[w])
                dst, t_dst = (XoB[w], t_XoB[w]) if lv == 0 else (XoA[w], t_XoA[w])
                pzz = PS(bk, 256)
                MM(pzz, ident, src, True, False, R=[t_ident, t_src], W=[pst[bk]])
                for j in range(3):
                    shf = shs[j]
                    MM(pzz[:, shf * 16:256], Rg[w][:, idxs[j], :], src[:, 0:(16 - shf) * 16], False, j == 2,
                       R=[t_Rg[w], t_src], W=[pst[bk]])
                if lv == 0:
                    evac(w + lv, dst, pzz, [pst[bk]], [t_dst])
                else:
                    evac(w + lv, dst.rearrange("p (m i) -> p i m", i=16), pzz.rearrange("p (i m) -> p i m", i=16),
                         [pst[bk]], [t_dst])
        for w in range(NW):
            g = g0 + w
            gl = g % 8
            bk = 2 * w
            py = PS(bk, 256)
            MM(py, K0[:, g, :], Uown[wsl][:, w, :], True, False, R=[t_k0[g], t_Uown[wsl]], W=[pst[bk]])
            MM(py, Ymat[:, g, :], XoA[w], False, True, R=[t_ssmw, t_XoA[w]], W=[pst[bk]])
            ACTV(Yall[:, gl, :], py, AF.Gelu_apprx_tanh, R=[pst[bk]], W=[t_Yall])
            if gl == 7:
                oc = g // 8
                P.dma(ACT, ys_s[oc * 8:(oc + 1) * 8].rearrange("g t h c -> (t h) g c"), Yall, R=[t_Yall], W=[t_ys[oc]])
                pending_reload.append(oc)
    while pending_reload:
        do_reload(pending_reload.pop(0))
    sg = A([128, 512]); t_sg_ = Tok()
    ymq = [ymix[:, 0:4, q4_ * 512:(q4_ + 1) * 512] for q4_ in range(4)]
    t_ymq = [Tok() for _ in range(4)]
    ygf = ygT.rearrange("p a s c -> p a (s c)")
    zsf = zs.rearrange("p a s c -> p a (s c)")
    zmf = zm.rearrange("p a s c -> p a (s c)")
    xo_v = xo.rearrange("(c s) d -> s c d", s=8)
    yo_v = y_out.rearrange("(c s) d -> s c d", s=8)
    t_out = Tok()
    for q4 in range(4):
        for co in range(4):
            b_ = (co * 4 + q4) % 2
            pg = PS(b_)
            for cc in range(4):
                MM(pg, w_glu_b[:, cc, co * 128:(co + 1) * 128], ygf[:, cc, q4 * 512:(q4 + 1) * 512], cc == 0, cc == 3,
                   R=[t_wsm] + t_ygl[cc], W=[pst[b_]])
            ACTV(sg, pg, AF.Sigmoid, R=[pst[b_], t_small], W=[t_sg_], bias=s_bglu[:, co:co + 1])
            TT(DVE, sg, sg, ygf[:, co, q4 * 512:(q4 + 1) * 512], ALU.mult, R=[t_sg_] + t_ygl[co], W=[t_sg_])
            TT(DVE, ymq[q4][:, co, :], sg, zsf[:, co, q4 * 512:(q4 + 1) * 512], ALU.mult,
               R=[t_sg_, t_z], W=[t_ymq[q4]])
        for st in range(q4 * 4, q4 * 4 + 4):
            sl = st % 2
            s_ = st // 2
            c0 = (st % 2) * 128
            if st == 0:
                P.dma(SP, xr[0], xo_v[0, 0:128, :], W=[t_xr[0]])
            if st + 1 < 16:
                P.dma(SP, xr[(st + 1) % 2], xo_v[(st + 1) // 2, ((st + 1) % 2) * 128:((st + 1) % 2) * 128 + 128, :],
                      W=[t_xr[(st + 1) % 2]])
            pf = PS(2 + 2 * sl, 1024, nb=2)
            for hf in range(2):
                for kc in range(8):
                    if kc < 4:
                        lh = ymq[q4][:, kc, (st % 4) * 128:(st % 4 + 1) * 128]
                        rr = [t_ymq[q4], t_wout]
                    else:
                        lh = ymix[:, kc, st * 128:(st + 1) * 128]
                        rr = [t_ymix[kc], t_wout]
                    MM(pf[:, hf * 512:(hf + 1) * 512], lh, w_out_b[:, kc, hf * 512:(hf + 1) * 512],
                       kc == 0, kc == 7, R=rr, W=[pst[2 + 2 * sl], pst[3 + 2 * sl]], inc=(hf == 1 and kc == 7))
            TT(DVE, ot[sl], pf, xr[sl], ALU.add, R=[pst[2 + 2 * sl], pst[3 + 2 * sl], t_xr[sl]], W=[t_ot[sl]])
            P.dma(SP, yo_v[s_, c0:c0 + 128, :], ot[sl], R=[t_ot[sl]], W=[t_out])
    P.barrier()

    with nc.Block() as block:
        def run(E):
            def f(e):
                for waits, fn, inc in E.ops:
                    for key, val in waits:
                        e.wait_ge(sems[key], val)
                    if fn is not None:
                        ins_ = fn(e)
                        if inc is not None:
                            ins_.then_inc(sems[inc[0]], inc[1])
            return f
        block.tensor(run(PE))
        block.scalar(run(ACT))
        block.vector(run(DVE))
        block.gpsimd(run(POOL))
        block.sync(run(SP))
    es.close()
    return nc


_NC = [None]


def _prep(c, x, cvec, positions, w_ada, b_ada, norm_g, w_in, log_dt, lam_re, lam_im, b_re, b_im, c_re, c_im,
          d_skip, w_glu, b_glu, q_a_g, w_q_b, kv_a_g, w_kv_b, q_norm_g, k_norm_g, w_out):
    b = c // 4
    j = c % 4
    f = np.float32
    ac = np.ascontiguousarray
    own_blocks = [4 * m + j for m in range(NO)]
    xbv = ac(x[b])
    xov = ac(np.concatenate([x[b, blk * 128:(blk + 1) * 128] for blk in own_blocks], axis=0))
    pos = positions[b].astype(np.int32)
    posb = ac(pos.reshape(NT, 128).T)
    poso = ac(np.stack([pos[blk * 128:(blk + 1) * 128] for blk in own_blocks], axis=1))

    def T128(v):
        return ac(v.reshape(-1, 128).T.astype(f))

    lamP = np.zeros((128, 2, 32), f)
    for hf in range(2):
        lamP[hf * 64:(hf + 1) * 64, 0, :] = lam_re[0].T
        lamP[hf * 64:(hf + 1) * 64, 1, :] = lam_im[0].T
    bPv = np.zeros((128, 2, 32, 16), f)
    cPv = np.zeros((128, 2, 32, 16), f)
    for hf in range(2):
        bPv[hf * 64:(hf + 1) * 64, 0] = b_re[0].transpose(1, 0, 2)
        bPv[hf * 64:(hf + 1) * 64, 1] = b_im[0].transpose(1, 0, 2)
        cPv[hf * 64:(hf + 1) * 64, 0] = c_re[0].transpose(2, 0, 1)
        cPv[hf * 64:(hf + 1) * 64, 1] = c_im[0].transpose(2, 0, 1)
    lamS = np.zeros((128, 2, 32, 64), f)
    lamS[:, 0] = lam_re[0][None]
    lamS[:, 1] = lam_im[0][None]
    bSv = np.zeros((128, 2, 32, 64), f)
    dSv = np.zeros((128, 32), f)
    for s in range(8):
        bSv[s * 16:(s + 1) * 16, 0] = b_re[0].transpose(2, 0, 1)
        bSv[s * 16:(s + 1) * 16, 1] = b_im[0].transpose(2, 0, 1)
        dSv[s * 16:(s + 1) * 16, :] = d_skip[0].T
    kk = np.arange(128)[:, None, None]
    ii = np.arange(4)[None, :, None]
    qq = np.arange(128)[None, None, :]
    mdiag = ((128 * ii + kk) <= (128 * j + qq)).astype(f)
    sel = np.zeros((128, 4), f)
    sel[:, j] = 1.0
    return {
        "xb": xbv, "xo": xov, "posb": posb, "poso": poso,
        "cT": T128(cvec[b]), "w_ada": ac(w_ada[0]), "b_adaT": T128(b_ada[0]), "b_gate": ac(b_ada[0][None, 2 * D:3 * D]),
        "norm_gT": T128(norm_g[0]), "w_in": ac(w_in[0]), "w_q_b": ac(w_q_b[0]), "q_a_gT": T128(q_a_g[0]),
        "w_kv_b": ac(w_kv_b[0]), "kv_a_gT": T128(kv_a_g[0]),
        "qg_rep": ac(np.broadcast_to(q_norm_g[0][None], (128, 96)).astype(f)),
        "kg_rep": ac(np.broadcast_to(k_norm_g[0][None], (128, 96)).astype(f)),
        "w_glu": ac(w_glu[0]), "b_gluT": T128(b_glu[0]), "w_out": ac(w_out[0]),
        "ldt_rep": ac(np.broadcast_to(log_dt[0][None], (128, 32)).astype(f)),
        "lamP": lamP, "bP": bPv, "cP": cPv, "lamS": lamS, "bS": bSv, "dS": dSv,
        "mdiag": ac(mdiag), "sel4": sel,
    }


def kernel(**inputs):
    inp = {k: np.asarray(v) for k, v in inputs.items()}
    if _NC[0] is None:
        _NC[0] = build()
    nc = _NC[0]
    args = (inp["x"], inp["c"], inp["positions"], inp["w_ada"], inp["b_ada"], inp["norm_g"], inp["w_in"],
            inp["log_dt"], inp["lam_re"], inp["lam_im"], inp["b_re"], inp["b_im"], inp["c_re"], inp["c_im"],
            inp["d_skip"], inp["w_glu"], inp["b_glu"], inp["q_a_g"], inp["w_q_b"], inp["kv_a_g"], inp["w_kv_b"],
            inp["q_norm_g"], inp["k_norm_g"], inp["w_out"])
    in_maps = [_prep(c, *args) for c in range(8)]
    res = run_bass_kernel_spmd(nc, in_maps, core_ids=list(range(8)))
    out = np.zeros((2, L, D), np.float32)
    for c in range(8):
        b, j = c // 4, c % 4
        y = res.results[c]["y"]
        for m in range(NO):
            blk = 4 * m + j
            out[b, blk * 128:(blk + 1) * 128] = y[m * 128:(m + 1) * 128]
    return out
```
